# Optimizing a Trainium2 kernel written in Bass

```python
import jax, jax.numpy as jnp
from jax import lax
import numpy as np

D_MODEL = 4096
BATCH = 1
SEQ = 8192
DEPTH = 1
DEC_BATCH = 16
DEC_SEQ = 16
PAST_LEN = 1024

CHUNK = 64
Q_BLOCK = 128
SPARSE_Q_BLOCK = 64
EPS = 1e-6
NEG = -1e30

MLA_HEADS = D_MODEL // 128
MLA_Q_LORA = D_MODEL // 4
MLA_KV_LORA = 512
MLA_NOPE = 128
MLA_ROPE = 64
MLA_V = 128
MLA_THETA = 10000.0
MLA_SCALE = (MLA_NOPE + MLA_ROPE) ** -0.5

DSA_HEADS = D_MODEL // 128
DSA_KV_HEADS = DSA_HEADS // 4
DSA_HEAD_DIM = 128
DSA_ROT = DSA_HEAD_DIM // 4
DSA_SCALE = DSA_HEAD_DIM ** -0.5
IDX_HEADS = D_MODEL // 128
IDX_DIM = 128
IDX_ROT = IDX_DIM // 4
IDX_W_SCALE = IDX_HEADS ** -0.5
TOPK_MAX = 256
ROPE_THETA = 500000.0

D_FF = 4 * D_MODEL
N_ADA = 6

IN_SPLITS = (MLA_Q_LORA, MLA_KV_LORA, MLA_ROPE,
             DSA_HEADS * DSA_HEAD_DIM, DSA_KV_HEADS * DSA_HEAD_DIM, DSA_KV_HEADS * DSA_HEAD_DIM,
             IDX_HEADS * IDX_DIM, IDX_DIM, IDX_HEADS,
             D_MODEL, D_MODEL)
D_IN = sum(IN_SPLITS)

kernel_name = 'hybrid_mla_dsa_streaming_encoder_step'


def rmsnorm(x, g):
    xf = x.astype(jnp.float32)
    y = xf * lax.rsqrt(jnp.mean(xf * xf, axis=-1, keepdims=True) + EPS)
    return (y * g.astype(jnp.float32)).astype(x.dtype)


def rope(x, pos, theta, rot_dim):
    half = rot_dim // 2
    inv = theta ** (-jnp.arange(half, dtype=jnp.float32) / half)
    ang = pos.astype(jnp.float32)[:, None] * inv[None, :]
    ang = ang.reshape(ang.shape[0], *([1] * (x.ndim - 3)), half)
    cos = jnp.cos(ang).astype(x.dtype)
    sin = jnp.sin(ang).astype(x.dtype)
    x1, x2, rest = x[..., :half], x[..., half:rot_dim], x[..., rot_dim:]
    return jnp.concatenate([x1 * cos - x2 * sin, x2 * cos + x1 * sin, rest], axis=-1)


def chunk_mask(q_pos, k_pos):
    return (k_pos[None, :] // CHUNK) <= (q_pos[:, None] // CHUNK)


def split_cols(z):
    offsets = [int(o) for o in np.cumsum(IN_SPLITS)[:-1]]
    return jnp.split(z, offsets, axis=-1)


def sweep_blocks(fn, block, q_pos, *qs):
    T = q_pos.shape[0]
    if T <= block:
        return fn(q_pos, *qs)
    nb = T // block
    qb = tuple(jnp.moveaxis(q.reshape(q.shape[0], nb, block, *q.shape[2:]), 1, 0) for q in qs)
    pb = q_pos.reshape(nb, block)
    out = lax.map(lambda a: fn(*a), (pb,) + qb)
    out = jnp.moveaxis(out, 0, 1)
    return out.reshape(out.shape[0], T, *out.shape[3:])


def mla_attend(q_pos, q_lat, q_rope, ckv_all, krope_all, k_pos):
    s = (jnp.einsum('bthc,blc->bhtl', q_lat, ckv_all)
         + jnp.einsum('bthr,blr->bhtl', q_rope, krope_all)).astype(jnp.float32) * MLA_SCALE
    s = jnp.where(chunk_mask(q_pos, k_pos)[None, None], s, NEG)
    p = jax.nn.softmax(s, axis=-1).astype(ckv_all.dtype)
    return jnp.einsum('bhtl,blc->bthc', p, ckv_all)


def dsa_attend(q_pos, q, qi, wi, k_all, v_all, ki_all, k_pos, topk):
    f32 = jnp.float32
    adm = chunk_mask(q_pos, k_pos)
    rel = jax.nn.relu(jnp.einsum('bthd,bld->bthl', qi, ki_all).astype(f32))
    score = jnp.einsum('bth,bthl->btl', wi.astype(f32) * IDX_W_SCALE, rel)
    score = jnp.where(adm[None], score, NEG)
    _, sel = lax.top_k(score, topk)
    valid = (k_pos[sel] // CHUNK) <= (q_pos[None, :, None] // CHUNK)
    kg = jax.vmap(lambda kb, sb: kb[sb])(k_all, sel)
    vg = jax.vmap(lambda vb, sb: vb[sb])(v_all, sel)
    B, T, H, D = q.shape
    qg = q.reshape(B, T, DSA_KV_HEADS, H // DSA_KV_HEADS, D)
    s = jnp.einsum('btgrd,btkgd->btgrk', qg, kg).astype(f32) * DSA_SCALE
    s = jnp.where(valid[:, :, None, None, :], s, NEG)
    p = jax.nn.softmax(s, axis=-1).astype(v_all.dtype)
    o = jnp.einsum('btgrk,btkgd->btgrd', p, vg)
    return o.reshape(B, T, H * D)


def encoder_layer(x, c, past, q_pos, k_pos, topk,
                  w_ada, b_ada, g_norm1, w_in, g_q_lora, w_uq, g_kv_lora, w_uk, w_uv,
                  w_out, g_norm2, w_up, w_down):
    B, T, _ = x.shape
    ada = jnp.einsum('bd,de->be', jax.nn.silu(c), w_ada) + b_ada
    sh1, sc1, gt1, sh2, sc2, gt2 = jnp.split(ada[:, None, :], N_ADA, axis=-1)

    h = rmsnorm(x, g_norm1) * (1.0 + sc1) + sh1
    z = jnp.einsum('btd,de->bte', h, w_in)
    (q_lat_in, ckv, krope, q_b, k_b, v_b, qi, ki, wi, gate_a, gate_b) = split_cols(z)

    q_a = jnp.einsum('btc,ce->bte', rmsnorm(q_lat_in, g_q_lora), w_uq)
    q_a = q_a.reshape(B, T, MLA_HEADS, MLA_NOPE + MLA_ROPE)
    q_nope = q_a[..., :MLA_NOPE]
    q_rope = rope(q_a[..., MLA_NOPE:], q_pos, MLA_THETA, MLA_ROPE)
    ckv = rmsnorm(ckv, g_kv_lora)
    krope = rope(krope, q_pos, MLA_THETA, MLA_ROPE)
    q_lat = jnp.einsum('bthn,chn->bthc', q_nope, w_uk)

    q_b = rope(q_b.reshape(B, T, DSA_HEADS, DSA_HEAD_DIM), q_pos, ROPE_THETA, DSA_ROT)
    k_b = rope(k_b.reshape(B, T, DSA_KV_HEADS, DSA_HEAD_DIM), q_pos, ROPE_THETA, DSA_ROT)
    v_b = v_b.reshape(B, T, DSA_KV_HEADS, DSA_HEAD_DIM)
    qi = rope(qi.reshape(B, T, IDX_HEADS, IDX_DIM), q_pos, ROPE_THETA, IDX_ROT)
    ki = rope(ki, q_pos, ROPE_THETA, IDX_ROT)

    if past is None:
        ckv_all, krope_all, k_all, v_all, ki_all = ckv, krope, k_b, v_b, ki
    else:
        p_ckv, p_krope, p_k, p_v, p_ki = past
        ckv_all = jnp.concatenate([p_ckv, ckv], axis=1)
        krope_all = jnp.concatenate([p_krope, krope], axis=1)
        k_all = jnp.concatenate([p_k, k_b], axis=1)
        v_all = jnp.concatenate([p_v, v_b], axis=1)
        ki_all = jnp.concatenate([p_ki, ki], axis=1)

    o_lat = sweep_blocks(lambda p, ql, qr: mla_attend(p, ql, qr, ckv_all, krope_all, k_pos),
                         Q_BLOCK, q_pos, q_lat, q_rope)
    out_a = jnp.einsum('bthc,chv->bthv', o_lat, w_uv).reshape(B, T, MLA_HEADS * MLA_V)
    out_b = sweep_blocks(lambda p, qq, qqi, wwi: dsa_attend(p, qq, qqi, wwi, k_all, v_all, ki_all, k_pos, topk),
                         SPARSE_Q_BLOCK, q_pos, q_b, qi, wi)

    merged = jax.nn.sigmoid(gate_a) * out_a + jax.nn.sigmoid(gate_b) * out_b
    x = x + gt1 * jnp.einsum('btd,de->bte', merged, w_out)

    h2 = rmsnorm(x, g_norm2) * (1.0 + sc2) + sh2
    u = jax.nn.relu(jnp.einsum('btd,df->btf', h2, w_up))
    x = x + gt2 * jnp.einsum('btf,fd->btd', u * u, w_down)
    return x, (ckv, krope, k_b, v_b, ki)


def setup_inputs(seed: int = 0) -> dict:
    key = jax.random.key(seed)
    ks = jax.random.split(key, 23)
    f32 = jnp.float32

    def nrm(k, shape, scale=1.0):
        return jax.random.normal(k, shape, f32) * scale

    def gain(k, shape):
        return 1.0 + 0.02 * jax.random.normal(k, shape, f32)

    return {
        'x_prompt': nrm(ks[0], (BATCH, SEQ, D_MODEL)),
        'x_sample': nrm(ks[1], (DEC_BATCH, DEC_SEQ, D_MODEL)),
        'c_prompt': nrm(ks[2], (BATCH, D_MODEL)),
        'c_sample': nrm(ks[3], (DEC_BATCH, D_MODEL)),
        'cache_mla_ckv': nrm(ks[4], (DEPTH, DEC_BATCH, PAST_LEN, MLA_KV_LORA)),
        'cache_mla_krope': nrm(ks[5], (DEPTH, DEC_BATCH, PAST_LEN, MLA_ROPE)),
        'cache_dsa_k': nrm(ks[6], (DEPTH, DEC_BATCH, PAST_LEN, DSA_KV_HEADS, DSA_HEAD_DIM)),
        'cache_dsa_v': nrm(ks[7], (DEPTH, DEC_BATCH, PAST_LEN, DSA_KV_HEADS, DSA_HEAD_DIM)),
        'cache_idx_k': nrm(ks[8], (DEPTH, DEC_BATCH, PAST_LEN, IDX_DIM)),
        'w_ada': nrm(ks[9], (DEPTH, D_MODEL, N_ADA * D_MODEL), 0.5 * D_MODEL ** -0.5),
        'b_ada': nrm(ks[10], (DEPTH, N_ADA * D_MODEL), 0.02),
        'g_norm1': gain(ks[11], (DEPTH, D_MODEL)),
        'w_in': nrm(ks[12], (DEPTH, D_MODEL, D_IN), D_MODEL ** -0.5),
        'g_q_lora': gain(ks[13], (DEPTH, MLA_Q_LORA)),
        'w_uq': nrm(ks[14], (DEPTH, MLA_Q_LORA, MLA_HEADS * (MLA_NOPE + MLA_ROPE)), MLA_Q_LORA ** -0.5),
        'g_kv_lora': gain(ks[15], (DEPTH, MLA_KV_LORA)),
        'w_uk': nrm(ks[16], (DEPTH, MLA_KV_LORA, MLA_HEADS, MLA_NOPE), MLA_KV_LORA ** -0.5),
        'w_uv': nrm(ks[17], (DEPTH, MLA_KV_LORA, MLA_HEADS, MLA_V), MLA_KV_LORA ** -0.5),
        'w_out': nrm(ks[18], (DEPTH, D_MODEL, D_MODEL), D_MODEL ** -0.5),
        'g_norm2': gain(ks[19], (DEPTH, D_MODEL)),
        'w_up': nrm(ks[20], (DEPTH, D_MODEL, D_FF), D_MODEL ** -0.5),
        'w_down': nrm(ks[21], (DEPTH, D_FF, D_MODEL), D_FF ** -0.5),
        'g_final': gain(ks[22], (D_MODEL,)),
    }


def reference(x_prompt, x_sample, c_prompt, c_sample,
              cache_mla_ckv, cache_mla_krope, cache_dsa_k, cache_dsa_v, cache_idx_k,
              w_ada, b_ada, g_norm1, w_in, g_q_lora, w_uq, g_kv_lora, w_uk, w_uv,
              w_out, g_norm2, w_up, w_down, g_final):
    topk_prompt = min(TOPK_MAX, SEQ // 4)
    topk_sample = min(TOPK_MAX, (PAST_LEN + DEC_SEQ) // 4)
    pos_prompt = jnp.arange(SEQ, dtype=jnp.int32)
    pos_sample_q = PAST_LEN + jnp.arange(DEC_SEQ, dtype=jnp.int32)
    pos_sample_k = jnp.arange(PAST_LEN + DEC_SEQ, dtype=jnp.int32)

    xp, xs = x_prompt, x_sample
    rows_p, rows_s = [], []
    for l in range(DEPTH):
        lw = (w_ada[l], b_ada[l], g_norm1[l], w_in[l], g_q_lora[l], w_uq[l], g_kv_lora[l],
              w_uk[l], w_uv[l], w_out[l], g_norm2[l], w_up[l], w_down[l])
        xp, rp = encoder_layer(xp, c_prompt, None, pos_prompt, pos_prompt, topk_prompt, *lw)
        past = (cache_mla_ckv[l], cache_mla_krope[l], cache_dsa_k[l], cache_dsa_v[l], cache_idx_k[l])
        xs, rs = encoder_layer(xs, c_sample, past, pos_sample_q, pos_sample_k, topk_sample, *lw)
        rows_p.append(rp)
        rows_s.append(rs)

    y_prompt = rmsnorm(xp, g_final)
    y_sample = rmsnorm(xs, g_final)
    new_ckv_prompt = jnp.stack([r[0] for r in rows_p])
    new_krope_prompt = jnp.stack([r[1] for r in rows_p])
    new_k_prompt = jnp.stack([r[2] for r in rows_p])
    new_v_prompt = jnp.stack([r[3] for r in rows_p])
    new_idxk_prompt = jnp.stack([r[4] for r in rows_p])
    new_ckv_sample = jnp.stack([r[0] for r in rows_s])
    new_krope_sample = jnp.stack([r[1] for r in rows_s])
    new_k_sample = jnp.stack([r[2] for r in rows_s])
    new_v_sample = jnp.stack([r[3] for r in rows_s])
    new_idxk_sample = jnp.stack([r[4] for r in rows_s])
    return (y_prompt, y_sample,
            new_ckv_prompt, new_krope_prompt, new_k_prompt, new_v_prompt, new_idxk_prompt,
            new_ckv_sample, new_krope_sample, new_k_sample, new_v_sample, new_idxk_sample)
```

```python
import numpy as np
import ml_dtypes
from contextlib import ExitStack
import concourse.bass as bass
import concourse.mybir as mybir
from concourse.bass_utils import run_bass_kernel_spmd

F32 = mybir.dt.float32
BF16 = mybir.dt.bfloat16
AF = mybir.ActivationFunctionType
ALU = mybir.AluOpType
AX = mybir.AxisListType

NCORES = 8
NPHYS = 4
V = NCORES // NPHYS
NB = 1 + 2 * V
D = 4096
KD = 32
SEQ = 8192
NP = 1024
NS = 32
NT = NP + NS
PAST = 1024
LS = PAST + 16
EPS = 1e-6
NEG = -1e30
D_IN = 20192
O_QL, O_CKV, O_KR, O_QB, O_KB, O_VB, O_QI, O_KI, O_WI, O_GA, O_GB = (
    0, 1024, 1536, 1600, 5696, 6720, 7744, 11840, 11968, 12000, 16096)
MLA_SCALE = 192.0 ** -0.5
DSA_SCALE = 128.0 ** -0.5
IDX_W_SCALE = 32.0 ** -0.5
TOPK = 256
TT = [(i * 128, 128) for i in range(8)] + [(1024, 32)]
TGRP = [(0, 512), (512, 512), (1024, 32)]
DEBUG = False


class Sched:
    ENG = ("pe", "act", "dve", "pool", "sp")

    def __init__(self, nc, es, nphase=22, ndma=8):
        self.nc = nc
        self.eng = {"pe": nc.tensor, "act": nc.scalar, "dve": nc.vector, "pool": nc.gpsimd, "sp": nc.sync}
        self.psems = [{e: es.enter_context(nc.semaphore(f"p{p}_{e}")) for e in ("pe", "act", "dve")} for p in range(nphase)]
        self.phase = 0
        self.cnt = {e: 0 for e in self.ENG}
        self.seen = {e: {} for e in self.ENG}
        self.bufs = {}
        self.dsem = {}
        for q in ("sp", "pool"):
            self.dsem[q] = [[es.enter_context(nc.semaphore(f"d_{q}{i}")), 0] for i in range(ndma)]
        self.drr = {q: 0 for q in self.dsem}
        self.nins = 0

    def _deps(self, reads, writes):
        deps = {}

        def add(ev):
            k, s, v = ev
            if k not in deps or deps[k][1] < v:
                deps[k] = (s, v)
        for b in reads:
            st = self.bufs.get(b)
            if st and st[0]:
                add(st[0])
        for b in writes:
            st = self.bufs.get(b)
            if st:
                if st[0]:
                    add(st[0])
                for ev in st[1].values():
                    add(ev)
        return deps

    def _wait(self, e, deps):
        for k, (s, v) in deps.items():
            if e == "pe" and k == ("p", self.phase, "pe"):
                continue
            if self.seen[e].get(k, 0) < v:
                self.eng[e].wait_ge(s, v)
                self.seen[e][k] = v

    def _update(self, ev, reads, writes):
        for b in reads:
            st = self.bufs.setdefault(b, [None, {}])
            st[1][ev[0]] = ev
        for b in writes:
            self.bufs[b] = [ev, {}]

    def op(self, e, fn, reads=(), writes=()):
        self._wait(e, self._deps(reads, writes))
        ins = fn(self.eng[e])
        self.cnt[e] += 1
        sem = self.psems[self.phase][e]
        ins.then_inc(sem, 1)
        ev = (("p", self.phase, e), sem, self.cnt[e])
        self._update(ev, reads, writes)
        self.nins += 1
        return ev

    def dma(self, q, out, in_, reads=(), writes=()):
        self._wait(q, self._deps(reads, writes))
        i = self.drr[q]
        self.drr[q] = (i + 1) % len(self.dsem[q])
        ent = self.dsem[q][i]
        k = ("d", q, i)
        if ent[1] > 0 and self.seen[q].get(k, 0) < ent[1]:
            self.eng[q].wait_ge(ent[0], ent[1])
            self.seen[q][k] = ent[1]
        self.eng[q].dma_start(out=out, in_=in_).then_inc(ent[0], 16)
        ent[1] += 16
        ev = (k, ent[0], ent[1])
        self._update(ev, reads, writes)
        self.nins += 1
        return ev

    def barrier(self):
        for e in self.ENG:
            for e2 in ("pe", "act", "dve"):
                if self.cnt[e2] > 0:
                    self.eng[e].wait_ge(self.psems[self.phase][e2], self.cnt[e2])
            for q, lst in self.dsem.items():
                for i, ent in enumerate(lst):
                    if ent[1] > 0 and self.seen[e].get(("d", q, i), 0) < ent[1]:
                        self.eng[e].wait_ge(ent[0], ent[1])
                        self.seen[e][("d", q, i)] = ent[1]
        self.phase += 1
        assert self.phase < len(self.psems)
        self.cnt = {e: 0 for e in self.ENG}
        for e in self.ENG:
            self.seen[e] = {k: v for k, v in self.seen[e].items() if k[0] == "d"}
        self.bufs = {}


def build_program():
    nc = bass.Bass("TRN2", target_bir_lowering=False)
    es = ExitStack()

    def din(name, shape, dt=F32):
        return nc.dram_tensor(name, list(shape), dt, kind="ExternalInput").ap()

    def dout(name, shape, dt=F32):
        return nc.dram_tensor(name, list(shape), dt, kind="ExternalOutput").ap()

    def dscr(name, shape, dt=BF16):
        return nc.dram_tensor(name, list(shape), dt).ap()

    xT_own = din("xT_own", [V, D, NT])
    xT_all = din("xT_all", [D, SEQ])
    cT = din("cT", [128, KD * NB])
    w_ada = din("w_ada", [D, 6 * D])
    b_adaT = din("b_adaT", [128, 192])
    g1T = din("g1T", [128, KD])
    g2T = din("g2T", [128, KD])
    gfT = din("gfT", [128, KD])
    w_in = din("w_in", [D, D_IN])
    gq_rep = din("gq_rep", [128, 1024])
    gkv_rep = din("gkv_rep", [128, 512])
    w_uq = din("w_uq", [1024, 6144])
    w_uk = din("w_uk", [512, 4096])
    w_uv = din("w_uv", [512, 4096])
    w_out = din("w_out", [D, D])
    w_up = din("w_up", [D, 4 * D])
    w_down = din("w_down", [4 * D, D])
    cs_mla_own = din("cs_mla_own", [V, NT, 64])
    cs_dsa_own = din("cs_dsa_own", [V, NT, 32])
    cs_mla_all = din("cs_mla_all", [SEQ, 64])
    cs_dsa_all = din("cs_dsa_all", [SEQ, 32])
    c_ckvT = din("c_ckvT", [2 * V, 512, PAST])
    c_krT = din("c_krT", [2 * V, 64, PAST])
    c_kT = din("c_kT", [2 * V, 8, 128, PAST])
    c_v = din("c_v", [2 * V, PAST, 1024])
    c_kiT = din("c_kiT", [2 * V, 128, PAST])
    maskB_in = din("maskB", [128, V * 4])
    admA_in = din("admA", [128, V * 1024])
    ident_in = din("ident", [128, 128], BF16)
    ones_in = din("ones", [128, 128], BF16)
    yT = dout("yT", [V, D, NT])
    o_ckv = dout("o_ckv", [V, NT, 512])
    o_kr = dout("o_kr", [V, NT, 64])
    o_k = dout("o_k", [V, NT, 1024])
    o_v = dout("o_v", [V, NT, 1024])
    o_ki = dout("o_ki", [V, NT, 128])
    if DEBUG:
        dbg_mg = dout("dbg_mg", [V, 32, 128, NT], BF16)
    QBT = dscr("QBT", [V, 32, 128, NT])
    QIT = dscr("QIT", [V, 32, 128, NT])
    QNT = dscr("QNT", [V, 32, 128, NT])
    QRT = dscr("QRT", [V, 16, 128, NT])
    GAT = dscr("GAT", [V, 32, 128, NT])
    GBT = dscr("GBT", [V, 32, 128, NT])
    MAT = dscr("MAT", [V, 32, 128, NT])
    MGT = dscr("MGT", [V, 32, 128, NT])
    WKB = dscr("WKB", [128, KD, 2752])
    CKVT = dscr("CKVT", [4, 128, SEQ])
    KRT = dscr("KRT", [128, SEQ])
    KBT = dscr("KBT", [8, 128, SEQ])
    KIT = dscr("KIT", [128, SEQ])
    VB = dscr("VB", [SEQ, 1024])
    SCKVT = dscr("SCKVT", [2 * V, 4, 128, LS])
    SKRT = dscr("SKRT", [2 * V, 128, LS])
    SKBT = dscr("SKBT", [2 * V, 8, 128, LS])
    SKIT = dscr("SKIT", [2 * V, 128, LS])
    SVB = dscr("SVB", [2 * V, LS + 112, 1024])
    MASKT = dscr("MASKT", [V, 64, 128, NP])
    SMASKT = dscr("SMASKT", [2 * V, 9, 128, 16])
    X1T = dscr("X1T", [V, KD, 128, NT], F32)
    X2T = dscr("X2T", [V, KD, 128, NT], F32)

    S = Sched(nc, es)

    def sb(name, shape, dt=F32):
        return es.enter_context(nc.sbuf_tensor("sb_" + name, list(shape), dt))

    ps = [es.enter_context(nc.psum_tensor(f"ps{i}", [128, 512], F32)) for i in range(8)]

    def psk(i):
        return ("ps", i)

    ident = sb("ident", [128, 128], BF16)
    ones = sb("ones", [128, 128], BF16)
    epsT = sb("epsT", [128, 1])
    adaT = sb("adaT", [128, 192, NB])
    A1 = sb("A1", [128, KD, NB])
    A2 = sb("A2", [128, KD, NB])
    gf = sb("gf", [128, KD])
    WI = sb("WI", [128, V, 9, 32])
    maskB = sb("maskB", [128, V, 4])
    admA = sb("admA", [128, V, 1024])
    S.dma("sp", ident[:], ident_in[:, :], writes=["ident"])
    S.dma("sp", ones[:], ones_in[:, :], writes=["ones"])
    S.dma("sp", gf[:], gfT[:, :], writes=["gf"])
    S.dma("sp", maskB[:].rearrange("p v r -> p (v r)"), maskB_in[:, :], writes=["maskB"])
    S.dma("sp", admA[:].rearrange("p v r -> p (v r)"), admA_in[:, :], writes=["admA"])
    S.op("dve", lambda e: e.memset(epsT[:], EPS), writes=["epsT"])

    def mm(bank, cols, pairs, reads):
        m = pairs[0][0].shape[-1] if len(pairs[0][0].shape) == 2 else None
        n = len(pairs)

        def fn(pe):
            ins = None
            for i, (l, r) in enumerate(pairs):
                mrows = l.shape[1]
                ins = pe.matmul(ps[bank][0:mrows, cols[0]:cols[1]], lhsT=l, rhs=r, start=(i == 0), stop=(i == n - 1))
            return ins
        return S.op("pe", fn, reads=reads, writes=[psk(bank)])

    def transpose_to(bank, col0, src, nrow, reads):
        pv = ps[bank][:].bitcast(BF16)

        def fn(pe):
            return pe.transpose(out=pv[0:src.shape[1], col0:col0 + nrow], in_=src, identity=ident[0:nrow, 0:nrow])
        return S.op("pe", fn, reads=list(reads) + ["ident"], writes=[psk(bank)])

    def psbf(bank):
        return ps[bank][:].bitcast(BF16)

    with ExitStack() as ph:
        def psb(name, shape, dt=F32):
            return ph.enter_context(nc.sbuf_tensor(name, list(shape), dt))
        c_sb = psb("c_sb", [128, KD * NB])
        sT = psb("sT", [128, KD, NB], BF16)
        bada = psb("bada", [128, 192])
        g1 = psb("g1", [128, KD])
        g2 = psb("g2", [128, KD])
        tmpA = psb("tmpA", [128, KD, NB])
        wp = [psb(f"wpa{i}", [128, KD, 512], BF16) for i in range(3)]
        S.dma("sp", c_sb[:], cT[:, :], writes=["c_sb"])
        S.dma("sp", bada[:], b_adaT[:, :], writes=["bada"])
        S.dma("sp", g1[:], g1T[:, :], writes=["g1"])
        S.dma("sp", g2[:], g2T[:, :], writes=["g2"])
        S.op("act", lambda e: e.activation(out=sT[:].rearrange("p k b -> p (k b)"), in_=c_sb[:], func=AF.Silu),
             reads=["c_sb"], writes=["sT"])
        wav = w_ada.rearrange("(k p) c -> p k c", p=128)
        NPA = 48
        for pn in range(min(3, NPA)):
            S.dma("pool", wp[pn % 3][:], wav[:, :, pn * 512:(pn + 1) * 512], writes=[("wpa", pn % 3)])
        for pn in range(NPA):
            buf = wp[pn % 3]
            for cc in range(4):
                j = pn * 4 + cc
                bank = j % 8
                mm(bank, (0, NB), [(buf[:, k, cc * 128:(cc + 1) * 128], sT[:, k, :]) for k in range(KD)],
                   reads=[("wpa", pn % 3), "sT"])
                S.op("dve", lambda e, j=j, bank=bank: e.tensor_scalar(
                    out=adaT[:, j, :], in0=ps[bank][:, 0:NB], scalar1=bada[:, j:j + 1], scalar2=None, op0=ALU.add),
                    reads=[psk(bank), "bada"], writes=[("adaT", j)])
            if pn + 3 < NPA:
                S.dma("pool", wp[pn % 3][:], wav[:, :, (pn + 3) * 512:(pn + 4) * 512], writes=[("wpa", pn % 3)])
        for (A, g, gname, off) in ((A1, g1, "g1", 32), (A2, g2, "g2", 128)):
            S.op("dve", lambda e, off=off: e.tensor_scalar(out=tmpA[:], in0=adaT[:, off:off + 32, :], scalar1=1.0,
                                                          scalar2=None, op0=ALU.add),
                 reads=[("adaT", j) for j in range(off, off + 32)], writes=["tmpA"])
            S.op("dve", lambda e, A=A, g=g: e.tensor_tensor(out=A[:], in0=tmpA[:],
                                                            in1=g[:].unsqueeze(2).to_broadcast([128, KD, NB]), op=ALU.mult),
                 reads=["tmpA", gname], writes=["Amod"])
        S.barrier()
    SH1, GT1, SH2, GT2 = 0, 64, 96, 160

    def bcols(v):
        return [(0, 1024, 0), (1024, 1040, 1 + 2 * v), (1040, 1056, 2 + 2 * v)]

    def norm_phase(src, A, Boff, hT, final=False, tag="n", v=0):
        with ExitStack() as ph:
            def psb(name, shape, dt=F32):
                return ph.enter_context(nc.sbuf_tensor(f"{tag}_{name}", list(shape), dt))
            xb = [psb(f"xb{i}", [128, NT]) for i in range(3)]
            sq = [psb(f"sq{i}", [128, NT], BF16) for i in range(2)]
            rstd = psb("rstd", [128, NT])
            tmp = [psb(f"tmp{i}", [128, NT]) for i in range(2)]
            for k in range(KD):
                S.dma("sp", xb[k % 3][:], src[k], writes=[("xb", k % 3)])
                S.op("act", lambda e, k=k: e.activation(out=sq[k % 2][:], in_=xb[k % 3][:], func=AF.Square),
                     reads=[("xb", k % 3)], writes=[("sq", k % 2)])

                def fn(pe, k=k):
                    ins = None
                    for gi, (c0, n) in enumerate(TGRP):
                        ins = pe.matmul(ps[gi][:, 0:n], lhsT=ones[:, :], rhs=sq[k % 2][:, c0:c0 + n],
                                        start=(k == 0), stop=(k == KD - 1))
                    return ins
                S.op("pe", fn, reads=[("sq", k % 2), "ones"], writes=[psk(0), psk(1), psk(2)])
            for gi, (c0, n) in enumerate(TGRP):
                S.op("act", lambda e, gi=gi, c0=c0, n=n: e.activation(
                    out=rstd[:, c0:c0 + n], in_=ps[gi][:, 0:n], func=AF.Sqrt, bias=epsT[:, 0:1], scale=1.0 / D),
                    reads=[psk(gi), "epsT"], writes=[("rstd", gi)])
                S.op("dve", lambda e, c0=c0, n=n: e.reciprocal(out=rstd[:, c0:c0 + n], in_=rstd[:, c0:c0 + n]),
                     reads=[("rstd", gi)], writes=[("rstd", gi)])
            rk = [("rstd", gi) for gi in range(3)]
            for k in range(KD):
                S.dma("sp", xb[k % 3][:], src[k], writes=[("xb", k % 3)])
                if final:
                    S.op("dve", lambda e, k=k: e.scalar_tensor_tensor(
                        out=tmp[k % 2][:], in0=xb[k % 3][:], scalar=gf[:, k:k + 1], in1=rstd[:],
                        op0=ALU.mult, op1=ALU.mult), reads=[("xb", k % 3), "gf"] + rk, writes=[("tmp", k % 2)])
                    S.dma("sp", yT[v, k * 128:(k + 1) * 128, :], tmp[k % 2][:], reads=[("tmp", k % 2)], writes=[("yT", k)])
                else:
                    S.op("dve", lambda e, k=k: e.tensor_tensor(out=tmp[k % 2][:], in0=xb[k % 3][:], in1=rstd[:], op=ALU.mult),
                         reads=[("xb", k % 3)] + rk, writes=[("tmp", k % 2)])
                    for (c0, c1, b) in bcols(v):
                        S.op("act", lambda e, k=k, c0=c0, c1=c1, b=b: e.activation(
                            out=hT[:, k, c0:c1], in_=tmp[k % 2][:, c0:c1], func=AF.Identity,
                            bias=adaT[:, Boff + k, b:b + 1], scale=A[:, k, b:b + 1]),
                            reads=[("tmp", k % 2)], writes=[("hT", k)])
            S.barrier()

    def rope_inplace(e_tag, z3, n, H, half, cs, tmps, zkey, cskey="cs"):
        cb = cs[0:n, 0:half].unsqueeze(1).to_broadcast([n, H, half])
        sn = cs[0:n, half:2 * half].unsqueeze(1).to_broadcast([n, H, half])
        x1 = z3[:, :, 0:half]
        x2 = z3[:, :, half:2 * half]
        t = [tm[0:n, 0:H * half].rearrange("p (h d) -> p h d", d=half) for tm in tmps]
        tk = [("ropetmp", e_tag, i) for i in range(4)]
        S.op("dve", lambda e: e.tensor_tensor(out=t[0], in0=x1, in1=cb, op=ALU.mult), reads=[zkey, cskey], writes=[tk[0]])
        S.op("dve", lambda e: e.tensor_tensor(out=t[1], in0=x2, in1=sn, op=ALU.mult), reads=[zkey, cskey], writes=[tk[1]])
        S.op("dve", lambda e: e.tensor_tensor(out=t[2], in0=x2, in1=cb, op=ALU.mult), reads=[zkey, cskey], writes=[tk[2]])
        S.op("dve", lambda e: e.tensor_tensor(out=t[3], in0=x1, in1=sn, op=ALU.mult), reads=[zkey, cskey], writes=[tk[3]])
        S.op("dve", lambda e: e.tensor_tensor(out=x1, in0=t[0], in1=t[1], op=ALU.subtract), reads=[tk[0], tk[1]], writes=[zkey])
        S.op("dve", lambda e: e.tensor_tensor(out=x2, in0=t[2], in1=t[3], op=ALU.add), reads=[tk[2], tk[3]], writes=[zkey])

    def rms_rows(z, n, width, grep, out, zkey, gkey, ss, outkey, tag):
        S.op("act", lambda e: e.activation(out=ss[1][0:n, 0:width], in_=z, func=AF.Square, accum_out=ss[0][0:n, 0:1]),
             reads=[zkey], writes=[("ss", tag)])
        S.op("act", lambda e: e.activation(out=ss[0][0:n, 0:1], in_=ss[0][0:n, 0:1], func=AF.Sqrt,
                                           bias=epsT[0:n, 0:1], scale=1.0 / width),
             reads=[("ss", tag), "epsT"], writes=[("ss", tag)])
        S.op("dve", lambda e: e.reciprocal(out=ss[0][0:n, 0:1], in_=ss[0][0:n, 0:1]), reads=[("ss", tag)], writes=[("ss", tag)])
        S.op("dve", lambda e: e.scalar_tensor_tensor(out=out, in0=z, scalar=ss[0][0:n, 0:1], in1=grep[0:n, 0:width],
                                                     op0=ALU.mult, op1=ALU.mult),
             reads=[zkey, ("ss", tag), gkey], writes=[outkey])

    for v in range(V):
        ph12 = ExitStack()
        hT = ph12.enter_context(nc.sbuf_tensor(f"hT{v}", [128, KD, NT], BF16))
        qlnT = ph12.enter_context(nc.sbuf_tensor(f"qlnT{v}", [128, 8, NT], BF16))
        norm_phase(xT_own[v].rearrange("(k p) t -> k p t", p=128), A1, SH1, hT, tag=f"n1{v}", v=v)
        hkeys = [("hT", k) for k in range(KD)]

        with ExitStack() as ph:
            def psb(name, shape, dt=F32):
                return ph.enter_context(nc.sbuf_tensor(f"p2{v}_{name}", list(shape), dt))
            wp = [psb(f"wp{i}", [128, KD, 512], BF16) for i in range(2)]
            zt = [psb(f"z{i}", [128, 512]) for i in range(3)]
            zb = [psb(f"zb{i}", [128, 512], BF16) for i in range(2)]
            rt = [psb(f"rt{i}", [128, 128]) for i in range(4)]
            ssq = [psb("ss0", [128, 1]), psb("ssj", [128, 1024], BF16)]
            ql = psb("ql", [128, 1024])
            stg = [psb(f"stg{i}", [128, 4, NT], BF16) for i in range(2)]
            csm = psb("csm", [128, 9, 64])
            csd = psb("csd", [128, 9, 32])
            gq = psb("gq", [128, 1024])
            gkv = psb("gkv", [128, 512])
            sstg = psb("sstg", [128, 14, 32], BF16)
            S.dma("sp", gq[:], gq_rep[:, :], writes=["gq"])
            S.dma("sp", gkv[:], gkv_rep[:, :], writes=["gkv"])
            for ti, (t0, n) in enumerate(TT):
                S.dma("sp", csm[0:n, ti, :], cs_mla_own[v, t0:t0 + n, :], writes=["cs"])
                S.dma("sp", csd[0:n, ti, :], cs_dsa_own[v, t0:t0 + n, :], writes=["cs"])
            wv = w_in.rearrange("(k p) c -> p k c", p=128)
            cnt = {"pn": 0, "z": 0, "bank": 0, "tb": 0, "stg": 0, "zb": 0}

            def load_panel(c0, w):
                i = cnt["pn"] % 2
                cnt["pn"] += 1
                S.dma("pool", wp[i][:, :, 0:w], wv[:, :, c0:c0 + w], writes=[("wp", i)])
                return i

            def gemm_tok(pi, w, ti, kk=KD, lhs=None, lkeys=None):
                t0, n = TT[ti]
                bank = cnt["bank"] % 4
                cnt["bank"] += 1
                src = hT if lhs is None else lhs
                mm(bank, (0, w), [(src[:, k, t0:t0 + n], wp[pi][:, k, 0:w]) for k in range(kk)],
                   reads=[("wp", pi)] + (hkeys if lkeys is None else lkeys))
                return bank

            def evac_z(bank, n, w):
                zi = cnt["z"] % 3
                cnt["z"] += 1
                S.op("act", lambda e: e.activation(out=zt[zi][0:n, 0:w], in_=ps[bank][0:n, 0:w], func=AF.Copy),
                     reads=[psk(bank)], writes=[("z", zi)])
                return zi

            def to_bf(zi, n, w):
                bi = cnt["zb"] % 2
                cnt["zb"] += 1
                S.op("act", lambda e: e.activation(out=zb[bi][0:n, 0:w], in_=zt[zi][0:n, 0:w], func=AF.Copy),
                     reads=[("z", zi)], writes=[("zb", bi)])
                return bi

            def tr_blocks(src_tile, skey, n, blocks, dst_fn, dkey):
                for bi_, (c0, wdt) in enumerate(blocks):
                    tb = 4 + cnt["tb"] % 4
                    cnt["tb"] += 1
                    transpose_to(tb, 0, src_tile[0:n, c0:c0 + wdt], n, reads=[skey])
                    S.op("dve", lambda e, tb=tb, bi_=bi_, wdt=wdt: e.tensor_copy(out=dst_fn(bi_), in_=psbf(tb)[0:wdt, 0:n]),
                         reads=[psk(tb)], writes=[dkey])

            qlb = psb("qlb", [128, 1024], BF16)
            pis = [load_panel(O_QL + pnl * 512, 512) for pnl in range(2)]
            for ti, (t0, n) in enumerate(TT):
                for pnl in range(2):
                    bank = gemm_tok(pis[pnl], 512, ti)
                    S.op("act", lambda e, bank=bank, n=n, pnl=pnl: e.activation(
                        out=ql[0:n, pnl * 512:(pnl + 1) * 512], in_=ps[bank][0:n, 0:512], func=AF.Copy),
                        reads=[psk(bank)], writes=[("ql", pnl)])
                S.op("act", lambda e, n=n: e.activation(out=ssq[1][0:n, :], in_=ql[0:n, :], func=AF.Square,
                                                        accum_out=ssq[0][0:n, 0:1]),
                     reads=[("ql", 0), ("ql", 1)], writes=["ssq"])
                S.op("act", lambda e, n=n: e.activation(out=ssq[0][0:n, 0:1], in_=ssq[0][0:n, 0:1], func=AF.Sqrt,
                                                        bias=epsT[0:n, 0:1], scale=1.0 / 1024), reads=["ssq", "epsT"], writes=["ssq"])
                S.op("dve", lambda e, n=n: e.reciprocal(out=ssq[0][0:n, 0:1], in_=ssq[0][0:n, 0:1]), reads=["ssq"], writes=["ssq"])
                S.op("dve", lambda e, n=n: e.scalar_tensor_tensor(
                    out=qlb[0:n, :], in0=ql[0:n, :], scalar=ssq[0][0:n, 0:1], in1=gq[0:n, :], op0=ALU.mult, op1=ALU.mult),
                    reads=["ssq", "gq", ("ql", 0), ("ql", 1)], writes=["qlb"])
                tr_blocks(qlb, "qlb", n, [(c * 128, 128) for c in range(8)],
                          lambda bi_, t0=t0, n=n: qlnT[:, bi_, t0:t0 + n], ("qlnT", ti))
            qkeys = [("qlnT", ti) for ti in range(9)]

            pi = load_panel(O_CKV, 512)
            for ti, (t0, n) in enumerate(TT):
                bank = gemm_tok(pi, 512, ti)
                zi = evac_z(bank, n, 512)
                zo = (zi + 1) % 3
                cnt["z"] += 1
                rms_rows(zt[zi][0:n, :], n, 512, gkv, zt[zo][0:n, :], ("z", zi), "gkv", ssq, ("z", zo), "ckv")
                S.dma("sp", o_ckv[v, t0:t0 + n, :], zt[zo][0:n, :], reads=[("z", zo)], writes=[("o_ckv", ti)])
                if ti == 8:
                    bi = to_bf(zo, n, 512)
                    tr_blocks(zb[bi], ("zb", bi), n, [(c * 128, 128) for c in range(4)],
                              lambda bi_: sstg[:, bi_, 0:32], "sstg")
            pi = load_panel(O_KR, 64)
            for ti, (t0, n) in enumerate(TT):
                bank = gemm_tok(pi, 64, ti)
                zi = evac_z(bank, n, 64)
                rope_inplace("a", zt[zi][0:n, 0:64].rearrange("p (h d) -> p h d", h=1), n, 1, 32, csm[:, ti, :], rt, ("z", zi))
                S.dma("sp", o_kr[v, t0:t0 + n, :], zt[zi][0:n, 0:64], reads=[("z", zi)], writes=[("o_kr", ti)])
                if ti == 8:
                    bi = cnt["zb"] % 2
                    cnt["zb"] += 1
                    for hh in range(2):
                        S.op("act", lambda e, hh=hh, bi=bi, zi=zi, n=n: e.activation(
                            out=zb[bi][0:n, hh * 64:(hh + 1) * 64], in_=zt[zi][0:n, 0:64], func=AF.Copy),
                            reads=[("z", zi)], writes=[("zb", bi)])
                    tr_blocks(zb[bi], ("zb", bi), n, [(0, 128)], lambda bi_: sstg[:, 4, 0:32], "sstg")
            for pnl in range(2):
                pi = load_panel(O_KB + pnl * 512, 512)
                for ti, (t0, n) in enumerate(TT):
                    bank = gemm_tok(pi, 512, ti)
                    zi = evac_z(bank, n, 512)
                    rope_inplace("a", zt[zi][0:n, :].rearrange("p (h d) -> p h d", h=4), n, 4, 16, csd[:, ti, :], rt, ("z", zi))
                    S.dma("sp", o_k[v, t0:t0 + n, pnl * 512:(pnl + 1) * 512], zt[zi][0:n, :], reads=[("z", zi)],
                          writes=[("o_k", ti, pnl)])
                    if ti == 8:
                        bi = to_bf(zi, n, 512)
                        tr_blocks(zb[bi], ("zb", bi), n, [(c * 128, 128) for c in range(4)],
                                  lambda bi_, pnl=pnl: sstg[:, 5 + pnl * 4 + bi_, 0:32], "sstg")
            for pnl in range(2):
                pi = load_panel(O_VB + pnl * 512, 512)
                for ti, (t0, n) in enumerate(TT):
                    bank = gemm_tok(pi, 512, ti)
                    zi = evac_z(bank, n, 512)
                    S.dma("sp", o_v[v, t0:t0 + n, pnl * 512:(pnl + 1) * 512], zt[zi][0:n, :], reads=[("z", zi)],
                          writes=[("o_v", ti, pnl)])
                    if ti == 8:
                        bi = to_bf(zi, n, 512)
                        for b in range(2):
                            S.dma("sp", SVB[2 * v + b, PAST:PAST + 16, pnl * 512:(pnl + 1) * 512], zb[bi][b * 16:(b + 1) * 16, :],
                                  reads=[("zb", bi)], writes=[("SVBn", b, pnl)])
            pi = load_panel(O_KI, 128)
            for ti, (t0, n) in enumerate(TT):
                bank = gemm_tok(pi, 128, ti)
                zi = evac_z(bank, n, 128)
                rope_inplace("a", zt[zi][0:n, 0:128].rearrange("p (h d) -> p h d", h=1), n, 1, 16, csd[:, ti, :], rt, ("z", zi))
                S.dma("sp", o_ki[v, t0:t0 + n, :], zt[zi][0:n, 0:128], reads=[("z", zi)], writes=[("o_ki", ti)])
                if ti == 8:
                    bi = to_bf(zi, n, 128)
                    tr_blocks(zb[bi], ("zb", bi), n, [(0, 128)], lambda bi_: sstg[:, 13, 0:32], "sstg")
            for b in range(2):
                S.dma("sp", SCKVT[2 * v + b].rearrange("c p l -> p c l")[:, :, PAST:LS], sstg[:, 0:4, b * 16:(b + 1) * 16],
                      reads=["sstg"], writes=[("SCKVTn", b)])
                S.dma("sp", SKRT[2 * v + b][:, PAST:LS], sstg[:, 4, b * 16:(b + 1) * 16], reads=["sstg"], writes=[("SKRTn", b)])
                S.dma("sp", SKBT[2 * v + b].rearrange("c p l -> p c l")[:, :, PAST:LS], sstg[:, 5:13, b * 16:(b + 1) * 16],
                      reads=["sstg"], writes=[("SKBTn", b)])
                S.dma("sp", SKIT[2 * v + b][:, PAST:LS], sstg[:, 13, b * 16:(b + 1) * 16], reads=["sstg"], writes=[("SKITn", b)])
            pi = load_panel(O_WI, 32)
            for ti, (t0, n) in enumerate(TT):
                bank = gemm_tok(pi, 32, ti)
                S.op("act", lambda e, bank=bank, ti=ti, n=n: e.activation(out=WI[0:n, v, ti, :], in_=ps[bank][0:n, 0:32],
                                                                        func=AF.Copy, scale=IDX_W_SCALE),
                     reads=[psk(bank)], writes=[("WI", ti)])
            for (off, dst, dname) in ((O_QB, QBT[v], "QBT"), (O_QI, QIT[v], "QIT")):
                for pnl in range(8):
                    pi = load_panel(off + pnl * 512, 512)
                    si = cnt["stg"] % 2
                    cnt["stg"] += 1
                    for ti, (t0, n) in enumerate(TT):
                        bank = gemm_tok(pi, 512, ti)
                        zi = evac_z(bank, n, 512)
                        rope_inplace("a", zt[zi][0:n, :].rearrange("p (h d) -> p h d", h=4), n, 4, 16, csd[:, ti, :], rt, ("z", zi))
                        bi = to_bf(zi, n, 512)
                        tr_blocks(zb[bi], ("zb", bi), n, [(c * 128, 128) for c in range(4)],
                                  lambda bi_, si=si, t0=t0, n=n: stg[si][:, bi_, t0:t0 + n], ("stg", si))
                    S.dma("sp", dst[pnl * 4:(pnl + 1) * 4].rearrange("h p t -> p h t"), stg[si][:],
                          reads=[("stg", si)], writes=[(dname, pnl)])
            for (off, dst, dname) in ((O_GA, GAT[v], "GAT"), (O_GB, GBT[v], "GBT")):
                for pnl in range(8):
                    pi = load_panel(off + pnl * 512, 512)
                    si = cnt["stg"] % 2
                    cnt["stg"] += 1
                    for cc in range(4):
                        for gi, (c0, n) in enumerate(TGRP):
                            bank = cnt["bank"] % 4
                            cnt["bank"] += 1
                            mm(bank, (0, n), [(wp[pi][:, k, cc * 128:(cc + 1) * 128], hT[:, k, c0:c0 + n]) for k in range(KD)],
                               reads=[("wp", pi)] + hkeys)
                            S.op("act", lambda e, bank=bank, si=si, cc=cc, c0=c0, n=n: e.activation(
                                out=stg[si][:, cc, c0:c0 + n], in_=ps[bank][:, 0:n], func=AF.Sigmoid),
                                reads=[psk(bank)], writes=[("stg", si)])
                    S.dma("sp", dst[pnl * 4:(pnl + 1) * 4].rearrange("h p t -> p h t"), stg[si][:],
                          reads=[("stg", si)], writes=[(dname, pnl)])
            wq = w_uq.rearrange("(k p) c -> p k c", p=128)
            for pr in range(16):
                i = cnt["pn"] % 2
                cnt["pn"] += 1
                S.dma("pool", wp[i][:, 0:8, 0:384], wq[:, :, pr * 384:(pr + 1) * 384], writes=[("wp", i)])
                si = cnt["stg"] % 2
                cnt["stg"] += 1
                for ti, (t0, n) in enumerate(TT):
                    bank = gemm_tok(i, 384, ti, kk=8, lhs=qlnT, lkeys=qkeys)
                    zi = evac_z(bank, n, 384)
                    rope_inplace("a", zt[zi][0:n, 0:384].rearrange("p (h d) -> p h d", h=2)[:, :, 128:192], n, 2, 32,
                                 csm[:, ti, :], rt, ("z", zi))
                    bi = cnt["zb"] % 2
                    cnt["zb"] += 1
                    for (d0, s0, wd) in ((0, 0, 128), (128, 192, 128), (256, 128, 64), (320, 320, 64)):
                        S.op("act", lambda e, d0=d0, s0=s0, wd=wd, bi=bi, zi=zi, n=n: e.activation(
                            out=zb[bi][0:n, d0:d0 + wd], in_=zt[zi][0:n, s0:s0 + wd], func=AF.Copy),
                            reads=[("z", zi)], writes=[("zb", bi)])
                    tr_blocks(zb[bi], ("zb", bi), n, [(0, 128), (128, 128), (256, 128)],
                              lambda bi_, si=si, t0=t0, n=n: stg[si][:, bi_, t0:t0 + n], ("stg", si))
                S.dma("sp", QNT[v, pr * 2:(pr + 1) * 2].rearrange("h p t -> p h t"), stg[si][:, 0:2, :],
                      reads=[("stg", si)], writes=[("QNT", pr)])
                S.dma("sp", QRT[v, pr], stg[si][:, 2, :], reads=[("stg", si)], writes=[("QRT", pr)])
            S.barrier()
        ph12.close()

    KOFF = [(O_CKV, 512), (O_KR, 64), (O_KB, 512), (O_KB + 512, 512), (O_VB, 512), (O_VB + 512, 512), (O_KI, 128)]
    KPOS = []
    acc_ = 0
    for (o, w) in KOFF:
        KPOS.append(acc_)
        acc_ += w
    assert acc_ == 2752
    with ExitStack() as ph:
        def psb(name, shape, dt=F32):
            return ph.enter_context(nc.sbuf_tensor(f"p3_{name}", list(shape), dt))
        G = 512
        NTI = G // 128
        hg = psb("hg", [128, KD, G], BF16)
        wp = [psb(f"wp{i}", [128, KD, 512], BF16) for i in range(2)]
        sq = [psb(f"sq{i}", [128, G], BF16) for i in range(2)]
        rstd = psb("rstd", [128, G])
        tmp = [psb(f"tmp{i}", [128, G]) for i in range(2)]
        zt = [psb(f"z{i}", [128, 512]) for i in range(3)]
        zb = [psb(f"zb{i}", [128, 512], BF16) for i in range(2)]
        rt = [psb(f"rt{i}", [128, 128]) for i in range(4)]
        ssq = [psb("ss0", [128, 1]), psb("ssj", [128, 512])]
        gkv = psb("gkv", [128, 512])
        csm = psb("csm", [128, 4, 64])
        csd = psb("csd", [128, 4, 32])
        tstg = psb("tstg", [128, 14, 1024], BF16)
        vstg = psb("vstg", [128, 8, 1024], BF16)
        S.dma("sp", gkv[:], gkv_rep[:, :], writes=["gkv"])
        wv = w_in.rearrange("(k p) c -> p k c", p=128)
        for pi_, (o, w) in enumerate(KOFF):
            S.dma("pool", wp[pi_ % 2][:, :, 0:w], wv[:, :, o:o + w], writes=[("wp", pi_ % 2)])
            S.dma("sp", WKB[:, :, KPOS[pi_]:KPOS[pi_] + w], wp[pi_ % 2][:, :, 0:w], reads=[("wp", pi_ % 2)], writes=[("WKB", pi_)])
        for b in range(2 * V):
            S.dma("pool", tstg[:, 0:4, 0:PAST], c_ckvT[b].rearrange("(c p) l -> p c l", p=128), writes=["tstg"])
            S.dma("sp", SCKVT[b].rearrange("c p l -> p c l")[:, :, 0:PAST], tstg[:, 0:4, 0:PAST], reads=["tstg"], writes=[("SCKVTc", b)])
            for hh in range(2):
                S.dma("pool", tstg[hh * 64:(hh + 1) * 64, 4, 0:PAST], c_krT[b], writes=["tstg"])
            S.dma("sp", SKRT[b][:, 0:PAST], tstg[:, 4, 0:PAST], reads=["tstg"], writes=[("SKRTc", b)])
            S.dma("pool", tstg[:, 5:13, 0:PAST], c_kT[b].rearrange("g p l -> p g l"), writes=["tstg"])
            S.dma("sp", SKBT[b].rearrange("c p l -> p c l")[:, :, 0:PAST], tstg[:, 5:13, 0:PAST], reads=["tstg"], writes=[("SKBTc", b)])
            S.dma("pool", tstg[:, 13, 0:PAST], c_kiT[b], writes=["tstg"])
            S.dma("sp", SKIT[b][:, 0:PAST], tstg[:, 13, 0:PAST], reads=["tstg"], writes=[("SKITc", b)])
            S.dma("pool", vstg[:], c_v[b].rearrange("(t p) d -> p t d", p=128), writes=["vstg"])
            S.dma("sp", SVB[b, 0:PAST, :].rearrange("(t p) d -> p t d", p=128), vstg[:], reads=["vstg"], writes=[("SVBc", b)])
        S.barrier()
        xav = xT_all.rearrange("(k p) t -> p k t", p=128)
        cnt = {"z": 0, "bank": 0, "tb": 0, "zb": 0, "pn": 0}
        for g in range(SEQ // G):
            gc = g * G
            for q4 in range(4):
                S.dma("pool", hg[:, q4 * 8:(q4 + 1) * 8, :], xav[:, q4 * 8:(q4 + 1) * 8, gc:gc + G],
                      writes=[("hg", k) for k in range(q4 * 8, q4 * 8 + 8)])
            for ti in range(NTI):
                S.dma("sp", csm[:, ti, :], cs_mla_all[gc + ti * 128:gc + (ti + 1) * 128, :], writes=[("csm", ti)])
                S.dma("sp", csd[:, ti, :], cs_dsa_all[gc + ti * 128:gc + (ti + 1) * 128, :], writes=[("csd", ti)])
            for k in range(KD):
                S.op("act", lambda e, k=k: e.activation(out=sq[k % 2][:], in_=hg[:, k, :], func=AF.Square),
                     reads=[("hg", k)], writes=[("sq", k % 2)])
                S.op("pe", lambda pe, k=k: pe.matmul(ps[0][:, 0:G], lhsT=ones[:, :], rhs=sq[k % 2][:, :],
                                                     start=(k == 0), stop=(k == KD - 1)),
                     reads=[("sq", k % 2), "ones"], writes=[psk(0)])
            S.op("act", lambda e: e.activation(out=rstd[:, :], in_=ps[0][:, 0:G], func=AF.Sqrt, bias=epsT[:, 0:1], scale=1.0 / D),
                 reads=[psk(0), "epsT"], writes=["rstd"])
            S.op("dve", lambda e: e.reciprocal(out=rstd[:, :], in_=rstd[:, :]), reads=["rstd"], writes=["rstd"])
            for k in range(KD):
                S.op("dve", lambda e, k=k: e.tensor_tensor(out=tmp[k % 2][:], in0=hg[:, k, :], in1=rstd[:], op=ALU.mult),
                     reads=[("hg", k), "rstd"], writes=[("tmp", k % 2)])
                S.op("act", lambda e, k=k: e.activation(out=hg[:, k, :], in_=tmp[k % 2][:], func=AF.Identity,
                                                        bias=adaT[:, SH1 + k, 0:1], scale=A1[:, k, 0:1]),
                     reads=[("tmp", k % 2)], writes=[("hg", k)])
            hgk = [("hg", k) for k in range(KD)]
            for pi_, (o, w) in enumerate(KOFF):
                i = cnt["pn"] % 2
                cnt["pn"] += 1
                S.dma("pool", wp[i][:, :, 0:w], WKB[:, :, KPOS[pi_]:KPOS[pi_] + w],
                      reads=[("WKB", p_) for p_ in range(len(KOFF))], writes=[("wp", i)])
                for ti in range(NTI):
                    t0 = ti * 128
                    bank = 1 + cnt["bank"] % 3
                    cnt["bank"] += 1
                    mm(bank, (0, w), [(hg[:, k, t0:t0 + 128], wp[i][:, k, 0:w]) for k in range(KD)], reads=[("wp", i)] + hgk)
                    zi = cnt["z"] % 3
                    cnt["z"] += 1
                    S.op("act", lambda e, zi=zi, bank=bank, w=w: e.activation(out=zt[zi][:, 0:w], in_=ps[bank][:, 0:w], func=AF.Copy),
                         reads=[psk(bank)], writes=[("z", zi)])
                    blocks = None
                    if pi_ == 0:
                        zo = (zi + 1) % 3
                        cnt["z"] += 1
                        rms_rows(zt[zi][:, :], 128, 512, gkv, zt[zo][:, :], ("z", zi), "gkv", ssq, ("z", zo), "ckv")
                        zi = zo
                        blocks, sbase = [(c * 128, 128) for c in range(4)], 0
                    elif pi_ == 1:
                        rope_inplace("a", zt[zi][:, 0:64].rearrange("p (h d) -> p h d", h=1), 128, 1, 32, csm[:, ti, :], rt, ("z", zi), ("csm", ti))
                    elif pi_ in (2, 3):
                        rope_inplace("a", zt[zi][:, :].rearrange("p (h d) -> p h d", h=4), 128, 4, 16, csd[:, ti, :], rt, ("z", zi), ("csd", ti))
                        blocks, sbase = [(c * 128, 128) for c in range(4)], 5 + (pi_ - 2) * 4
                    elif pi_ == 6:
                        rope_inplace("a", zt[zi][:, 0:128].rearrange("p (h d) -> p h d", h=1), 128, 1, 16, csd[:, ti, :], rt, ("z", zi), ("csd", ti))
                        blocks, sbase = [(0, 128)], 13
                    if pi_ in (4, 5):
                        S.op("act", lambda e, zi=zi, ti=ti, pi_=pi_: e.activation(
                            out=vstg[:, ti, (pi_ - 4) * 512:(pi_ - 3) * 512], in_=zt[zi][:, :], func=AF.Copy),
                            reads=[("z", zi)], writes=[("vstg", ti, pi_)])
                        continue
                    bi = cnt["zb"] % 2
                    cnt["zb"] += 1
                    if pi_ == 1:
                        for hh in range(2):
                            S.op("act", lambda e, hh=hh, bi=bi, zi=zi: e.activation(
                                out=zb[bi][:, hh * 64:(hh + 1) * 64], in_=zt[zi][:, 0:64], func=AF.Copy),
                                reads=[("z", zi)], writes=[("zb", bi)])
                        blocks, sbase = [(0, 128)], 4
                    else:
                        S.op("act", lambda e, bi=bi, zi=zi, w=w: e.activation(out=zb[bi][:, 0:w], in_=zt[zi][:, 0:w], func=AF.Copy),
                             reads=[("z", zi)], writes=[("zb", bi)])
                    for bi_, (c0, wdt) in enumerate(blocks):
                        tb = 4 + cnt["tb"] % 4
                        cnt["tb"] += 1
                        transpose_to(tb, 0, zb[bi][:, c0:c0 + wdt], 128, reads=[("zb", bi)])
                        S.op("dve", lambda e, tb=tb, bi_=bi_, sbase=sbase, t0=t0: e.tensor_copy(
                            out=tstg[:, sbase + bi_, t0:t0 + 128], in_=psbf(tb)[:, 0:128]),
                            reads=[psk(tb)], writes=[("tstg", sbase + bi_, ti)])
            tk = lambda lo, hi: [("tstg", s_, ti_) for s_ in range(lo, hi) for ti_ in range(NTI)]
            S.dma("sp", CKVT.rearrange("c p l -> p c l")[:, :, gc:gc + G], tstg[:, 0:4, 0:G], reads=tk(0, 4), writes=[("CKVT", g)])
            S.dma("sp", KRT[:, gc:gc + G], tstg[:, 4, 0:G], reads=tk(4, 5), writes=[("KRT", g)])
            S.dma("sp", KBT.rearrange("c p l -> p c l")[:, :, gc:gc + G], tstg[:, 5:13, 0:G], reads=tk(5, 13), writes=[("KBT", g)])
            S.dma("sp", KIT[:, gc:gc + G], tstg[:, 13, 0:G], reads=tk(13, 14), writes=[("KIT", g)])
            S.dma("sp", VB[gc:gc + G, :].rearrange("(t p) d -> p t d", p=128), vstg[:, 0:NTI, :],
                  reads=[("vstg", ti_, p_) for ti_ in range(NTI) for p_ in (4, 5)], writes=[("VB", g)])
        S.barrier()

    with ExitStack() as ph:
        def psb(name, shape, dt=F32):
            return ph.enter_context(nc.sbuf_tensor(f"p4_{name}", list(shape), dt))
        kit = psb("kit", [128, SEQ], BF16)
        skit = psb("skit", [128, 2 * V, LS], BF16)
        qi = [psb(f"qi{i}", [128, 32, 128], BF16) for i in range(2)]
        SA = psb("SA", [128, SEQ])
        Wk = psb("Wk", [128, SEQ])
        mk = psb("mk", [128, SEQ], BF16)
        rr = [psb(f"rr{i}", [128, 512]) for i in range(4)]
        m8 = psb("m8", [128, 8])
        thr = psb("thr", [128, 1])
        mstg = psb("mstg", [128, 64, 128], BF16)
        S.dma("sp", kit[:], KIT[:, :], writes=["kit"])
        for b in range(2 * V):
            S.dma("sp", skit[:, b, :], SKIT[b], writes=["kit"])
        cnt = {"bank": 0, "rr": 0, "tb": 0}

        def index_tile(v, j, qcols, nq, keys_ap_fn, L, wi_ap, adm, out_fn, wkey):
            qb_ = qi[j % 2]
            S.dma("sp", qb_[:, :, 0:nq], QIT[v].rearrange("h p t -> p h t")[:, :, qcols:qcols + nq], writes=[("qi", j % 2)])
            blocks = [(c0, min(512, L - c0)) for c0 in range(0, L, 512)]
            for (c0, w) in blocks:
                for h in range(32):
                    bank = cnt["bank"] % 4
                    cnt["bank"] += 1
                    mm(bank, (0, w), [(qb_[:, h, 0:nq], keys_ap_fn(c0, w))], reads=[("qi", j % 2), "kit"])
                    ri = cnt["rr"] % 4
                    cnt["rr"] += 1
                    S.op("act", lambda e, bank=bank, ri=ri, w=w: e.activation(out=rr[ri][0:nq, 0:w], in_=ps[bank][0:nq, 0:w], func=AF.Relu),
                         reads=[psk(bank)], writes=[("rr", ri)])
                    if h == 0:
                        S.op("dve", lambda e, ri=ri, c0=c0, w=w, h=h: e.tensor_scalar(
                            out=SA[0:nq, c0:c0 + w], in0=rr[ri][0:nq, 0:w], scalar1=wi_ap[:, h:h + 1], scalar2=None, op0=ALU.mult),
                            reads=[("rr", ri), wkey], writes=[("SA", c0)])
                    else:
                        S.op("dve", lambda e, ri=ri, c0=c0, w=w, h=h: e.scalar_tensor_tensor(
                            out=SA[0:nq, c0:c0 + w], in0=rr[ri][0:nq, 0:w], scalar=wi_ap[:, h:h + 1], in1=SA[0:nq, c0:c0 + w],
                            op0=ALU.mult, op1=ALU.add), reads=[("rr", ri), ("SA", c0), wkey], writes=[("SA", c0)])
            sak = [("SA", c0) for (c0, w) in blocks]
            if adm:
                S.op("dve", lambda e: e.tensor_tensor(out=SA[0:nq, L - 1024:L], in0=SA[0:nq, L - 1024:L], in1=admA[0:nq, v, :], op=ALU.add),
                     reads=sak + ["admA"], writes=sak)
            cur = SA
            for r in range(TOPK // 8):
                S.op("dve", lambda e, cur=cur: e.max(out=m8[0:nq, :], in_=cur[0:nq, 0:L]), reads=sak + ["Wk"], writes=["m8"])
                if r < TOPK // 8 - 1:
                    S.op("dve", lambda e, cur=cur: e.match_replace(out=Wk[0:nq, 0:L], in_to_replace=m8[0:nq, :],
                                                                   in_values=cur[0:nq, 0:L], imm_value=NEG),
                         reads=sak + ["m8", "Wk"], writes=["Wk"])
                    cur = Wk
            S.op("dve", lambda e: e.tensor_reduce(out=thr[0:nq, :], in_=m8[0:nq, :], axis=AX.X, op=ALU.min), reads=["m8"], writes=["thr"])
            S.op("dve", lambda e: e.tensor_scalar(out=thr[0:nq, :], in0=thr[0:nq, :], scalar1=-1e29, scalar2=None, op0=ALU.max),
                 reads=["thr"], writes=["thr"])
            S.op("dve", lambda e: e.tensor_scalar(out=mk[0:nq, 0:L], in0=SA[0:nq, 0:L], scalar1=thr[0:nq, 0:1], scalar2=None, op0=ALU.is_ge),
                 reads=sak + ["thr"], writes=["mk"])
            nblk = (L + 127) // 128
            for kb in range(nblk):
                kw = min(128, L - kb * 128)
                tb = 4 + cnt["tb"] % 4
                cnt["tb"] += 1
                transpose_to(tb, 0, mk[0:nq, kb * 128:kb * 128 + kw], nq, reads=["mk"])
                S.op("act", lambda e, tb=tb, kb=kb, kw=kw: e.activation(out=mstg[0:kw, kb, 0:nq], in_=psbf(tb)[0:kw, 0:nq], func=AF.Copy),
                     reads=[psk(tb)], writes=[("mstg", kb)])
            out_fn(nblk, [("mstg", kb) for kb in range(nblk)])

        wis = [psb(f"wis{b}", [16, 32]) for b in range(2)]
        for v in range(V):
            for j in range(8):
                L = 1024 * (j + 1)
                index_tile(v, j, j * 128, 128, lambda c0, w: kit[:, c0:c0 + w], L, WI[:, v, j, :], True,
                           lambda nblk, keys, j=j, v=v: S.dma("sp", MASKT[v, 0:nblk].rearrange("k p q -> p k q")[:, :, j * 128:(j + 1) * 128],
                                                             mstg[:, 0:nblk, :], reads=keys, writes=[("MASKT", j)]), "WIp")
            for b in range(2):
                def outs(nblk, keys, b=b, v=v):
                    S.dma("sp", SMASKT[2 * v + b, 0:8].rearrange("k p q -> p k q"), mstg[:, 0:8, 0:16], reads=keys, writes=[("SMASKT", b, 0)])
                    S.dma("sp", SMASKT[2 * v + b, 8, 0:16, :], mstg[0:16, 8, 0:16], reads=keys, writes=[("SMASKT", b, 1)])
                S.dma("sp", wis[b][:], WI[b * 16:(b + 1) * 16, v, 8, :], writes=[("wis", b)])
                index_tile(v, 8 + b, 1024 + b * 16, 16, lambda c0, w, b=b, v=v: skit[:, 2 * v + b, c0:c0 + w], LS, wis[b], False, outs, ("wis", b))
        S.barrier()

    def attention(tagp, sbufs, kparts_fn, v_fn, kblocks, q0_fn, QA, QB, scale, mask_fn, rdeps, fin_fn):
        pT = sbufs["pT"]
        nkb = len(kblocks)

        def s_mm(i):
            kb, kw = kblocks[i]
            q0 = q0_fn(kb)
            par = (i % 2) * 2
            parts = kparts_fn(kb, kw)
            for gi, (c0, c1) in enumerate(((QA, QA + 512), (QA + 512, QB))):
                a, bb = max(c0, q0), min(c1, QB)
                if a >= bb:
                    continue

                def fn(pe, a=a, bb=bb, gi=gi):
                    ins = None
                    for pi_, (l, rfn) in enumerate(parts):
                        ins = pe.matmul(ps[par + gi][0:kw, a - c0:bb - c0], lhsT=l, rhs=rfn(a, bb),
                                        start=(pi_ == 0), stop=(pi_ == len(parts) - 1))
                    return ins
                S.op("pe", fn, reads=rdeps, writes=[psk(par + gi)])

        s_mm(0)
        for i, (kb, kw) in enumerate(kblocks):
            if i + 1 < nkb:
                s_mm(i + 1)
            q0 = q0_fn(kb)
            par = (i % 2) * 2
            pt = pT[i % 2]
            ptk = ("pT", i % 2)
            for gi, (c0, c1) in enumerate(((QA, QA + 512), (QA + 512, QB))):
                a, bb = max(c0, q0), min(c1, QB)
                if a >= bb:
                    continue
                S.op("act", lambda e, a=a, bb=bb, c0=c0, gi=gi, par=par, pt=pt, kw=kw: e.activation(
                    out=pt[0:kw, a - QA:bb - QA], in_=ps[par + gi][0:kw, a - c0:bb - c0], func=AF.Exp, scale=scale),
                    reads=[psk(par + gi)], writes=[ptk])
            mask_fn(kb, kw, q0, pt, ptk)
            for gi, (c0, c1) in enumerate(((QA, QA + 512), (QA + 512, QB))):
                a, bb = max(c0, q0), min(c1, QB)
                if a >= bb:
                    continue

                def fn(pe, a=a, bb=bb, c0=c0, gi=gi, kb=kb, kw=kw, i=i, pt=pt):
                    pe.matmul(ps[4 + gi][:, a - c0:bb - c0], lhsT=v_fn(kb, kw), rhs=pt[0:kw, a - QA:bb - QA],
                              start=(i == 0), stop=(i == nkb - 1))
                    return pe.matmul(ps[6 + gi][:, a - c0:bb - c0], lhsT=ones[0:kw, :], rhs=pt[0:kw, a - QA:bb - QA],
                                     start=(i == 0), stop=(i == nkb - 1))
                S.op("pe", fn, reads=[ptk, "ones"] + rdeps, writes=[psk(4 + gi), psk(6 + gi)])
        rec = sbufs["rec"]
        for gi, (c0, c1) in enumerate(((QA, QA + 512), (QA + 512, QB))):
            if c0 >= QB:
                continue
            n = min(c1, QB) - c0
            S.op("dve", lambda e, gi=gi, n=n, c0=c0: e.reciprocal(out=rec[:, c0 - QA:c0 - QA + n], in_=ps[6 + gi][:, 0:n]),
                 reads=[psk(6 + gi)], writes=[("rec", gi)])
            fin_fn(gi, c0, n, rec)

    with ExitStack() as ph:
        def psb(name, shape, dt=F32):
            return ph.enter_context(nc.sbuf_tensor(f"p5_{name}", list(shape), dt))
        ckvT = psb("ckvT", [128, 4, SEQ], BF16)
        krT = psb("krT", [128, SEQ], BF16)
        sckvT = psb("sckvT", [128, 2 * V, 4, LS], BF16)
        skrT = psb("skrT", [128, 2 * V, LS], BF16)
        kn = psb("kn", [128, SEQ], BF16)
        vh = psb("vh", [128, 64, 128], BF16)
        wuk = [psb(f"wuk{i}", [128, 4, 128], BF16) for i in range(2)]
        wuv = [psb(f"wuv{i}", [128, 4, 128], BF16) for i in range(2)]
        qn = [psb(f"qn{i}", [128, NT], BF16) for i in range(V)]
        qr = [psb(f"qr{i}", [128, NT], BF16) for i in range(V)]
        ga = [psb(f"ga{i}", [128, NT], BF16) for i in range(V)]
        pT = [psb(f"pT{i}", [128, 1024], BF16) for i in range(2)]
        rec = psb("rec", [128, 1024])
        otmp = psb("otmp", [128, 512])
        mo = [psb(f"mo{i}", [128, NT], BF16) for i in range(V)]
        for c in range(4):
            S.dma("sp", ckvT[:, c, :], CKVT[c], writes=[("ckvT", c)])
        S.dma("sp", krT[:], KRT[:, :], writes=["krT"])
        for b in range(2 * V):
            S.dma("sp", sckvT[:, b, :, :], SCKVT[b].rearrange("c p l -> p c l"), writes=[("sckvT", b)])
            S.dma("sp", skrT[:, b, :], SKRT[b], writes=[("skrT", b)])
        ckeys = [("ckvT", c) for c in range(4)] + [("sckvT", b) for b in range(2 * V)]
        krkeys = ["krT"] + [("skrT", b) for b in range(2 * V)]
        wukv = w_uk.rearrange("(c p) n -> p c n", p=128)
        wuvv = w_uv.rearrange("(c p) n -> p c n", p=128)
        bufs = {"pT": pT, "rec": rec}
        PBLK = [(kb, 128) for kb in range(64)]
        SBLK = [(kb, 128) for kb in range(8)] + [(8, 16)]
        for h in range(32):
            hb = h % 2
            rp = (h % 2) * 64
            S.dma("pool", wuk[hb][:], wukv[:, :, h * 128:(h + 1) * 128], writes=[("wuk", hb)])
            S.dma("pool", wuv[hb][:], wuvv[:, :, h * 128:(h + 1) * 128], writes=[("wuv", hb)])
            for v in range(V):
                S.dma("sp", qn[v][:], QNT[v, h], writes=[("qn", v)])
                S.dma("sp", qr[v][:], QRT[v, h // 2], writes=[("qr", v)])
                S.dma("sp", ga[v][:], GAT[v, h], writes=[("ga", v)])

            def materialize(cfn, L, kblocks):
                for c0 in range(0, L, 512):
                    w = min(512, L - c0)
                    bank = (c0 // 512) % 4
                    mm(bank, (0, w), [(wuk[hb][:, c, :], cfn(c, c0, w)) for c in range(4)], reads=[("wuk", hb)] + ckeys)
                    S.op("act", lambda e, bank=bank, c0=c0, w=w: e.activation(out=kn[:, c0:c0 + w], in_=ps[bank][:, 0:w], func=AF.Copy),
                         reads=[psk(bank)], writes=["kn"])
                for k4 in range(0, len(kblocks), 4):
                    bank = 4 + (k4 // 4) % 4
                    blk = kblocks[k4:k4 + 4]

                    def fn(pe, blk=blk, bank=bank):
                        ins = None
                        for bi_, (kb, kw) in enumerate(blk):
                            for c in range(4):
                                ins = pe.matmul(ps[bank][0:kw, bi_ * 128:(bi_ + 1) * 128], lhsT=cfn(c, kb * 128, kw), rhs=wuv[hb][:, c, :],
                                                start=(c == 0), stop=(c == 3))
                        return ins
                    S.op("pe", fn, reads=[("wuv", hb)] + ckeys, writes=[psk(bank)])
                    for bi_, (kb, kw) in enumerate(blk):
                        S.op("dve", lambda e, bank=bank, bi_=bi_, kb=kb, kw=kw: e.tensor_copy(
                            out=vh[0:kw, kb, :], in_=ps[bank][0:kw, bi_ * 128:(bi_ + 1) * 128]), reads=[psk(bank)], writes=["vh"])

            def run_set(v, is_prompt, kr_src, QA, QB, kblocks, q0_fn):
                def kparts(kb, kw):
                    return [(kn[:, kb * 128:kb * 128 + kw], lambda a, bb: qn[v][:, a:bb]),
                            (kr_src(kb * 128, kw), lambda a, bb: qr[v][rp:rp + 64, a:bb])]

                def mask_fn(kb, kw, q0, pt, ptk):
                    if is_prompt:
                        S.op("dve", lambda e: e.tensor_scalar(out=pt[:, q0:q0 + 64], in0=pt[:, q0:q0 + 64],
                                                             scalar1=maskB[:, v, kb % 4:kb % 4 + 1], scalar2=None, op0=ALU.mult),
                             reads=[ptk, "maskB"], writes=[ptk])

                def fin(gi, c0, n, rec_):
                    S.op("dve", lambda e: e.tensor_tensor(out=otmp[:, 0:n], in0=ps[4 + gi][:, 0:n], in1=rec_[:, c0 - QA:c0 - QA + n], op=ALU.mult),
                         reads=[psk(4 + gi), ("rec", gi)], writes=["otmp"])
                    S.op("dve", lambda e: e.tensor_tensor(out=mo[v][:, c0:c0 + n], in0=otmp[:, 0:n], in1=ga[v][:, c0:c0 + n], op=ALU.mult),
                         reads=["otmp", ("ga", v)], writes=[("mo", v, c0)])
                    mokeys[v].append(("mo", v, c0))
                attention("mla", bufs, kparts, lambda kb, kw: vh[0:kw, kb, :], kblocks, q0_fn, QA, QB, MLA_SCALE, mask_fn,
                          ["kn", "vh", ("qn", v), ("qr", v)] + krkeys, fin)

            mokeys = [[] for _ in range(V)]
            materialize(lambda c, a, w: ckvT[:, c, a:a + w], SEQ, PBLK)
            for v in range(V):
                run_set(v, True, lambda a, w: krT[rp:rp + 64, a:a + w], 0, NP, PBLK, lambda kb: 64 * (kb // 4))
            for v in range(V):
                for b in range(2):
                    sb_ = 2 * v + b
                    materialize(lambda c, a, w, sb_=sb_: sckvT[:, sb_, c, a:a + w], LS, SBLK)
                    run_set(v, False, lambda a, w, sb_=sb_: skrT[rp:rp + 64, sb_, a:a + w], NP + 16 * b, NP + 16 * b + 16, SBLK, lambda kb: 0)
            for v in range(V):
                S.dma("sp", MAT[v, h], mo[v][:], reads=mokeys[v], writes=[("MAT", v, h)])
        S.barrier()

    with ExitStack() as ph:
        def psb(name, shape, dt=F32):
            return ph.enter_context(nc.sbuf_tensor(f"p6_{name}", list(shape), dt))
        kg = [psb(f"kg{i}", [128, SEQ], BF16) for i in range(2)]
        vg = [psb(f"vg{i}", [128, 64, 128], BF16) for i in range(2)]
        skg = [psb(f"skg{i}", [128, 2, LS], BF16) for i in range(2)]
        svg = [psb(f"svg{i}", [128, 2, 9, 128], BF16) for i in range(2)]
        MOFF = []
        mo_ = 0
        for kb in range(64):
            MOFF.append(mo_)
            mo_ += NP - 64 * (kb // 4)
        mT = psb("mT", [128, mo_], BF16)
        smT = psb("smT", [128, 2, 9, 16], BF16)
        qb = [psb(f"qb{i}", [128, NT], BF16) for i in range(2)]
        gb = [psb(f"gb{i}", [128, NT], BF16) for i in range(2)]
        ma = [psb(f"ma{i}", [128, NT], BF16) for i in range(2)]
        pT = [psb(f"pT{i}", [128, 1024], BF16) for i in range(2)]
        rec = psb("rec", [128, 1024])
        otmp = psb("otmp", [128, 512])
        mo = [psb(f"mo{i}", [128, NT], BF16) for i in range(2)]
        bufs = {"pT": pT, "rec": rec}
        PBLK = [(kb, 128) for kb in range(64)]
        SBLK = [(kb, 128) for kb in range(8)] + [(8, 16)]
        gcount = 0
        for v in range(V):
            for kb in range(64):
                q0 = 64 * (kb // 4)
                S.dma("sp", mT[:, MOFF[kb]:MOFF[kb] + NP - q0], MASKT[v, kb][:, q0:NP], writes=[("mT", kb)])
            for b in range(2):
                S.dma("sp", smT[:, b, :, :], SMASKT[2 * v + b].rearrange("k p q -> p k q"), writes=[("smT", b)])
            for g in range(8):
                gbi = gcount % 2
                gcount += 1
                S.dma("sp", kg[gbi][:], KBT[g], writes=[("kg", gbi)])
                S.dma("sp", vg[gbi][:], VB[:, g * 128:(g + 1) * 128].rearrange("(k p) d -> p k d", p=128), writes=[("vg", gbi)])
                for b in range(2):
                    S.dma("sp", skg[gbi][:, b, :], SKBT[2 * v + b, g], writes=[("skg", gbi, b)])
                    S.dma("sp", svg[gbi][:, b, :, :], SVB[2 * v + b, 0:9 * 128, g * 128:(g + 1) * 128].rearrange("(k p) d -> p k d", p=128),
                          writes=[("svg", gbi, b)])
                kvkeys = [("kg", gbi), ("vg", gbi)] + [("skg", gbi, b) for b in range(2)] + [("svg", gbi, b) for b in range(2)]
                for hh in range(4):
                    h = g * 4 + hh
                    hb = h % 2
                    S.dma("sp", qb[hb][:], QBT[v, h], writes=[("qb", hb)])
                    S.dma("sp", gb[hb][:], GBT[v, h], writes=[("gb", hb)])
                    S.dma("sp", ma[hb][:], MAT[v, h], writes=[("ma", hb)])
                    mokeys = []
                    for (sname, kfn, vfn, QA, QB, kblocks, q0_fn) in (
                        [("p", lambda a, w: kg[gbi][:, a:a + w], lambda kb, kw: vg[gbi][0:kw, kb, :], 0, NP, PBLK, lambda kb: 64 * (kb // 4))] +
                        [(f"s{b}", lambda a, w, b=b: skg[gbi][:, b, a:a + w], lambda kb, kw, b=b: svg[gbi][0:kw, b, kb, :],
                          NP + 16 * b, NP + 16 * b + 16, SBLK, lambda kb: 0) for b in range(2)]):
                        def kparts(kb, kw, kfn=kfn):
                            return [(kfn(kb * 128, kw), lambda a, bb: qb[hb][:, a:bb])]

                        def mask_fn(kb, kw, q0, pt, ptk, sname=sname):
                            if sname == "p":
                                S.op("dve", lambda e: e.tensor_tensor(out=pt[:, q0:NP], in0=pt[:, q0:NP], in1=mT[:, MOFF[kb]:MOFF[kb] + NP - q0], op=ALU.mult),
                                     reads=[ptk, ("mT", kb)], writes=[ptk])
                            else:
                                b = int(sname[1])
                                S.op("dve", lambda e: e.tensor_tensor(out=pt[0:kw, 0:16], in0=pt[0:kw, 0:16], in1=smT[0:kw, b, kb, :], op=ALU.mult),
                                     reads=[ptk, ("smT", b)], writes=[ptk])

                        def fin(gi, c0, n, rec_, QA=QA):
                            S.op("dve", lambda e: e.tensor_tensor(out=otmp[:, 0:n], in0=ps[4 + gi][:, 0:n], in1=rec_[:, c0 - QA:c0 - QA + n], op=ALU.mult),
                                 reads=[psk(4 + gi), ("rec", gi)], writes=["otmp"])
                            S.op("dve", lambda e: e.tensor_tensor(out=otmp[:, 0:n], in0=otmp[:, 0:n], in1=gb[hb][:, c0:c0 + n], op=ALU.mult),
                                 reads=["otmp", ("gb", hb)], writes=["otmp"])
                            S.op("dve", lambda e: e.tensor_tensor(out=mo[hb][:, c0:c0 + n], in0=otmp[:, 0:n], in1=ma[hb][:, c0:c0 + n], op=ALU.add),
                                 reads=["otmp", ("ma", hb)], writes=[("mo", hb, c0)])
                            mokeys.append(("mo", hb, c0))
                        attention("dsa", bufs, kparts, vfn, kblocks, q0_fn, QA, QB, DSA_SCALE, mask_fn, kvkeys + [("qb", hb)], fin)
                    S.dma("sp", MGT[v, h], mo[hb][:], reads=mokeys, writes=[("MGT", v, h)])
                    if DEBUG:
                        S.dma("sp", dbg_mg[v, h], mo[hb][:], reads=mokeys, writes=[("dbg", v, h)])
        S.barrier()

    with ExitStack() as ph:
        def psb(name, shape, dt=F32):
            return ph.enter_context(nc.sbuf_tensor(f"p7_{name}", list(shape), dt))
        mg = psb("mg", [128, KD, NT], BF16)
        wp = [psb(f"wp{i}", [128, KD, 512], BF16) for i in range(2)]
        xb = [psb(f"xb{i}", [128, NT]) for i in range(2)]
        x1 = [psb(f"x1{i}", [128, NT]) for i in range(2)]
        wv = w_out.rearrange("(k p) c -> p k c", p=128)
        pcount = 0
        for v in range(V):
            xov = xT_own[v].rearrange("(k p) t -> k p t", p=128)
            S.dma("sp", mg[:], MGT[v].rearrange("h p t -> p h t"), writes=["mg"])
            for pn in range(8):
                pw = pcount % 2
                pcount += 1
                S.dma("pool", wp[pw][:], wv[:, :, pn * 512:(pn + 1) * 512], writes=[("wp", pw)])
                for cc in range(4):
                    oc = pn * 4 + cc
                    S.dma("sp", xb[oc % 2][:], xov[oc], writes=[("xb", oc % 2)])
                    for gi, (c0, n) in enumerate(TGRP):
                        bank = (oc % 2) * 3 + gi
                        mm(bank, (0, n), [(wp[pw][:, j, cc * 128:(cc + 1) * 128], mg[:, j, c0:c0 + n]) for j in range(KD)],
                           reads=[("wp", pw), "mg"])
                    for bi_, (c0, c1, b) in enumerate(bcols(v)):
                        if c0 == 0:
                            for g2_ in range(2):
                                S.op("dve", lambda e, oc=oc, g2_=g2_: e.scalar_tensor_tensor(
                                    out=x1[oc % 2][:, g2_ * 512:(g2_ + 1) * 512], in0=ps[(oc % 2) * 3 + g2_][:, :], scalar=adaT[:, GT1 + oc, 0:1],
                                    in1=xb[oc % 2][:, g2_ * 512:(g2_ + 1) * 512], op0=ALU.mult, op1=ALU.add),
                                    reads=[psk((oc % 2) * 3 + g2_), ("xb", oc % 2)], writes=[("x1", oc % 2, g2_)])
                        else:
                            S.op("dve", lambda e, oc=oc, c0=c0, c1=c1, b=b: e.scalar_tensor_tensor(
                                out=x1[oc % 2][:, c0:c1], in0=ps[(oc % 2) * 3 + 2][:, c0 - 1024:c1 - 1024], scalar=adaT[:, GT1 + oc, b:b + 1],
                                in1=xb[oc % 2][:, c0:c1], op0=ALU.mult, op1=ALU.add),
                                reads=[psk((oc % 2) * 3 + 2), ("xb", oc % 2)], writes=[("x1", oc % 2, 1 + bi_)])
                    S.dma("sp", X1T[v, oc], x1[oc % 2][:], reads=[("x1", oc % 2, i_) for i_ in range(4)], writes=[("X1T", v, oc)])
        S.barrier()

    H2T = dscr("H2T", [V, KD, 128, NT])
    for v in range(V):
        with ExitStack() as ph7:
            h2T = ph7.enter_context(nc.sbuf_tensor(f"h2T{v}", [128, KD, NT], BF16))
            norm_phase(X1T[v], A2, SH2, h2T, tag=f"n2{v}", v=v)
            S.dma("sp", H2T[v].rearrange("k p t -> p k t"), h2T[:], writes=["H2T"])
            S.barrier()
    with ExitStack() as ph:
        def psb(name, shape, dt=F32):
            return ph.enter_context(nc.sbuf_tensor(f"p8_{name}", list(shape), dt))
        HN = NT // 2
        FB = 8
        h2h = psb("h2h", [128, KD, HN], BF16)
        acc = psb("acc", [128, KD, HN])
        uT = psb("uT", [128, FB, HN], BF16)
        wu = [psb(f"wu{i}", [128, KD, 256], BF16) for i in range(2)]
        wd = [psb(f"wd{i}", [128, FB, 512], BF16) for i in range(2)]
        ur = [psb(f"ur{i}", [128, HN]) for i in range(2)]
        xb = [psb(f"xb{i}", [128, HN]) for i in range(2)]
        xo = [psb(f"xo{i}", [128, HN]) for i in range(2)]
        wuv_ = w_up.rearrange("(k p) c -> p k c", p=128)
        wdv_ = w_down.rearrange("(fb fc p) c -> fb p fc c", fc=FB, p=128)
        HG = [(0, 512), (512, HN - 512)]
        cnt = {"wu": 0, "wd": 0, "bank": 0}
        for vh_ in range(2 * V):
            v, half = vh_ // 2, vh_ % 2
            hc = half * HN
            S.dma("sp", h2h[:], H2T[v].rearrange("k p t -> p k t")[:, :, hc:hc + HN], writes=["h2h"])
            for fb in range(128 // FB):
                for f2 in range(FB // 2):
                    i = cnt["wu"] % 2
                    cnt["wu"] += 1
                    fcol = (fb * FB + f2 * 2) * 128
                    S.dma("pool", wu[i][:], wuv_[:, :, fcol:fcol + 256], writes=[("wu", i)])
                    for fl in range(2):
                        fci = f2 * 2 + fl
                        bp = (cnt["bank"] % 2) * 2
                        cnt["bank"] += 1
                        for gi, (c0, n) in enumerate(HG):
                            mm(bp + gi, (0, n), [(wu[i][:, k, fl * 128:(fl + 1) * 128], h2h[:, k, c0:c0 + n]) for k in range(KD)],
                               reads=[("wu", i), "h2h"])
                        ui = fci % 2
                        for gi, (c0, n) in enumerate(HG):
                            S.op("act", lambda e, bp=bp, gi=gi, c0=c0, n=n, ui=ui: e.activation(
                                out=ur[ui][:, c0:c0 + n], in_=ps[bp + gi][:, 0:n], func=AF.Relu), reads=[psk(bp + gi)], writes=[("ur", ui, gi)])
                        S.op("dve", lambda e, ui=ui, fci=fci: e.tensor_tensor(out=uT[:, fci, :], in0=ur[ui][:], in1=ur[ui][:], op=ALU.mult),
                             reads=[("ur", ui, 0), ("ur", ui, 1)], writes=[("uT", fci)])
                utk = [("uT", f_) for f_ in range(FB)]
                for og in range(8):
                    i = cnt["wd"] % 2
                    cnt["wd"] += 1
                    S.dma("pool", wd[i][:], wdv_[fb][:, :, og * 512:(og + 1) * 512], writes=[("wd", i)])
                    for ocl in range(4):
                        oc = og * 4 + ocl
                        bp = 4 + (oc % 2) * 2
                        for gi, (c0, n) in enumerate(HG):
                            mm(bp + gi, (0, n), [(wd[i][:, fc, ocl * 128:(ocl + 1) * 128], uT[:, fc, c0:c0 + n]) for fc in range(FB)],
                               reads=[("wd", i)] + utk)
                        for gi, (c0, n) in enumerate(HG):
                            if fb == 0:
                                S.op("dve", lambda e, bp=bp, gi=gi, c0=c0, n=n, oc=oc: e.tensor_copy(out=acc[:, oc, c0:c0 + n], in_=ps[bp + gi][:, 0:n]),
                                     reads=[psk(bp + gi)], writes=[("acc", oc, gi)])
                            else:
                                S.op("dve", lambda e, bp=bp, gi=gi, c0=c0, n=n, oc=oc: e.tensor_tensor(
                                    out=acc[:, oc, c0:c0 + n], in0=ps[bp + gi][:, 0:n], in1=acc[:, oc, c0:c0 + n], op=ALU.add),
                                    reads=[psk(bp + gi), ("acc", oc, gi)], writes=[("acc", oc, gi)])
            for oc in range(KD):
                S.dma("sp", xb[oc % 2][:], X1T[v, oc][:, hc:hc + HN], writes=[("xb", oc % 2)])
                wk_ = []
                for bi_, (c0, c1, b) in enumerate(bcols(v)):
                    a, bb = max(c0, hc), min(c1, hc + HN)
                    if a >= bb:
                        continue
                    S.op("dve", lambda e, oc=oc, a=a, bb=bb, b=b: e.scalar_tensor_tensor(
                        out=xo[oc % 2][:, a - hc:bb - hc], in0=acc[:, oc, a - hc:bb - hc], scalar=adaT[:, GT2 + oc, b:b + 1],
                        in1=xb[oc % 2][:, a - hc:bb - hc], op0=ALU.mult, op1=ALU.add),
                        reads=[("acc", oc, 0), ("acc", oc, 1), ("xb", oc % 2)], writes=[("xo", oc % 2, bi_)])
                    wk_.append(("xo", oc % 2, bi_))
                S.dma("sp", X2T[v, oc][:, hc:hc + HN], xo[oc % 2][:], reads=wk_, writes=[("X2T", v, oc, half)])
        S.barrier()

    for v in range(V):
        norm_phase(X2T[v], None, 0, None, final=True, tag=f"n3{v}", v=v)
    es.close()
    return nc


def _rope_table(pos, theta, rot):
    half = rot // 2
    inv = (np.float32(theta) ** (-np.arange(half, dtype=np.float32) / np.float32(half))).astype(np.float32)
    ang = pos.astype(np.float32)[:, None] * inv[None, :]
    return np.concatenate([np.cos(ang).astype(np.float32), np.sin(ang).astype(np.float32)], axis=1)


def _fm(v, nchunk):
    return np.ascontiguousarray(np.asarray(v, np.float32).reshape(nchunk, 128).T)


_NC_CACHE = {}


def kernel(x_prompt, x_sample, c_prompt, c_sample, cache_mla_ckv, cache_mla_krope, cache_dsa_k, cache_dsa_v,
           cache_idx_k, w_ada, b_ada, g_norm1, w_in, g_q_lora, w_uq, g_kv_lora, w_uk, w_uv, w_out, g_norm2,
           w_up, w_down, g_final):
    f32 = np.float32
    x_prompt = np.asarray(x_prompt, f32)
    x_sample = np.asarray(x_sample, f32)
    xT_all = np.ascontiguousarray(x_prompt[0].T)
    shared = {
        "xT_all": xT_all,
        "w_ada": np.asarray(w_ada, f32)[0], "b_adaT": _fm(np.asarray(b_ada)[0], 192),
        "g1T": _fm(np.asarray(g_norm1)[0], KD), "g2T": _fm(np.asarray(g_norm2)[0], KD), "gfT": _fm(g_final, KD),
        "w_in": np.asarray(w_in, f32)[0],
        "gq_rep": np.ascontiguousarray(np.broadcast_to(np.asarray(g_q_lora, f32)[0][None, :], (128, 1024))),
        "gkv_rep": np.ascontiguousarray(np.broadcast_to(np.asarray(g_kv_lora, f32)[0][None, :], (128, 512))),
        "w_uq": np.asarray(w_uq, f32)[0],
        "w_uk": np.asarray(w_uk, f32)[0].reshape(512, 4096), "w_uv": np.asarray(w_uv, f32)[0].reshape(512, 4096),
        "w_out": np.asarray(w_out, f32)[0], "w_up": np.asarray(w_up, f32)[0], "w_down": np.asarray(w_down, f32)[0],
        "cs_mla_all": _rope_table(np.arange(SEQ), 10000.0, 64), "cs_dsa_all": _rope_table(np.arange(SEQ), 500000.0, 32),
        "ident": np.eye(128, dtype=f32).astype(ml_dtypes.bfloat16), "ones": np.ones((128, 128), f32).astype(ml_dtypes.bfloat16),
    }
    in_maps = []
    own_idx = []
    csf = np.asarray(c_sample, f32)
    for pc in range(NPHYS):
        xo_l, csm_l, csd_l, mb_l, adm_l = [], [], [], [], []
        bs = []
        for v in range(V):
            i = pc * V + v
            chunks = [8 * m + i for m in range(16)]
            tok = np.concatenate([np.arange(64 * c, 64 * c + 64) for c in chunks])
            own_idx.append(tok)
            xo = np.concatenate([x_prompt[0][tok], x_sample[2 * i], x_sample[2 * i + 1]], axis=0)
            xo_l.append(np.ascontiguousarray(xo.T))
            pos = np.concatenate([tok, PAST + np.arange(16), PAST + np.arange(16)])
            csm_l.append(_rope_table(pos, 10000.0, 64))
            csd_l.append(_rope_table(pos, 500000.0, 32))
            maskB = np.zeros((128, 4), f32)
            for r in range(4):
                for a_ in range(2):
                    maskB[a_ * 64:(a_ + 1) * 64, r] = 1.0 if i >= 2 * r + a_ else 0.0
            admA = np.full((128, 1024), NEG, f32)
            admA[0:64, 0:64 * (i + 1)] = 0.0
            admA[64:128, 0:64 * (8 + i + 1)] = 0.0
            mb_l.append(maskB)
            adm_l.append(admA)
            bs += [2 * i, 2 * i + 1]
        cc = np.stack([np.asarray(c_prompt, f32)[0]] + [csf[b_] for b_ in bs], axis=1)
        cTm = np.ascontiguousarray(cc.reshape(KD, 128, NB).transpose(1, 0, 2).reshape(128, KD * NB))
        m = dict(shared)
        m.update({
            "xT_own": np.stack(xo_l), "cT": cTm,
            "cs_mla_own": np.stack(csm_l), "cs_dsa_own": np.stack(csd_l),
            "c_ckvT": np.ascontiguousarray(np.asarray(cache_mla_ckv, f32)[0][bs].transpose(0, 2, 1)),
            "c_krT": np.ascontiguousarray(np.asarray(cache_mla_krope, f32)[0][bs].transpose(0, 2, 1)),
            "c_kT": np.ascontiguousarray(np.asarray(cache_dsa_k, f32)[0][bs].transpose(0, 2, 3, 1)),
            "c_v": np.ascontiguousarray(np.asarray(cache_dsa_v, f32)[0][bs].reshape(2 * V, PAST, 1024)),
            "c_kiT": np.ascontiguousarray(np.asarray(cache_idx_k, f32)[0][bs].transpose(0, 2, 1)),
            "maskB": np.ascontiguousarray(np.concatenate(mb_l, axis=1)), "admA": np.ascontiguousarray(np.concatenate(adm_l, axis=1)),
        })
        in_maps.append(m)
    if "nc" not in _NC_CACHE:
        _NC_CACHE["nc"] = build_program()
    nc = _NC_CACHE["nc"]
    res = run_bass_kernel_spmd(nc, in_maps, core_ids=list(range(NPHYS)))
    R = res.results
    y_prompt = np.zeros((1, SEQ, D), f32)
    y_sample = np.zeros((16, 16, D), f32)
    outs_p = {k: np.zeros((1, 1, SEQ, w), f32) for k, w in (("o_ckv", 512), ("o_kr", 64), ("o_k", 1024), ("o_v", 1024), ("o_ki", 128))}
    outs_s = {k: np.zeros((1, 16, 16, w), f32) for k, w in (("o_ckv", 512), ("o_kr", 64), ("o_k", 1024), ("o_v", 1024), ("o_ki", 128))}
    for i in range(NCORES):
        r, v = R[i // V], i % V
        y = np.asarray(r["yT"])[v].T
        y_prompt[0][own_idx[i]] = y[0:NP]
        y_sample[2 * i] = y[NP:NP + 16]
        y_sample[2 * i + 1] = y[NP + 16:NT]
        for k in outs_p:
            a = np.asarray(r[k])[v]
            outs_p[k][0, 0][own_idx[i]] = a[0:NP]
            outs_s[k][0, 2 * i] = a[NP:NP + 16]
            outs_s[k][0, 2 * i + 1] = a[NP + 16:NT]
    if DEBUG:
        kernel.debug = [np.asarray(R[i // V]["dbg_mg"])[i % V] for i in range(NCORES)]
        kernel.own_idx = own_idx
    return (y_prompt, y_sample,
            outs_p["o_ckv"], outs_p["o_kr"], outs_p["o_k"].reshape(1, 1, SEQ, 8, 128), outs_p["o_v"].reshape(1, 1, SEQ, 8, 128), outs_p["o_ki"],
            outs_s["o_ckv"], outs_s["o_kr"], outs_s["o_k"].reshape(1, 16, 16, 8, 128), outs_s["o_v"].reshape(1, 16, 16, 8, 128), outs_s["o_ki"])
```

```python
import numpy as np
import ml_dtypes
from contextlib import ExitStack
import concourse.bass as bass
import concourse.mybir as mybir
from concourse.bass_utils import run_bass_kernel_spmd

F32 = mybir.dt.float32
BF16 = mybir.dt.bfloat16
AF = mybir.ActivationFunctionType
ALU = mybir.AluOpType
AX = mybir.AxisListType

NCORES = 8
NPHYS = 8
V = NCORES // NPHYS
NB = 1 + 2 * V
D = 4096
KD = 32
SEQ = 8192
NP = 1024
NS = 32
NT = NP + NS
PAST = 1024
LS = PAST + 16
EPS = 1e-6
NEG = -1e30
D_IN = 20192
O_QL, O_CKV, O_KR, O_QB, O_KB, O_VB, O_QI, O_KI, O_WI, O_GA, O_GB = (
    0, 1024, 1536, 1600, 5696, 6720, 7744, 11840, 11968, 12000, 16096)
MLA_SCALE = 192.0 ** -0.5
DSA_SCALE = 128.0 ** -0.5
IDX_W_SCALE = 32.0 ** -0.5
TOPK = 256
TT = [(i * 128, 128) for i in range(8)] + [(1024, 32)]
TGRP = [(0, 512), (512, 512), (1024, 32)]
DEBUG = False


class Sched:
    ENG = ("pe", "act", "dve", "pool", "sp")

    def __init__(self, nc, es, nphase=22, ndma=8):
        self.nc = nc
        self.eng = {"pe": nc.tensor, "act": nc.scalar, "dve": nc.vector, "pool": nc.gpsimd, "sp": nc.sync}
        self.psems = [{e: es.enter_context(nc.semaphore(f"p{p}_{e}")) for e in ("pe", "act", "dve")} for p in range(nphase)]
        self.phase = 0
        self.cnt = {e: 0 for e in self.ENG}
        self.seen = {e: {} for e in self.ENG}
        self.bufs = {}
        self.dsem = {}
        for q in ("sp", "pool"):
            self.dsem[q] = [[es.enter_context(nc.semaphore(f"d_{q}{i}")), 0] for i in range(ndma)]
        self.drr = {q: 0 for q in self.dsem}
        self.nins = 0

    def _deps(self, reads, writes):
        deps = {}

        def add(ev):
            k, s, v = ev
            if k not in deps or deps[k][1] < v:
                deps[k] = (s, v)
        for b in reads:
            st = self.bufs.get(b)
            if st and st[0]:
                add(st[0])
        for b in writes:
            st = self.bufs.get(b)
            if st:
                if st[0]:
                    add(st[0])
                for ev in st[1].values():
                    add(ev)
        return deps

    def _wait(self, e, deps):
        for k, (s, v) in deps.items():
            if e == "pe" and k == ("p", self.phase, "pe"):
                continue
            if self.seen[e].get(k, 0) < v:
                self.eng[e].wait_ge(s, v)
                self.seen[e][k] = v

    def _update(self, ev, reads, writes):
        for b in reads:
            st = self.bufs.setdefault(b, [None, {}])
            st[1][ev[0]] = ev
        for b in writes:
            self.bufs[b] = [ev, {}]

    def op(self, e, fn, reads=(), writes=()):
        self._wait(e, self._deps(reads, writes))
        ins = fn(self.eng[e])
        self.cnt[e] += 1
        sem = self.psems[self.phase][e]
        ins.then_inc(sem, 1)
        ev = (("p", self.phase, e), sem, self.cnt[e])
        self._update(ev, reads, writes)
        self.nins += 1
        return ev

    def dma(self, q, out, in_, reads=(), writes=()):
        self._wait(q, self._deps(reads, writes))
        i = self.drr[q]
        self.drr[q] = (i + 1) % len(self.dsem[q])
        ent = self.dsem[q][i]
        k = ("d", q, i)
        if ent[1] > 0 and self.seen[q].get(k, 0) < ent[1]:
            self.eng[q].wait_ge(ent[0], ent[1])
            self.seen[q][k] = ent[1]
        self.eng[q].dma_start(out=out, in_=in_).then_inc(ent[0], 16)
        ent[1] += 16
        ev = (k, ent[0], ent[1])
        self._update(ev, reads, writes)
        self.nins += 1
        return ev

    def barrier(self):
        for e in self.ENG:
            for e2 in ("pe", "act", "dve"):
                if self.cnt[e2] > 0:
                    self.eng[e].wait_ge(self.psems[self.phase][e2], self.cnt[e2])
            for q, lst in self.dsem.items():
                for i, ent in enumerate(lst):
                    if ent[1] > 0 and self.seen[e].get(("d", q, i), 0) < ent[1]:
                        self.eng[e].wait_ge(ent[0], ent[1])
                        self.seen[e][("d", q, i)] = ent[1]
        self.phase += 1
        assert self.phase < len(self.psems)
        self.cnt = {e: 0 for e in self.ENG}
        for e in self.ENG:
            self.seen[e] = {k: v for k, v in self.seen[e].items() if k[0] == "d"}
        self.bufs = {}


def build_program():
    nc = bass.Bass("TRN2", target_bir_lowering=False)
    es = ExitStack()

    def din(name, shape, dt=F32):
        return nc.dram_tensor(name, list(shape), dt, kind="ExternalInput").ap()

    def dout(name, shape, dt=F32):
        return nc.dram_tensor(name, list(shape), dt, kind="ExternalOutput").ap()

    def dscr(name, shape, dt=BF16):
        return nc.dram_tensor(name, list(shape), dt).ap()

    xT_own = din("xT_own", [V, D, NT])
    xT_all = din("xT_all", [D, SEQ])
    cT = din("cT", [128, KD * NB])
    w_ada = din("w_ada", [D, 6 * D])
    b_adaT = din("b_adaT", [128, 192])
    g1T = din("g1T", [128, KD])
    g2T = din("g2T", [128, KD])
    gfT = din("gfT", [128, KD])
    w_in = din("w_in", [D, D_IN])
    gq_rep = din("gq_rep", [128, 1024])
    gkv_rep = din("gkv_rep", [128, 512])
    w_uq = din("w_uq", [1024, 6144])
    w_uk = din("w_uk", [512, 4096])
    w_uv = din("w_uv", [512, 4096])
    w_out = din("w_out", [D, D])
    w_up = din("w_up", [D, 4 * D])
    w_down = din("w_down", [4 * D, D])
    cs_mla_own = din("cs_mla_own", [V, NT, 64])
    cs_dsa_own = din("cs_dsa_own", [V, NT, 32])
    cs_mla_all = din("cs_mla_all", [SEQ, 64])
    cs_dsa_all = din("cs_dsa_all", [SEQ, 32])
    c_ckvT = din("c_ckvT", [2 * V, 512, PAST])
    c_krT = din("c_krT", [2 * V, 64, PAST])
    c_kT = din("c_kT", [2 * V, 8, 128, PAST])
    c_v = din("c_v", [2 * V, PAST, 1024])
    c_kiT = din("c_kiT", [2 * V, 128, PAST])
    maskB_in = din("maskB", [128, V * 4])
    admA_in = din("admA", [128, V * 1024])
    ident_in = din("ident", [128, 128], BF16)
    ones_in = din("ones", [128, 128], BF16)
    yT = dout("yT", [V, D, NT])
    o_ckv = dout("o_ckv", [V, NT, 512])
    o_kr = dout("o_kr", [V, NT, 64])
    o_k = dout("o_k", [V, NT, 1024])
    o_v = dout("o_v", [V, NT, 1024])
    o_ki = dout("o_ki", [V, NT, 128])
    if DEBUG:
        dbg_mg = dout("dbg_mg", [V, 32, 128, NT], BF16)
    QBT = dscr("QBT", [V, 32, 128, NT])
    QIT = dscr("QIT", [V, 32, 128, NT])
    QNT = dscr("QNT", [V, 32, 128, NT])
    QRT = dscr("QRT", [V, 16, 128, NT])
    GAT = dscr("GAT", [V, 32, 128, NT])
    GBT = dscr("GBT", [V, 32, 128, NT])
    MAT = dscr("MAT", [V, 32, 128, NT])
    MGT = dscr("MGT", [V, 32, 128, NT])
    WKB = dscr("WKB", [128, KD, 2752])
    CKVT = dscr("CKVT", [4, 128, SEQ])
    KRT = dscr("KRT", [128, SEQ])
    KBT = dscr("KBT", [8, 128, SEQ])
    KIT = dscr("KIT", [128, SEQ])
    VB = dscr("VB", [SEQ, 1024])
    SCKVT = dscr("SCKVT", [2 * V, 4, 128, LS])
    SKRT = dscr("SKRT", [2 * V, 128, LS])
    SKBT = dscr("SKBT", [2 * V, 8, 128, LS])
    SKIT = dscr("SKIT", [2 * V, 128, LS])
    SVB = dscr("SVB", [2 * V, LS + 112, 1024])
    MASKT = dscr("MASKT", [V, 64, 128, NP])
    SMASKT = dscr("SMASKT", [2 * V, 9, 128, 16])
    X1T = dscr("X1T", [V, KD, 128, NT], F32)
    X2T = dscr("X2T", [V, KD, 128, NT], F32)

    S = Sched(nc, es)

    def sb(name, shape, dt=F32):
        return es.enter_context(nc.sbuf_tensor("sb_" + name, list(shape), dt))

    ps = [es.enter_context(nc.psum_tensor(f"ps{i}", [128, 512], F32)) for i in range(8)]

    def psk(i):
        return ("ps", i)

    ident = sb("ident", [128, 128], BF16)
    ones = sb("ones", [128, 128], BF16)
    epsT = sb("epsT", [128, 1])
    adaT = sb("adaT", [128, 192, NB])
    A1 = sb("A1", [128, KD, NB])
    A2 = sb("A2", [128, KD, NB])
    gf = sb("gf", [128, KD])
    WI = sb("WI", [128, V, 9, 32])
    maskB = sb("maskB", [128, V, 4])
    admA = sb("admA", [128, V, 1024])
    S.dma("sp", ident[:], ident_in[:, :], writes=["ident"])
    S.dma("sp", ones[:], ones_in[:, :], writes=["ones"])
    S.dma("sp", gf[:], gfT[:, :], writes=["gf"])
    S.dma("sp", maskB[:].rearrange("p v r -> p (v r)"), maskB_in[:, :], writes=["maskB"])
    S.dma("sp", admA[:].rearrange("p v r -> p (v r)"), admA_in[:, :], writes=["admA"])
    S.op("dve", lambda e: e.memset(epsT[:], EPS), writes=["epsT"])

    def mm(bank, cols, pairs, reads):
        m = pairs[0][0].shape[-1] if len(pairs[0][0].shape) == 2 else None
        n = len(pairs)

        def fn(pe):
            ins = None
            for i, (l, r) in enumerate(pairs):
                mrows = l.shape[1]
                ins = pe.matmul(ps[bank][0:mrows, cols[0]:cols[1]], lhsT=l, rhs=r, start=(i == 0), stop=(i == n - 1))
            return ins
        return S.op("pe", fn, reads=reads, writes=[psk(bank)])

    def transpose_to(bank, col0, src, nrow, reads):
        pv = ps[bank][:].bitcast(BF16)

        def fn(pe):
            return pe.transpose(out=pv[0:src.shape[1], col0:col0 + nrow], in_=src, identity=ident[0:nrow, 0:nrow])
        return S.op("pe", fn, reads=list(reads) + ["ident"], writes=[psk(bank)])

    def psbf(bank):
        return ps[bank][:].bitcast(BF16)

    with ExitStack() as ph:
        def psb(name, shape, dt=F32):
            return ph.enter_context(nc.sbuf_tensor(name, list(shape), dt))
        c_sb = psb("c_sb", [128, KD * NB])
        sT = psb("sT", [128, KD, NB], BF16)
        bada = psb("bada", [128, 192])
        g1 = psb("g1", [128, KD])
        g2 = psb("g2", [128, KD])
        tmpA = psb("tmpA", [128, KD, NB])
        wp = [psb(f"wpa{i}", [128, KD, 512], BF16) for i in range(3)]
        S.dma("sp", c_sb[:], cT[:, :], writes=["c_sb"])
        S.dma("sp", bada[:], b_adaT[:, :], writes=["bada"])
        S.dma("sp", g1[:], g1T[:, :], writes=["g1"])
        S.dma("sp", g2[:], g2T[:, :], writes=["g2"])
        S.op("act", lambda e: e.activation(out=sT[:].rearrange("p k b -> p (k b)"), in_=c_sb[:], func=AF.Silu),
             reads=["c_sb"], writes=["sT"])
        wav = w_ada.rearrange("(k p) c -> p k c", p=128)
        NPA = 48
        for pn in range(min(3, NPA)):
            S.dma("pool", wp[pn % 3][:], wav[:, :, pn * 512:(pn + 1) * 512], writes=[("wpa", pn % 3)])
        for pn in range(NPA):
            buf = wp[pn % 3]
            for cc in range(4):
                j = pn * 4 + cc
                bank = j % 8
                mm(bank, (0, NB), [(buf[:, k, cc * 128:(cc + 1) * 128], sT[:, k, :]) for k in range(KD)],
                   reads=[("wpa", pn % 3), "sT"])
                S.op("dve", lambda e, j=j, bank=bank: e.tensor_scalar(
                    out=adaT[:, j, :], in0=ps[bank][:, 0:NB], scalar1=bada[:, j:j + 1], scalar2=None, op0=ALU.add),
                    reads=[psk(bank), "bada"], writes=[("adaT", j)])
            if pn + 3 < NPA:
                S.dma("pool", wp[pn % 3][:], wav[:, :, (pn + 3) * 512:(pn + 4) * 512], writes=[("wpa", pn % 3)])
        for (A, g, gname, off) in ((A1, g1, "g1", 32), (A2, g2, "g2", 128)):
            S.op("dve", lambda e, off=off: e.tensor_scalar(out=tmpA[:], in0=adaT[:, off:off + 32, :], scalar1=1.0,
                                                          scalar2=None, op0=ALU.add),
                 reads=[("adaT", j) for j in range(off, off + 32)], writes=["tmpA"])
            S.op("dve", lambda e, A=A, g=g: e.tensor_tensor(out=A[:], in0=tmpA[:],
                                                            in1=g[:].unsqueeze(2).to_broadcast([128, KD, NB]), op=ALU.mult),
                 reads=["tmpA", gname], writes=["Amod"])
        S.barrier()
    SH1, GT1, SH2, GT2 = 0, 64, 96, 160

    def bcols(v):
        return [(0, 1024, 0), (1024, 1040, 1 + 2 * v), (1040, 1056, 2 + 2 * v)]

    def norm_phase(src, A, Boff, hT, final=False, tag="n", v=0):
        with ExitStack() as ph:
            def psb(name, shape, dt=F32):
                return ph.enter_context(nc.sbuf_tensor(f"{tag}_{name}", list(shape), dt))
            xb = [psb(f"xb{i}", [128, NT]) for i in range(3)]
            sq = [psb(f"sq{i}", [128, NT], BF16) for i in range(2)]
            rstd = psb("rstd", [128, NT])
            tmp = [psb(f"tmp{i}", [128, NT]) for i in range(2)]
            for k in range(KD):
                S.dma("sp", xb[k % 3][:], src[k], writes=[("xb", k % 3)])
                S.op("act", lambda e, k=k: e.activation(out=sq[k % 2][:], in_=xb[k % 3][:], func=AF.Square),
                     reads=[("xb", k % 3)], writes=[("sq", k % 2)])

                def fn(pe, k=k):
                    ins = None
                    for gi, (c0, n) in enumerate(TGRP):
                        ins = pe.matmul(ps[gi][:, 0:n], lhsT=ones[:, :], rhs=sq[k % 2][:, c0:c0 + n],
                                        start=(k == 0), stop=(k == KD - 1))
                    return ins
                S.op("pe", fn, reads=[("sq", k % 2), "ones"], writes=[psk(0), psk(1), psk(2)])
            for gi, (c0, n) in enumerate(TGRP):
                S.op("act", lambda e, gi=gi, c0=c0, n=n: e.activation(
                    out=rstd[:, c0:c0 + n], in_=ps[gi][:, 0:n], func=AF.Sqrt, bias=epsT[:, 0:1], scale=1.0 / D),
                    reads=[psk(gi), "epsT"], writes=[("rstd", gi)])
                S.op("dve", lambda e, c0=c0, n=n: e.reciprocal(out=rstd[:, c0:c0 + n], in_=rstd[:, c0:c0 + n]),
                     reads=[("rstd", gi)], writes=[("rstd", gi)])
            rk = [("rstd", gi) for gi in range(3)]
            for k in range(KD):
                S.dma("sp", xb[k % 3][:], src[k], writes=[("xb", k % 3)])
                if final:
                    S.op("dve", lambda e, k=k: e.scalar_tensor_tensor(
                        out=tmp[k % 2][:], in0=xb[k % 3][:], scalar=gf[:, k:k + 1], in1=rstd[:],
                        op0=ALU.mult, op1=ALU.mult), reads=[("xb", k % 3), "gf"] + rk, writes=[("tmp", k % 2)])
                    S.dma("sp", yT[v, k * 128:(k + 1) * 128, :], tmp[k % 2][:], reads=[("tmp", k % 2)], writes=[("yT", k)])
                else:
                    S.op("dve", lambda e, k=k: e.tensor_tensor(out=tmp[k % 2][:], in0=xb[k % 3][:], in1=rstd[:], op=ALU.mult),
                         reads=[("xb", k % 3)] + rk, writes=[("tmp", k % 2)])
                    for (c0, c1, b) in bcols(v):
                        S.op("act", lambda e, k=k, c0=c0, c1=c1, b=b: e.activation(
                            out=hT[:, k, c0:c1], in_=tmp[k % 2][:, c0:c1], func=AF.Identity,
                            bias=adaT[:, Boff + k, b:b + 1], scale=A[:, k, b:b + 1]),
                            reads=[("tmp", k % 2)], writes=[("hT", k)])
            S.barrier()

    def rope_inplace(e_tag, z3, n, H, half, cs, tmps, zkey, cskey="cs"):
        cb = cs[0:n, 0:half].unsqueeze(1).to_broadcast([n, H, half])
        sn = cs[0:n, half:2 * half].unsqueeze(1).to_broadcast([n, H, half])
        x1 = z3[:, :, 0:half]
        x2 = z3[:, :, half:2 * half]
        t = [tm[0:n, 0:H * half].rearrange("p (h d) -> p h d", d=half) for tm in tmps]
        tk = [("ropetmp", e_tag, i) for i in range(4)]
        S.op("dve", lambda e: e.tensor_tensor(out=t[0], in0=x1, in1=cb, op=ALU.mult), reads=[zkey, cskey], writes=[tk[0]])
        S.op("dve", lambda e: e.tensor_tensor(out=t[1], in0=x2, in1=sn, op=ALU.mult), reads=[zkey, cskey], writes=[tk[1]])
        S.op("dve", lambda e: e.tensor_tensor(out=t[2], in0=x2, in1=cb, op=ALU.mult), reads=[zkey, cskey], writes=[tk[2]])
        S.op("dve", lambda e: e.tensor_tensor(out=t[3], in0=x1, in1=sn, op=ALU.mult), reads=[zkey, cskey], writes=[tk[3]])
        S.op("dve", lambda e: e.tensor_tensor(out=x1, in0=t[0], in1=t[1], op=ALU.subtract), reads=[tk[0], tk[1]], writes=[zkey])
        S.op("dve", lambda e: e.tensor_tensor(out=x2, in0=t[2], in1=t[3], op=ALU.add), reads=[tk[2], tk[3]], writes=[zkey])

    def rms_rows(z, n, width, grep, out, zkey, gkey, ss, outkey, tag):
        S.op("act", lambda e: e.activation(out=ss[1][0:n, 0:width], in_=z, func=AF.Square, accum_out=ss[0][0:n, 0:1]),
             reads=[zkey], writes=[("ss", tag)])
        S.op("act", lambda e: e.activation(out=ss[0][0:n, 0:1], in_=ss[0][0:n, 0:1], func=AF.Sqrt,
                                           bias=epsT[0:n, 0:1], scale=1.0 / width),
             reads=[("ss", tag), "epsT"], writes=[("ss", tag)])
        S.op("dve", lambda e: e.reciprocal(out=ss[0][0:n, 0:1], in_=ss[0][0:n, 0:1]), reads=[("ss", tag)], writes=[("ss", tag)])
        S.op("dve", lambda e: e.scalar_tensor_tensor(out=out, in0=z, scalar=ss[0][0:n, 0:1], in1=grep[0:n, 0:width],
                                                     op0=ALU.mult, op1=ALU.mult),
             reads=[zkey, ("ss", tag), gkey], writes=[outkey])

    for v in range(V):
        ph12 = ExitStack()
        hT = ph12.enter_context(nc.sbuf_tensor(f"hT{v}", [128, KD, NT], BF16))
        qlnT = ph12.enter_context(nc.sbuf_tensor(f"qlnT{v}", [128, 8, NT], BF16))
        norm_phase(xT_own[v].rearrange("(k p) t -> k p t", p=128), A1, SH1, hT, tag=f"n1{v}", v=v)
        hkeys = [("hT", k) for k in range(KD)]

        with ExitStack() as ph:
            def psb(name, shape, dt=F32):
                return ph.enter_context(nc.sbuf_tensor(f"p2{v}_{name}", list(shape), dt))
            wp = [psb(f"wp{i}", [128, KD, 512], BF16) for i in range(2)]
            zt = [psb(f"z{i}", [128, 512]) for i in range(3)]
            zb = [psb(f"zb{i}", [128, 512], BF16) for i in range(2)]
            rt = [psb(f"rt{i}", [128, 128]) for i in range(4)]
            ssq = [psb("ss0", [128, 1]), psb("ssj", [128, 1024], BF16)]
            ql = psb("ql", [128, 1024])
            stg = [psb(f"stg{i}", [128, 4, NT], BF16) for i in range(2)]
            csm = psb("csm", [128, 9, 64])
            csd = psb("csd", [128, 9, 32])
            gq = psb("gq", [128, 1024])
            gkv = psb("gkv", [128, 512])
            sstg = psb("sstg", [128, 14, 32], BF16)
            S.dma("sp", gq[:], gq_rep[:, :], writes=["gq"])
            S.dma("sp", gkv[:], gkv_rep[:, :], writes=["gkv"])
            for ti, (t0, n) in enumerate(TT):
                S.dma("sp", csm[0:n, ti, :], cs_mla_own[v, t0:t0 + n, :], writes=["cs"])
                S.dma("sp", csd[0:n, ti, :], cs_dsa_own[v, t0:t0 + n, :], writes=["cs"])
            wv = w_in.rearrange("(k p) c -> p k c", p=128)
            cnt = {"pn": 0, "z": 0, "bank": 0, "tb": 0, "stg": 0, "zb": 0}

            def load_panel(c0, w):
                i = cnt["pn"] % 2
                cnt["pn"] += 1
                S.dma("pool", wp[i][:, :, 0:w], wv[:, :, c0:c0 + w], writes=[("wp", i)])
                return i

            def gemm_tok(pi, w, ti, kk=KD, lhs=None, lkeys=None):
                t0, n = TT[ti]
                bank = cnt["bank"] % 4
                cnt["bank"] += 1
                src = hT if lhs is None else lhs
                mm(bank, (0, w), [(src[:, k, t0:t0 + n], wp[pi][:, k, 0:w]) for k in range(kk)],
                   reads=[("wp", pi)] + (hkeys if lkeys is None else lkeys))
                return bank

            def evac_z(bank, n, w):
                zi = cnt["z"] % 3
                cnt["z"] += 1
                S.op("act", lambda e: e.activation(out=zt[zi][0:n, 0:w], in_=ps[bank][0:n, 0:w], func=AF.Copy),
                     reads=[psk(bank)], writes=[("z", zi)])
                return zi

            def to_bf(zi, n, w):
                bi = cnt["zb"] % 2
                cnt["zb"] += 1
                S.op("act", lambda e: e.activation(out=zb[bi][0:n, 0:w], in_=zt[zi][0:n, 0:w], func=AF.Copy),
                     reads=[("z", zi)], writes=[("zb", bi)])
                return bi

            def tr_blocks(src_tile, skey, n, blocks, dst_fn, dkey):
                for bi_, (c0, wdt) in enumerate(blocks):
                    tb = 4 + cnt["tb"] % 4
                    cnt["tb"] += 1
                    transpose_to(tb, 0, src_tile[0:n, c0:c0 + wdt], n, reads=[skey])
                    S.op("dve", lambda e, tb=tb, bi_=bi_, wdt=wdt: e.tensor_copy(out=dst_fn(bi_), in_=psbf(tb)[0:wdt, 0:n]),
                         reads=[psk(tb)], writes=[dkey])

            qlb = psb("qlb", [128, 1024], BF16)
            pis = [load_panel(O_QL + pnl * 512, 512) for pnl in range(2)]
            for ti, (t0, n) in enumerate(TT):
                for pnl in range(2):
                    bank = gemm_tok(pis[pnl], 512, ti)
                    S.op("act", lambda e, bank=bank, n=n, pnl=pnl: e.activation(
                        out=ql[0:n, pnl * 512:(pnl + 1) * 512], in_=ps[bank][0:n, 0:512], func=AF.Copy),
                        reads=[psk(bank)], writes=[("ql", pnl)])
                S.op("act", lambda e, n=n: e.activation(out=ssq[1][0:n, :], in_=ql[0:n, :], func=AF.Square,
                                                        accum_out=ssq[0][0:n, 0:1]),
                     reads=[("ql", 0), ("ql", 1)], writes=["ssq"])
                S.op("act", lambda e, n=n: e.activation(out=ssq[0][0:n, 0:1], in_=ssq[0][0:n, 0:1], func=AF.Sqrt,
                                                        bias=epsT[0:n, 0:1], scale=1.0 / 1024), reads=["ssq", "epsT"], writes=["ssq"])
                S.op("dve", lambda e, n=n: e.reciprocal(out=ssq[0][0:n, 0:1], in_=ssq[0][0:n, 0:1]), reads=["ssq"], writes=["ssq"])
                S.op("dve", lambda e, n=n: e.scalar_tensor_tensor(
                    out=qlb[0:n, :], in0=ql[0:n, :], scalar=ssq[0][0:n, 0:1], in1=gq[0:n, :], op0=ALU.mult, op1=ALU.mult),
                    reads=["ssq", "gq", ("ql", 0), ("ql", 1)], writes=["qlb"])
                tr_blocks(qlb, "qlb", n, [(c * 128, 128) for c in range(8)],
                          lambda bi_, t0=t0, n=n: qlnT[:, bi_, t0:t0 + n], ("qlnT", ti))
            qkeys = [("qlnT", ti) for ti in range(9)]

            pi = load_panel(O_CKV, 512)
            for ti, (t0, n) in enumerate(TT):
                bank = gemm_tok(pi, 512, ti)
                zi = evac_z(bank, n, 512)
                zo = (zi + 1) % 3
                cnt["z"] += 1
                rms_rows(zt[zi][0:n, :], n, 512, gkv, zt[zo][0:n, :], ("z", zi), "gkv", ssq, ("z", zo), "ckv")
                S.dma("sp", o_ckv[v, t0:t0 + n, :], zt[zo][0:n, :], reads=[("z", zo)], writes=[("o_ckv", ti)])
                if ti == 8:
                    bi = to_bf(zo, n, 512)
                    tr_blocks(zb[bi], ("zb", bi), n, [(c * 128, 128) for c in range(4)],
                              lambda bi_: sstg[:, bi_, 0:32], "sstg")
            pi = load_panel(O_KR, 64)
            for ti, (t0, n) in enumerate(TT):
                bank = gemm_tok(pi, 64, ti)
                zi = evac_z(bank, n, 64)
                rope_inplace("a", zt[zi][0:n, 0:64].rearrange("p (h d) -> p h d", h=1), n, 1, 32, csm[:, ti, :], rt, ("z", zi))
                S.dma("sp", o_kr[v, t0:t0 + n, :], zt[zi][0:n, 0:64], reads=[("z", zi)], writes=[("o_kr", ti)])
                if ti == 8:
                    bi = cnt["zb"] % 2
                    cnt["zb"] += 1
                    for hh in range(2):
                        S.op("act", lambda e, hh=hh, bi=bi, zi=zi, n=n: e.activation(
                            out=zb[bi][0:n, hh * 64:(hh + 1) * 64], in_=zt[zi][0:n, 0:64], func=AF.Copy),
                            reads=[("z", zi)], writes=[("zb", bi)])
                    tr_blocks(zb[bi], ("zb", bi), n, [(0, 128)], lambda bi_: sstg[:, 4, 0:32], "sstg")
            for pnl in range(2):
                pi = load_panel(O_KB + pnl * 512, 512)
                for ti, (t0, n) in enumerate(TT):
                    bank = gemm_tok(pi, 512, ti)
                    zi = evac_z(bank, n, 512)
                    rope_inplace("a", zt[zi][0:n, :].rearrange("p (h d) -> p h d", h=4), n, 4, 16, csd[:, ti, :], rt, ("z", zi))
                    S.dma("sp", o_k[v, t0:t0 + n, pnl * 512:(pnl + 1) * 512], zt[zi][0:n, :], reads=[("z", zi)],
                          writes=[("o_k", ti, pnl)])
                    if ti == 8:
                        bi = to_bf(zi, n, 512)
                        tr_blocks(zb[bi], ("zb", bi), n, [(c * 128, 128) for c in range(4)],
                                  lambda bi_, pnl=pnl: sstg[:, 5 + pnl * 4 + bi_, 0:32], "sstg")
            for pnl in range(2):
                pi = load_panel(O_VB + pnl * 512, 512)
                for ti, (t0, n) in enumerate(TT):
                    bank = gemm_tok(pi, 512, ti)
                    zi = evac_z(bank, n, 512)
                    S.dma("sp", o_v[v, t0:t0 + n, pnl * 512:(pnl + 1) * 512], zt[zi][0:n, :], reads=[("z", zi)],
                          writes=[("o_v", ti, pnl)])
                    if ti == 8:
                        bi = to_bf(zi, n, 512)
                        for b in range(2):
                            S.dma("sp", SVB[2 * v + b, PAST:PAST + 16, pnl * 512:(pnl + 1) * 512], zb[bi][b * 16:(b + 1) * 16, :],
                                  reads=[("zb", bi)], writes=[("SVBn", b, pnl)])
            pi = load_panel(O_KI, 128)
            for ti, (t0, n) in enumerate(TT):
                bank = gemm_tok(pi, 128, ti)
                zi = evac_z(bank, n, 128)
                rope_inplace("a", zt[zi][0:n, 0:128].rearrange("p (h d) -> p h d", h=1), n, 1, 16, csd[:, ti, :], rt, ("z", zi))
                S.dma("sp", o_ki[v, t0:t0 + n, :], zt[zi][0:n, 0:128], reads=[("z", zi)], writes=[("o_ki", ti)])
                if ti == 8:
                    bi = to_bf(zi, n, 128)
                    tr_blocks(zb[bi], ("zb", bi), n, [(0, 128)], lambda bi_: sstg[:, 13, 0:32], "sstg")
            for b in range(2):
                S.dma("sp", SCKVT[2 * v + b].rearrange("c p l -> p c l")[:, :, PAST:LS], sstg[:, 0:4, b * 16:(b + 1) * 16],
                      reads=["sstg"], writes=[("SCKVTn", b)])
                S.dma("sp", SKRT[2 * v + b][:, PAST:LS], sstg[:, 4, b * 16:(b + 1) * 16], reads=["sstg"], writes=[("SKRTn", b)])
                S.dma("sp", SKBT[2 * v + b].rearrange("c p l -> p c l")[:, :, PAST:LS], sstg[:, 5:13, b * 16:(b + 1) * 16],
                      reads=["sstg"], writes=[("SKBTn", b)])
                S.dma("sp", SKIT[2 * v + b][:, PAST:LS], sstg[:, 13, b * 16:(b + 1) * 16], reads=["sstg"], writes=[("SKITn", b)])
            pi = load_panel(O_WI, 32)
            for ti, (t0, n) in enumerate(TT):
                bank = gemm_tok(pi, 32, ti)
                S.op("act", lambda e, bank=bank, ti=ti, n=n: e.activation(out=WI[0:n, v, ti, :], in_=ps[bank][0:n, 0:32],
                                                                        func=AF.Copy, scale=IDX_W_SCALE),
                     reads=[psk(bank)], writes=[("WI", ti)])
            for (off, dst, dname) in ((O_QB, QBT[v], "QBT"), (O_QI, QIT[v], "QIT")):
                for pnl in range(8):
                    pi = load_panel(off + pnl * 512, 512)
                    si = cnt["stg"] % 2
                    cnt["stg"] += 1
                    for ti, (t0, n) in enumerate(TT):
                        bank = gemm_tok(pi, 512, ti)
                        zi = evac_z(bank, n, 512)
                        rope_inplace("a", zt[zi][0:n, :].rearrange("p (h d) -> p h d", h=4), n, 4, 16, csd[:, ti, :], rt, ("z", zi))
                        bi = to_bf(zi, n, 512)
                        tr_blocks(zb[bi], ("zb", bi), n, [(c * 128, 128) for c in range(4)],
                                  lambda bi_, si=si, t0=t0, n=n: stg[si][:, bi_, t0:t0 + n], ("stg", si))
                    S.dma("sp", dst[pnl * 4:(pnl + 1) * 4].rearrange("h p t -> p h t"), stg[si][:],
                          reads=[("stg", si)], writes=[(dname, pnl)])
            for (off, dst, dname) in ((O_GA, GAT[v], "GAT"), (O_GB, GBT[v], "GBT")):
                for pnl in range(8):
                    pi = load_panel(off + pnl * 512, 512)
                    si = cnt["stg"] % 2
                    cnt["stg"] += 1
                    for cc in range(4):
                        for gi, (c0, n) in enumerate(TGRP):
                            bank = cnt["bank"] % 4
                            cnt["bank"] += 1
                            mm(bank, (0, n), [(wp[pi][:, k, cc * 128:(cc + 1) * 128], hT[:, k, c0:c0 + n]) for k in range(KD)],
                               reads=[("wp", pi)] + hkeys)
                            S.op("act", lambda e, bank=bank, si=si, cc=cc, c0=c0, n=n: e.activation(
                                out=stg[si][:, cc, c0:c0 + n], in_=ps[bank][:, 0:n], func=AF.Sigmoid),
                                reads=[psk(bank)], writes=[("stg", si)])
                    S.dma("sp", dst[pnl * 4:(pnl + 1) * 4].rearrange("h p t -> p h t"), stg[si][:],
                          reads=[("stg", si)], writes=[(dname, pnl)])
            wq = w_uq.rearrange("(k p) c -> p k c", p=128)
            for pr in range(16):
                i = cnt["pn"] % 2
                cnt["pn"] += 1
                S.dma("pool", wp[i][:, 0:8, 0:384], wq[:, :, pr * 384:(pr + 1) * 384], writes=[("wp", i)])
                si = cnt["stg"] % 2
                cnt["stg"] += 1
                for ti, (t0, n) in enumerate(TT):
                    bank = gemm_tok(i, 384, ti, kk=8, lhs=qlnT, lkeys=qkeys)
                    zi = evac_z(bank, n, 384)
                    rope_inplace("a", zt[zi][0:n, 0:384].rearrange("p (h d) -> p h d", h=2)[:, :, 128:192], n, 2, 32,
                                 csm[:, ti, :], rt, ("z", zi))
                    bi = cnt["zb"] % 2
                    cnt["zb"] += 1
                    for (d0, s0, wd) in ((0, 0, 128), (128, 192, 128), (256, 128, 64), (320, 320, 64)):
                        S.op("act", lambda e, d0=d0, s0=s0, wd=wd, bi=bi, zi=zi, n=n: e.activation(
                            out=zb[bi][0:n, d0:d0 + wd], in_=zt[zi][0:n, s0:s0 + wd], func=AF.Copy),
                            reads=[("z", zi)], writes=[("zb", bi)])
                    tr_blocks(zb[bi], ("zb", bi), n, [(0, 128), (128, 128), (256, 128)],
                              lambda bi_, si=si, t0=t0, n=n: stg[si][:, bi_, t0:t0 + n], ("stg", si))
                S.dma("sp", QNT[v, pr * 2:(pr + 1) * 2].rearrange("h p t -> p h t"), stg[si][:, 0:2, :],
                      reads=[("stg", si)], writes=[("QNT", pr)])
                S.dma("sp", QRT[v, pr], stg[si][:, 2, :], reads=[("stg", si)], writes=[("QRT", pr)])
            S.barrier()
        ph12.close()

    KOFF = [(O_CKV, 512), (O_KR, 64), (O_KB, 512), (O_KB + 512, 512), (O_VB, 512), (O_VB + 512, 512), (O_KI, 128)]
    KPOS = []
    acc_ = 0
    for (o, w) in KOFF:
        KPOS.append(acc_)
        acc_ += w
    assert acc_ == 2752
    with ExitStack() as ph:
        def psb(name, shape, dt=F32):
            return ph.enter_context(nc.sbuf_tensor(f"p3_{name}", list(shape), dt))
        G = 512
        NTI = G // 128
        hg = psb("hg", [128, KD, G], BF16)
        wp = [psb(f"wp{i}", [128, KD, 512], BF16) for i in range(2)]
        sq = [psb(f"sq{i}", [128, G], BF16) for i in range(2)]
        rstd = psb("rstd", [128, G])
        tmp = [psb(f"tmp{i}", [128, G]) for i in range(2)]
        zt = [psb(f"z{i}", [128, 512]) for i in range(3)]
        zb = [psb(f"zb{i}", [128, 512], BF16) for i in range(2)]
        rt = [psb(f"rt{i}", [128, 128]) for i in range(4)]
        ssq = [psb("ss0", [128, 1]), psb("ssj", [128, 512])]
        gkv = psb("gkv", [128, 512])
        csm = psb("csm", [128, 4, 64])
        csd = psb("csd", [128, 4, 32])
        tstg = psb("tstg", [128, 14, 1024], BF16)
        vstg = psb("vstg", [128, 8, 1024], BF16)
        S.dma("sp", gkv[:], gkv_rep[:, :], writes=["gkv"])
        wv = w_in.rearrange("(k p) c -> p k c", p=128)
        for pi_, (o, w) in enumerate(KOFF):
            S.dma("pool", wp[pi_ % 2][:, :, 0:w], wv[:, :, o:o + w], writes=[("wp", pi_ % 2)])
            S.dma("sp", WKB[:, :, KPOS[pi_]:KPOS[pi_] + w], wp[pi_ % 2][:, :, 0:w], reads=[("wp", pi_ % 2)], writes=[("WKB", pi_)])
        for b in range(2 * V):
            S.dma("pool", tstg[:, 0:4, 0:PAST], c_ckvT[b].rearrange("(c p) l -> p c l", p=128), writes=["tstg"])
            S.dma("sp", SCKVT[b].rearrange("c p l -> p c l")[:, :, 0:PAST], tstg[:, 0:4, 0:PAST], reads=["tstg"], writes=[("SCKVTc", b)])
            for hh in range(2):
                S.dma("pool", tstg[hh * 64:(hh + 1) * 64, 4, 0:PAST], c_krT[b], writes=["tstg"])
            S.dma("sp", SKRT[b][:, 0:PAST], tstg[:, 4, 0:PAST], reads=["tstg"], writes=[("SKRTc", b)])
            S.dma("pool", tstg[:, 5:13, 0:PAST], c_kT[b].rearrange("g p l -> p g l"), writes=["tstg"])
            S.dma("sp", SKBT[b].rearrange("c p l -> p c l")[:, :, 0:PAST], tstg[:, 5:13, 0:PAST], reads=["tstg"], writes=[("SKBTc", b)])
            S.dma("pool", tstg[:, 13, 0:PAST], c_kiT[b], writes=["tstg"])
            S.dma("sp", SKIT[b][:, 0:PAST], tstg[:, 13, 0:PAST], reads=["tstg"], writes=[("SKITc", b)])
            S.dma("pool", vstg[:], c_v[b].rearrange("(t p) d -> p t d", p=128), writes=["vstg"])
            S.dma("sp", SVB[b, 0:PAST, :].rearrange("(t p) d -> p t d", p=128), vstg[:], reads=["vstg"], writes=[("SVBc", b)])
        S.barrier()
        xav = xT_all.rearrange("(k p) t -> p k t", p=128)
        cnt = {"z": 0, "bank": 0, "tb": 0, "zb": 0, "pn": 0}
        for g in range(SEQ // G):
            gc = g * G
            for q4 in range(4):
                S.dma("pool", hg[:, q4 * 8:(q4 + 1) * 8, :], xav[:, q4 * 8:(q4 + 1) * 8, gc:gc + G],
                      writes=[("hg", k) for k in range(q4 * 8, q4 * 8 + 8)])
            for ti in range(NTI):
                S.dma("sp", csm[:, ti, :], cs_mla_all[gc + ti * 128:gc + (ti + 1) * 128, :], writes=[("csm", ti)])
                S.dma("sp", csd[:, ti, :], cs_dsa_all[gc + ti * 128:gc + (ti + 1) * 128, :], writes=[("csd", ti)])
            for k in range(KD):
                S.op("act", lambda e, k=k: e.activation(out=sq[k % 2][:], in_=hg[:, k, :], func=AF.Square),
                     reads=[("hg", k)], writes=[("sq", k % 2)])
                S.op("pe", lambda pe, k=k: pe.matmul(ps[0][:, 0:G], lhsT=ones[:, :], rhs=sq[k % 2][:, :],
                                                     start=(k == 0), stop=(k == KD - 1)),
                     reads=[("sq", k % 2), "ones"], writes=[psk(0)])
            S.op("act", lambda e: e.activation(out=rstd[:, :], in_=ps[0][:, 0:G], func=AF.Sqrt, bias=epsT[:, 0:1], scale=1.0 / D),
                 reads=[psk(0), "epsT"], writes=["rstd"])
            S.op("dve", lambda e: e.reciprocal(out=rstd[:, :], in_=rstd[:, :]), reads=["rstd"], writes=["rstd"])
            for k in range(KD):
                S.op("dve", lambda e, k=k: e.tensor_tensor(out=tmp[k % 2][:], in0=hg[:, k, :], in1=rstd[:], op=ALU.mult),
                     reads=[("hg", k), "rstd"], writes=[("tmp", k % 2)])
                S.op("act", lambda e, k=k: e.activation(out=hg[:, k, :], in_=tmp[k % 2][:], func=AF.Identity,
                                                        bias=adaT[:, SH1 + k, 0:1], scale=A1[:, k, 0:1]),
                     reads=[("tmp", k % 2)], writes=[("hg", k)])
            hgk = [("hg", k) for k in range(KD)]
            for pi_, (o, w) in enumerate(KOFF):
                i = cnt["pn"] % 2
                cnt["pn"] += 1
                S.dma("pool", wp[i][:, :, 0:w], WKB[:, :, KPOS[pi_]:KPOS[pi_] + w],
                      reads=[("WKB", p_) for p_ in range(len(KOFF))], writes=[("wp", i)])
                for ti in range(NTI):
                    t0 = ti * 128
                    bank = 1 + cnt["bank"] % 3
                    cnt["bank"] += 1
                    mm(bank, (0, w), [(hg[:, k, t0:t0 + 128], wp[i][:, k, 0:w]) for k in range(KD)], reads=[("wp", i)] + hgk)
                    zi = cnt["z"] % 3
                    cnt["z"] += 1
                    S.op("act", lambda e, zi=zi, bank=bank, w=w: e.activation(out=zt[zi][:, 0:w], in_=ps[bank][:, 0:w], func=AF.Copy),
                         reads=[psk(bank)], writes=[("z", zi)])
                    blocks = None
                    if pi_ == 0:
                        zo = (zi + 1) % 3
                        cnt["z"] += 1
                        rms_rows(zt[zi][:, :], 128, 512, gkv, zt[zo][:, :], ("z", zi), "gkv", ssq, ("z", zo), "ckv")
                        zi = zo
                        blocks, sbase = [(c * 128, 128) for c in range(4)], 0
                    elif pi_ == 1:
                        rope_inplace("a", zt[zi][:, 0:64].rearrange("p (h d) -> p h d", h=1), 128, 1, 32, csm[:, ti, :], rt, ("z", zi), ("csm", ti))
                    elif pi_ in (2, 3):
                        rope_inplace("a", zt[zi][:, :].rearrange("p (h d) -> p h d", h=4), 128, 4, 16, csd[:, ti, :], rt, ("z", zi), ("csd", ti))
                        blocks, sbase = [(c * 128, 128) for c in range(4)], 5 + (pi_ - 2) * 4
                    elif pi_ == 6:
                        rope_inplace("a", zt[zi][:, 0:128].rearrange("p (h d) -> p h d", h=1), 128, 1, 16, csd[:, ti, :], rt, ("z", zi), ("csd", ti))
                        blocks, sbase = [(0, 128)], 13
                    if pi_ in (4, 5):
                        S.op("act", lambda e, zi=zi, ti=ti, pi_=pi_: e.activation(
                            out=vstg[:, ti, (pi_ - 4) * 512:(pi_ - 3) * 512], in_=zt[zi][:, :], func=AF.Copy),
                            reads=[("z", zi)], writes=[("vstg", ti, pi_)])
                        continue
                    bi = cnt["zb"] % 2
                    cnt["zb"] += 1
                    if pi_ == 1:
                        for hh in range(2):
                            S.op("act", lambda e, hh=hh, bi=bi, zi=zi: e.activation(
                                out=zb[bi][:, hh * 64:(hh + 1) * 64], in_=zt[zi][:, 0:64], func=AF.Copy),
                                reads=[("z", zi)], writes=[("zb", bi)])
                        blocks, sbase = [(0, 128)], 4
                    else:
                        S.op("act", lambda e, bi=bi, zi=zi, w=w: e.activation(out=zb[bi][:, 0:w], in_=zt[zi][:, 0:w], func=AF.Copy),
                             reads=[("z", zi)], writes=[("zb", bi)])
                    for bi_, (c0, wdt) in enumerate(blocks):
                        tb = 4 + cnt["tb"] % 4
                        cnt["tb"] += 1
                        transpose_to(tb, 0, zb[bi][:, c0:c0 + wdt], 128, reads=[("zb", bi)])
                        S.op("dve", lambda e, tb=tb, bi_=bi_, sbase=sbase, t0=t0: e.tensor_copy(
                            out=tstg[:, sbase + bi_, t0:t0 + 128], in_=psbf(tb)[:, 0:128]),
                            reads=[psk(tb)], writes=[("tstg", sbase + bi_, ti)])
            tk = lambda lo, hi: [("tstg", s_, ti_) for s_ in range(lo, hi) for ti_ in range(NTI)]
            S.dma("sp", CKVT.rearrange("c p l -> p c l")[:, :, gc:gc + G], tstg[:, 0:4, 0:G], reads=tk(0, 4), writes=[("CKVT", g)])
            S.dma("sp", KRT[:, gc:gc + G], tstg[:, 4, 0:G], reads=tk(4, 5), writes=[("KRT", g)])
            S.dma("sp", KBT.rearrange("c p l -> p c l")[:, :, gc:gc + G], tstg[:, 5:13, 0:G], reads=tk(5, 13), writes=[("KBT", g)])
            S.dma("sp", KIT[:, gc:gc + G], tstg[:, 13, 0:G], reads=tk(13, 14), writes=[("KIT", g)])
            S.dma("sp", VB[gc:gc + G, :].rearrange("(t p) d -> p t d", p=128), vstg[:, 0:NTI, :],
                  reads=[("vstg", ti_, p_) for ti_ in range(NTI) for p_ in (4, 5)], writes=[("VB", g)])
        S.barrier()

    with ExitStack() as ph:
        def psb(name, shape, dt=F32):
            return ph.enter_context(nc.sbuf_tensor(f"p4_{name}", list(shape), dt))
        kit = psb("kit", [128, SEQ], BF16)
        skit = psb("skit", [128, 2 * V, LS], BF16)
        qi = [psb(f"qi{i}", [128, 32, 128], BF16) for i in range(2)]
        SA = psb("SA", [128, SEQ])
        Wk = psb("Wk", [128, SEQ])
        mk = psb("mk", [128, SEQ], BF16)
        rr = [psb(f"rr{i}", [128, 512]) for i in range(4)]
        m8 = psb("m8", [128, 8])
        thr = psb("thr", [128, 1])
        mstg = psb("mstg", [128, 64, 128], BF16)
        S.dma("sp", kit[:], KIT[:, :], writes=["kit"])
        for b in range(2 * V):
            S.dma("sp", skit[:, b, :], SKIT[b], writes=["kit"])
        cnt = {"bank": 0, "rr": 0, "tb": 0}

        def index_tile(v, j, qcols, nq, keys_ap_fn, L, wi_ap, adm, out_fn, wkey):
            qb_ = qi[j % 2]
            S.dma("sp", qb_[:, :, 0:nq], QIT[v].rearrange("h p t -> p h t")[:, :, qcols:qcols + nq], writes=[("qi", j % 2)])
            blocks = [(c0, min(512, L - c0)) for c0 in range(0, L, 512)]
            for (c0, w) in blocks:
                for h in range(32):
                    bank = cnt["bank"] % 4
                    cnt["bank"] += 1
                    mm(bank, (0, w), [(qb_[:, h, 0:nq], keys_ap_fn(c0, w))], reads=[("qi", j % 2), "kit"])
                    ri = cnt["rr"] % 4
                    cnt["rr"] += 1
                    S.op("act", lambda e, bank=bank, ri=ri, w=w: e.activation(out=rr[ri][0:nq, 0:w], in_=ps[bank][0:nq, 0:w], func=AF.Relu),
                         reads=[psk(bank)], writes=[("rr", ri)])
                    if h == 0:
                        S.op("dve", lambda e, ri=ri, c0=c0, w=w, h=h: e.tensor_scalar(
                            out=SA[0:nq, c0:c0 + w], in0=rr[ri][0:nq, 0:w], scalar1=wi_ap[:, h:h + 1], scalar2=None, op0=ALU.mult),
                            reads=[("rr", ri), wkey], writes=[("SA", c0)])
                    else:
                        S.op("dve", lambda e, ri=ri, c0=c0, w=w, h=h: e.scalar_tensor_tensor(
                            out=SA[0:nq, c0:c0 + w], in0=rr[ri][0:nq, 0:w], scalar=wi_ap[:, h:h + 1], in1=SA[0:nq, c0:c0 + w],
                            op0=ALU.mult, op1=ALU.add), reads=[("rr", ri), ("SA", c0), wkey], writes=[("SA", c0)])
            sak = [("SA", c0) for (c0, w) in blocks]
            if adm:
                S.op("dve", lambda e: e.tensor_tensor(out=SA[0:nq, L - 1024:L], in0=SA[0:nq, L - 1024:L], in1=admA[0:nq, v, :], op=ALU.add),
                     reads=sak + ["admA"], writes=sak)
            cur = SA
            for r in range(TOPK // 8):
                S.op("dve", lambda e, cur=cur: e.max(out=m8[0:nq, :], in_=cur[0:nq, 0:L]), reads=sak + ["Wk"], writes=["m8"])
                if r < TOPK // 8 - 1:
                    S.op("dve", lambda e, cur=cur: e.match_replace(out=Wk[0:nq, 0:L], in_to_replace=m8[0:nq, :],
                                                                   in_values=cur[0:nq, 0:L], imm_value=NEG),
                         reads=sak + ["m8", "Wk"], writes=["Wk"])
                    cur = Wk
            S.op("dve", lambda e: e.tensor_reduce(out=thr[0:nq, :], in_=m8[0:nq, :], axis=AX.X, op=ALU.min), reads=["m8"], writes=["thr"])
            S.op("dve", lambda e: e.tensor_scalar(out=thr[0:nq, :], in0=thr[0:nq, :], scalar1=-1e29, scalar2=None, op0=ALU.max),
                 reads=["thr"], writes=["thr"])
            S.op("dve", lambda e: e.tensor_scalar(out=mk[0:nq, 0:L], in0=SA[0:nq, 0:L], scalar1=thr[0:nq, 0:1], scalar2=None, op0=ALU.is_ge),
                 reads=sak + ["thr"], writes=["mk"])
            nblk = (L + 127) // 128
            for kb in range(nblk):
                kw = min(128, L - kb * 128)
                tb = 4 + cnt["tb"] % 4
                cnt["tb"] += 1
                transpose_to(tb, 0, mk[0:nq, kb * 128:kb * 128 + kw], nq, reads=["mk"])
                S.op("act", lambda e, tb=tb, kb=kb, kw=kw: e.activation(out=mstg[0:kw, kb, 0:nq], in_=psbf(tb)[0:kw, 0:nq], func=AF.Copy),
                     reads=[psk(tb)], writes=[("mstg", kb)])
            out_fn(nblk, [("mstg", kb) for kb in range(nblk)])

        wis = [psb(f"wis{b}", [16, 32]) for b in range(2)]
        for v in range(V):
            for j in range(8):
                L = 1024 * (j + 1)
                index_tile(v, j, j * 128, 128, lambda c0, w: kit[:, c0:c0 + w], L, WI[:, v, j, :], True,
                           lambda nblk, keys, j=j, v=v: S.dma("sp", MASKT[v, 0:nblk].rearrange("k p q -> p k q")[:, :, j * 128:(j + 1) * 128],
                                                             mstg[:, 0:nblk, :], reads=keys, writes=[("MASKT", j)]), "WIp")
            for b in range(2):
                def outs(nblk, keys, b=b, v=v):
                    S.dma("sp", SMASKT[2 * v + b, 0:8].rearrange("k p q -> p k q"), mstg[:, 0:8, 0:16], reads=keys, writes=[("SMASKT", b, 0)])
                    S.dma("sp", SMASKT[2 * v + b, 8, 0:16, :], mstg[0:16, 8, 0:16], reads=keys, writes=[("SMASKT", b, 1)])
                S.dma("sp", wis[b][:], WI[b * 16:(b + 1) * 16, v, 8, :], writes=[("wis", b)])
                index_tile(v, 8 + b, 1024 + b * 16, 16, lambda c0, w, b=b, v=v: skit[:, 2 * v + b, c0:c0 + w], LS, wis[b], False, outs, ("wis", b))
        S.barrier()

    def attention(tagp, sbufs, kparts_fn, v_fn, kblocks, q0_fn, QA, QB, scale, mask_fn, rdeps, fin_fn):
        pT = sbufs["pT"]
        nkb = len(kblocks)

        def s_mm(i):
            kb, kw = kblocks[i]
            q0 = q0_fn(kb)
            par = (i % 2) * 2
            parts = kparts_fn(kb, kw)
            for gi, (c0, c1) in enumerate(((QA, QA + 512), (QA + 512, QB))):
                a, bb = max(c0, q0), min(c1, QB)
                if a >= bb:
                    continue

                def fn(pe, a=a, bb=bb, gi=gi):
                    ins = None
                    for pi_, (l, rfn) in enumerate(parts):
                        ins = pe.matmul(ps[par + gi][0:kw, a - c0:bb - c0], lhsT=l, rhs=rfn(a, bb),
                                        start=(pi_ == 0), stop=(pi_ == len(parts) - 1))
                    return ins
                S.op("pe", fn, reads=rdeps, writes=[psk(par + gi)])

        s_mm(0)
        for i, (kb, kw) in enumerate(kblocks):
            if i + 1 < nkb:
                s_mm(i + 1)
            q0 = q0_fn(kb)
            par = (i % 2) * 2
            pt = pT[i % 2]
            ptk = ("pT", i % 2)
            for gi, (c0, c1) in enumerate(((QA, QA + 512), (QA + 512, QB))):
                a, bb = max(c0, q0), min(c1, QB)
                if a >= bb:
                    continue
                S.op("act", lambda e, a=a, bb=bb, c0=c0, gi=gi, par=par, pt=pt, kw=kw: e.activation(
                    out=pt[0:kw, a - QA:bb - QA], in_=ps[par + gi][0:kw, a - c0:bb - c0], func=AF.Exp, scale=scale),
                    reads=[psk(par + gi)], writes=[ptk])
            mask_fn(kb, kw, q0, pt, ptk)
            for gi, (c0, c1) in enumerate(((QA, QA + 512), (QA + 512, QB))):
                a, bb = max(c0, q0), min(c1, QB)
                if a >= bb:
                    continue

                def fn(pe, a=a, bb=bb, c0=c0, gi=gi, kb=kb, kw=kw, i=i, pt=pt):
                    pe.matmul(ps[4 + gi][:, a - c0:bb - c0], lhsT=v_fn(kb, kw), rhs=pt[0:kw, a - QA:bb - QA],
                              start=(i == 0), stop=(i == nkb - 1))
                    return pe.matmul(ps[6 + gi][:, a - c0:bb - c0], lhsT=ones[0:kw, :], rhs=pt[0:kw, a - QA:bb - QA],
                                     start=(i == 0), stop=(i == nkb - 1))
                S.op("pe", fn, reads=[ptk, "ones"] + rdeps, writes=[psk(4 + gi), psk(6 + gi)])
        rec = sbufs["rec"]
        for gi, (c0, c1) in enumerate(((QA, QA + 512), (QA + 512, QB))):
            if c0 >= QB:
                continue
            n = min(c1, QB) - c0
            S.op("dve", lambda e, gi=gi, n=n, c0=c0: e.reciprocal(out=rec[:, c0 - QA:c0 - QA + n], in_=ps[6 + gi][:, 0:n]),
                 reads=[psk(6 + gi)], writes=[("rec", gi)])
            fin_fn(gi, c0, n, rec)

    with ExitStack() as ph:
        def psb(name, shape, dt=F32):
            return ph.enter_context(nc.sbuf_tensor(f"p5_{name}", list(shape), dt))
        ckvT = psb("ckvT", [128, 4, SEQ], BF16)
        krT = psb("krT", [128, SEQ], BF16)
        sckvT = psb("sckvT", [128, 2 * V, 4, LS], BF16)
        skrT = psb("skrT", [128, 2 * V, LS], BF16)
        kn = psb("kn", [128, SEQ], BF16)
        vh = psb("vh", [128, 64, 128], BF16)
        wuk = [psb(f"wuk{i}", [128, 4, 128], BF16) for i in range(2)]
        wuv = [psb(f"wuv{i}", [128, 4, 128], BF16) for i in range(2)]
        qn = [psb(f"qn{i}", [128, NT], BF16) for i in range(V)]
        qr = [psb(f"qr{i}", [128, NT], BF16) for i in range(V)]
        ga = [psb(f"ga{i}", [128, NT], BF16) for i in range(V)]
        pT = [psb(f"pT{i}", [128, 1024], BF16) for i in range(2)]
        rec = psb("rec", [128, 1024])
        otmp = psb("otmp", [128, 512])
        mo = [psb(f"mo{i}", [128, NT], BF16) for i in range(V)]
        for c in range(4):
            S.dma("sp", ckvT[:, c, :], CKVT[c], writes=[("ckvT", c)])
        S.dma("sp", krT[:], KRT[:, :], writes=["krT"])
        for b in range(2 * V):
            S.dma("sp", sckvT[:, b, :, :], SCKVT[b].rearrange("c p l -> p c l"), writes=[("sckvT", b)])
            S.dma("sp", skrT[:, b, :], SKRT[b], writes=[("skrT", b)])
        ckeys = [("ckvT", c) for c in range(4)] + [("sckvT", b) for b in range(2 * V)]
        krkeys = ["krT"] + [("skrT", b) for b in range(2 * V)]
        wukv = w_uk.rearrange("(c p) n -> p c n", p=128)
        wuvv = w_uv.rearrange("(c p) n -> p c n", p=128)
        bufs = {"pT": pT, "rec": rec}
        PBLK = [(kb, 128) for kb in range(64)]
        SBLK = [(kb, 128) for kb in range(8)] + [(8, 16)]
        for h in range(32):
            hb = h % 2
            rp = (h % 2) * 64
            S.dma("pool", wuk[hb][:], wukv[:, :, h * 128:(h + 1) * 128], writes=[("wuk", hb)])
            S.dma("pool", wuv[hb][:], wuvv[:, :, h * 128:(h + 1) * 128], writes=[("wuv", hb)])
            for v in range(V):
                S.dma("sp", qn[v][:], QNT[v, h], writes=[("qn", v)])
                S.dma("sp", qr[v][:], QRT[v, h // 2], writes=[("qr", v)])
                S.dma("sp", ga[v][:], GAT[v, h], writes=[("ga", v)])

            def materialize(cfn, L, kblocks):
                for c0 in range(0, L, 512):
                    w = min(512, L - c0)
                    bank = (c0 // 512) % 4
                    mm(bank, (0, w), [(wuk[hb][:, c, :], cfn(c, c0, w)) for c in range(4)], reads=[("wuk", hb)] + ckeys)
                    S.op("act", lambda e, bank=bank, c0=c0, w=w: e.activation(out=kn[:, c0:c0 + w], in_=ps[bank][:, 0:w], func=AF.Copy),
                         reads=[psk(bank)], writes=["kn"])
                for k4 in range(0, len(kblocks), 4):
                    bank = 4 + (k4 // 4) % 4
                    blk = kblocks[k4:k4 + 4]

                    def fn(pe, blk=blk, bank=bank):
                        ins = None
                        for bi_, (kb, kw) in enumerate(blk):
                            for c in range(4):
                                ins = pe.matmul(ps[bank][0:kw, bi_ * 128:(bi_ + 1) * 128], lhsT=cfn(c, kb * 128, kw), rhs=wuv[hb][:, c, :],
                                                start=(c == 0), stop=(c == 3))
                        return ins
                    S.op("pe", fn, reads=[("wuv", hb)] + ckeys, writes=[psk(bank)])
                    for bi_, (kb, kw) in enumerate(blk):
                        S.op("dve", lambda e, bank=bank, bi_=bi_, kb=kb, kw=kw: e.tensor_copy(
                            out=vh[0:kw, kb, :], in_=ps[bank][0:kw, bi_ * 128:(bi_ + 1) * 128]), reads=[psk(bank)], writes=["vh"])

            def run_set(v, is_prompt, kr_src, QA, QB, kblocks, q0_fn):
                def kparts(kb, kw):
                    return [(kn[:, kb * 128:kb * 128 + kw], lambda a, bb: qn[v][:, a:bb]),
                            (kr_src(kb * 128, kw), lambda a, bb: qr[v][rp:rp + 64, a:bb])]

                def mask_fn(kb, kw, q0, pt, ptk):
                    if is_prompt:
                        S.op("dve", lambda e: e.tensor_scalar(out=pt[:, q0:q0 + 64], in0=pt[:, q0:q0 + 64],
                                                             scalar1=maskB[:, v, kb % 4:kb % 4 + 1], scalar2=None, op0=ALU.mult),
                             reads=[ptk, "maskB"], writes=[ptk])

                def fin(gi, c0, n, rec_):
                    S.op("dve", lambda e: e.tensor_tensor(out=otmp[:, 0:n], in0=ps[4 + gi][:, 0:n], in1=rec_[:, c0 - QA:c0 - QA + n], op=ALU.mult),
                         reads=[psk(4 + gi), ("rec", gi)], writes=["otmp"])
                    S.op("dve", lambda e: e.tensor_tensor(out=mo[v][:, c0:c0 + n], in0=otmp[:, 0:n], in1=ga[v][:, c0:c0 + n], op=ALU.mult),
                         reads=["otmp", ("ga", v)], writes=[("mo", v, c0)])
                    mokeys[v].append(("mo", v, c0))
                attention("mla", bufs, kparts, lambda kb, kw: vh[0:kw, kb, :], kblocks, q0_fn, QA, QB, MLA_SCALE, mask_fn,
                          ["kn", "vh", ("qn", v), ("qr", v)] + krkeys, fin)

            mokeys = [[] for _ in range(V)]
            materialize(lambda c, a, w: ckvT[:, c, a:a + w], SEQ, PBLK)
            for v in range(V):
                run_set(v, True, lambda a, w: krT[rp:rp + 64, a:a + w], 0, NP, PBLK, lambda kb: 64 * (kb // 4))
            for v in range(V):
                for b in range(2):
                    sb_ = 2 * v + b
                    materialize(lambda c, a, w, sb_=sb_: sckvT[:, sb_, c, a:a + w], LS, SBLK)
                    run_set(v, False, lambda a, w, sb_=sb_: skrT[rp:rp + 64, sb_, a:a + w], NP + 16 * b, NP + 16 * b + 16, SBLK, lambda kb: 0)
            for v in range(V):
                S.dma("sp", MAT[v, h], mo[v][:], reads=mokeys[v], writes=[("MAT", v, h)])
        S.barrier()

    with ExitStack() as ph:
        def psb(name, shape, dt=F32):
            return ph.enter_context(nc.sbuf_tensor(f"p6_{name}", list(shape), dt))
        kg = [psb(f"kg{i}", [128, SEQ], BF16) for i in range(2)]
        vg = [psb(f"vg{i}", [128, 64, 128], BF16) for i in range(2)]
        skg = [psb(f"skg{i}", [128, 2, LS], BF16) for i in range(2)]
        svg = [psb(f"svg{i}", [128, 2, 9, 128], BF16) for i in range(2)]
        MOFF = []
        mo_ = 0
        for kb in range(64):
            MOFF.append(mo_)
            mo_ += NP - 64 * (kb // 4)
        mT = psb("mT", [128, mo_], BF16)
        smT = psb("smT", [128, 2, 9, 16], BF16)
        qb = [psb(f"qb{i}", [128, NT], BF16) for i in range(2)]
        gb = [psb(f"gb{i}", [128, NT], BF16) for i in range(2)]
        ma = [psb(f"ma{i}", [128, NT], BF16) for i in range(2)]
        pT = [psb(f"pT{i}", [128, 1024], BF16) for i in range(2)]
        rec = psb("rec", [128, 1024])
        otmp = psb("otmp", [128, 512])
        mo = [psb(f"mo{i}", [128, NT], BF16) for i in range(2)]
        bufs = {"pT": pT, "rec": rec}
        PBLK = [(kb, 128) for kb in range(64)]
        SBLK = [(kb, 128) for kb in range(8)] + [(8, 16)]
        gcount = 0
        for v in range(V):
            for kb in range(64):
                q0 = 64 * (kb // 4)
                S.dma("sp", mT[:, MOFF[kb]:MOFF[kb] + NP - q0], MASKT[v, kb][:, q0:NP], writes=[("mT", kb)])
            for b in range(2):
                S.dma("sp", smT[:, b, :, :], SMASKT[2 * v + b].rearrange("k p q -> p k q"), writes=[("smT", b)])
            for g in range(8):
                gbi = gcount % 2
                gcount += 1
                S.dma("sp", kg[gbi][:], KBT[g], writes=[("kg", gbi)])
                S.dma("sp", vg[gbi][:], VB[:, g * 128:(g + 1) * 128].rearrange("(k p) d -> p k d", p=128), writes=[("vg", gbi)])
                for b in range(2):
                    S.dma("sp", skg[gbi][:, b, :], SKBT[2 * v + b, g], writes=[("skg", gbi, b)])
                    S.dma("sp", svg[gbi][:, b, :, :], SVB[2 * v + b, 0:9 * 128, g * 128:(g + 1) * 128].rearrange("(k p) d -> p k d", p=128),
                          writes=[("svg", gbi, b)])
                kvkeys = [("kg", gbi), ("vg", gbi)] + [("skg", gbi, b) for b in range(2)] + [("svg", gbi, b) for b in range(2)]
                for hh in range(4):
                    h = g * 4 + hh
                    hb = h % 2
                    S.dma("sp", qb[hb][:], QBT[v, h], writes=[("qb", hb)])
                    S.dma("sp", gb[hb][:], GBT[v, h], writes=[("gb", hb)])
                    S.dma("sp", ma[hb][:], MAT[v, h], writes=[("ma", hb)])
                    mokeys = []
                    for (sname, kfn, vfn, QA, QB, kblocks, q0_fn) in (
                        [("p", lambda a, w: kg[gbi][:, a:a + w], lambda kb, kw: vg[gbi][0:kw, kb, :], 0, NP, PBLK, lambda kb: 64 * (kb // 4))] +
                        [(f"s{b}", lambda a, w, b=b: skg[gbi][:, b, a:a + w], lambda kb, kw, b=b: svg[gbi][0:kw, b, kb, :],
                          NP + 16 * b, NP + 16 * b + 16, SBLK, lambda kb: 0) for b in range(2)]):
                        def kparts(kb, kw, kfn=kfn):
                            return [(kfn(kb * 128, kw), lambda a, bb: qb[hb][:, a:bb])]

                        def mask_fn(kb, kw, q0, pt, ptk, sname=sname):
                            if sname == "p":
                                S.op("dve", lambda e: e.tensor_tensor(out=pt[:, q0:NP], in0=pt[:, q0:NP], in1=mT[:, MOFF[kb]:MOFF[kb] + NP - q0], op=ALU.mult),
                                     reads=[ptk, ("mT", kb)], writes=[ptk])
                            else:
                                b = int(sname[1])
                                S.op("dve", lambda e: e.tensor_tensor(out=pt[0:kw, 0:16], in0=pt[0:kw, 0:16], in1=smT[0:kw, b, kb, :], op=ALU.mult),
                                     reads=[ptk, ("smT", b)], writes=[ptk])

                        def fin(gi, c0, n, rec_, QA=QA):
                            S.op("dve", lambda e: e.tensor_tensor(out=otmp[:, 0:n], in0=ps[4 + gi][:, 0:n], in1=rec_[:, c0 - QA:c0 - QA + n], op=ALU.mult),
                                 reads=[psk(4 + gi), ("rec", gi)], writes=["otmp"])
                            S.op("dve", lambda e: e.tensor_tensor(out=otmp[:, 0:n], in0=otmp[:, 0:n], in1=gb[hb][:, c0:c0 + n], op=ALU.mult),
                                 reads=["otmp", ("gb", hb)], writes=["otmp"])
                            S.op("dve", lambda e: e.tensor_tensor(out=mo[hb][:, c0:c0 + n], in0=otmp[:, 0:n], in1=ma[hb][:, c0:c0 + n], op=ALU.add),
                                 reads=["otmp", ("ma", hb)], writes=[("mo", hb, c0)])
                            mokeys.append(("mo", hb, c0))
                        attention("dsa", bufs, kparts, vfn, kblocks, q0_fn, QA, QB, DSA_SCALE, mask_fn, kvkeys + [("qb", hb)], fin)
                    S.dma("sp", MGT[v, h], mo[hb][:], reads=mokeys, writes=[("MGT", v, h)])
                    if DEBUG:
                        S.dma("sp", dbg_mg[v, h], mo[hb][:], reads=mokeys, writes=[("dbg", v, h)])
        S.barrier()

    with ExitStack() as ph:
        def psb(name, shape, dt=F32):
            return ph.enter_context(nc.sbuf_tensor(f"p7_{name}", list(shape), dt))
        mg = psb("mg", [128, KD, NT], BF16)
        wp = [psb(f"wp{i}", [128, KD, 512], BF16) for i in range(2)]
        xb = [psb(f"xb{i}", [128, NT]) for i in range(2)]
        x1 = [psb(f"x1{i}", [128, NT]) for i in range(2)]
        wv = w_out.rearrange("(k p) c -> p k c", p=128)
        pcount = 0
        for v in range(V):
            xov = xT_own[v].rearrange("(k p) t -> k p t", p=128)
            S.dma("sp", mg[:], MGT[v].rearrange("h p t -> p h t"), writes=["mg"])
            for pn in range(8):
                pw = pcount % 2
                pcount += 1
                S.dma("pool", wp[pw][:], wv[:, :, pn * 512:(pn + 1) * 512], writes=[("wp", pw)])
                for cc in range(4):
                    oc = pn * 4 + cc
                    S.dma("sp", xb[oc % 2][:], xov[oc], writes=[("xb", oc % 2)])
                    for gi, (c0, n) in enumerate(TGRP):
                        bank = (oc % 2) * 3 + gi
                        mm(bank, (0, n), [(wp[pw][:, j, cc * 128:(cc + 1) * 128], mg[:, j, c0:c0 + n]) for j in range(KD)],
                           reads=[("wp", pw), "mg"])
                    for bi_, (c0, c1, b) in enumerate(bcols(v)):
                        if c0 == 0:
                            for g2_ in range(2):
                                S.op("dve", lambda e, oc=oc, g2_=g2_: e.scalar_tensor_tensor(
                                    out=x1[oc % 2][:, g2_ * 512:(g2_ + 1) * 512], in0=ps[(oc % 2) * 3 + g2_][:, :], scalar=adaT[:, GT1 + oc, 0:1],
                                    in1=xb[oc % 2][:, g2_ * 512:(g2_ + 1) * 512], op0=ALU.mult, op1=ALU.add),
                                    reads=[psk((oc % 2) * 3 + g2_), ("xb", oc % 2)], writes=[("x1", oc % 2, g2_)])
                        else:
                            S.op("dve", lambda e, oc=oc, c0=c0, c1=c1, b=b: e.scalar_tensor_tensor(
                                out=x1[oc % 2][:, c0:c1], in0=ps[(oc % 2) * 3 + 2][:, c0 - 1024:c1 - 1024], scalar=adaT[:, GT1 + oc, b:b + 1],
                                in1=xb[oc % 2][:, c0:c1], op0=ALU.mult, op1=ALU.add),
                                reads=[psk((oc % 2) * 3 + 2), ("xb", oc % 2)], writes=[("x1", oc % 2, 1 + bi_)])
                    S.dma("sp", X1T[v, oc], x1[oc % 2][:], reads=[("x1", oc % 2, i_) for i_ in range(4)], writes=[("X1T", v, oc)])
        S.barrier()

    H2T = dscr("H2T", [V, KD, 128, NT])
    for v in range(V):
        with ExitStack() as ph7:
            h2T = ph7.enter_context(nc.sbuf_tensor(f"h2T{v}", [128, KD, NT], BF16))
            norm_phase(X1T[v], A2, SH2, h2T, tag=f"n2{v}", v=v)
            S.dma("sp", H2T[v].rearrange("k p t -> p k t"), h2T[:], writes=["H2T"])
            S.barrier()
    with ExitStack() as ph:
        def psb(name, shape, dt=F32):
            return ph.enter_context(nc.sbuf_tensor(f"p8_{name}", list(shape), dt))
        HN = NT // 2
        FB = 8
        h2h = psb("h2h", [128, KD, HN], BF16)
        acc = psb("acc", [128, KD, HN])
        uT = psb("uT", [128, FB, HN], BF16)
        wu = [psb(f"wu{i}", [128, KD, 256], BF16) for i in range(2)]
        wd = [psb(f"wd{i}", [128, FB, 512], BF16) for i in range(2)]
        ur = [psb(f"ur{i}", [128, HN]) for i in range(2)]
        xb = [psb(f"xb{i}", [128, HN]) for i in range(2)]
        xo = [psb(f"xo{i}", [128, HN]) for i in range(2)]
        wuv_ = w_up.rearrange("(k p) c -> p k c", p=128)
        wdv_ = w_down.rearrange("(fb fc p) c -> fb p fc c", fc=FB, p=128)
        HG = [(0, 512), (512, HN - 512)]
        cnt = {"wu": 0, "wd": 0, "bank": 0}
        for vh_ in range(2 * V):
            v, half = vh_ // 2, vh_ % 2
            hc = half * HN
            S.dma("sp", h2h[:], H2T[v].rearrange("k p t -> p k t")[:, :, hc:hc + HN], writes=["h2h"])
            for fb in range(128 // FB):
                for f2 in range(FB // 2):
                    i = cnt["wu"] % 2
                    cnt["wu"] += 1
                    fcol = (fb * FB + f2 * 2) * 128
                    S.dma("pool", wu[i][:], wuv_[:, :, fcol:fcol + 256], writes=[("wu", i)])
                    for fl in range(2):
                        fci = f2 * 2 + fl
                        bp = (cnt["bank"] % 2) * 2
                        cnt["bank"] += 1
                        for gi, (c0, n) in enumerate(HG):
                            mm(bp + gi, (0, n), [(wu[i][:, k, fl * 128:(fl + 1) * 128], h2h[:, k, c0:c0 + n]) for k in range(KD)],
                               reads=[("wu", i), "h2h"])
                        ui = fci % 2
                        for gi, (c0, n) in enumerate(HG):
                            S.op("act", lambda e, bp=bp, gi=gi, c0=c0, n=n, ui=ui: e.activation(
                                out=ur[ui][:, c0:c0 + n], in_=ps[bp + gi][:, 0:n], func=AF.Relu), reads=[psk(bp + gi)], writes=[("ur", ui, gi)])
                        S.op("dve", lambda e, ui=ui, fci=fci: e.tensor_tensor(out=uT[:, fci, :], in0=ur[ui][:], in1=ur[ui][:], op=ALU.mult),
                             reads=[("ur", ui, 0), ("ur", ui, 1)], writes=[("uT", fci)])
                utk = [("uT", f_) for f_ in range(FB)]
                for og in range(8):
                    i = cnt["wd"] % 2
                    cnt["wd"] += 1
                    S.dma("pool", wd[i][:], wdv_[fb][:, :, og * 512:(og + 1) * 512], writes=[("wd", i)])
                    for ocl in range(4):
                        oc = og * 4 + ocl
                        bp = 4 + (oc % 2) * 2
                        for gi, (c0, n) in enumerate(HG):
                            mm(bp + gi, (0, n), [(wd[i][:, fc, ocl * 128:(ocl + 1) * 128], uT[:, fc, c0:c0 + n]) for fc in range(FB)],
                               reads=[("wd", i)] + utk)
                        for gi, (c0, n) in enumerate(HG):
                            if fb == 0:
                                S.op("dve", lambda e, bp=bp, gi=gi, c0=c0, n=n, oc=oc: e.tensor_copy(out=acc[:, oc, c0:c0 + n], in_=ps[bp + gi][:, 0:n]),
                                     reads=[psk(bp + gi)], writes=[("acc", oc, gi)])
                            else:
                                S.op("dve", lambda e, bp=bp, gi=gi, c0=c0, n=n, oc=oc: e.tensor_tensor(
                                    out=acc[:, oc, c0:c0 + n], in0=ps[bp + gi][:, 0:n], in1=acc[:, oc, c0:c0 + n], op=ALU.add),
                                    reads=[psk(bp + gi), ("acc", oc, gi)], writes=[("acc", oc, gi)])
            for oc in range(KD):
                S.dma("sp", xb[oc % 2][:], X1T[v, oc][:, hc:hc + HN], writes=[("xb", oc % 2)])
                wk_ = []
                for bi_, (c0, c1, b) in enumerate(bcols(v)):
                    a, bb = max(c0, hc), min(c1, hc + HN)
                    if a >= bb:
                        continue
                    S.op("dve", lambda e, oc=oc, a=a, bb=bb, b=b: e.scalar_tensor_tensor(
                        out=xo[oc % 2][:, a - hc:bb - hc], in0=acc[:, oc, a - hc:bb - hc], scalar=adaT[:, GT2 + oc, b:b + 1],
                        in1=xb[oc % 2][:, a - hc:bb - hc], op0=ALU.mult, op1=ALU.add),
                        reads=[("acc", oc, 0), ("acc", oc, 1), ("xb", oc % 2)], writes=[("xo", oc % 2, bi_)])
                    wk_.append(("xo", oc % 2, bi_))
                S.dma("sp", X2T[v, oc][:, hc:hc + HN], xo[oc % 2][:], reads=wk_, writes=[("X2T", v, oc, half)])
        S.barrier()

    for v in range(V):
        norm_phase(X2T[v], None, 0, None, final=True, tag=f"n3{v}", v=v)
    es.close()
    return nc


def _rope_table(pos, theta, rot):
    half = rot // 2
    inv = (np.float32(theta) ** (-np.arange(half, dtype=np.float32) / np.float32(half))).astype(np.float32)
    ang = pos.astype(np.float32)[:, None] * inv[None, :]
    return np.concatenate([np.cos(ang).astype(np.float32), np.sin(ang).astype(np.float32)], axis=1)


def _fm(v, nchunk):
    return np.ascontiguousarray(np.asarray(v, np.float32).reshape(nchunk, 128).T)


_NC_CACHE = {}


def kernel(x_prompt, x_sample, c_prompt, c_sample, cache_mla_ckv, cache_mla_krope, cache_dsa_k, cache_dsa_v,
           cache_idx_k, w_ada, b_ada, g_norm1, w_in, g_q_lora, w_uq, g_kv_lora, w_uk, w_uv, w_out, g_norm2,
           w_up, w_down, g_final):
    f32 = np.float32
    x_prompt = np.asarray(x_prompt, f32)
    x_sample = np.asarray(x_sample, f32)
    xT_all = np.ascontiguousarray(x_prompt[0].T)
    shared = {
        "xT_all": xT_all,
        "w_ada": np.asarray(w_ada, f32)[0], "b_adaT": _fm(np.asarray(b_ada)[0], 192),
        "g1T": _fm(np.asarray(g_norm1)[0], KD), "g2T": _fm(np.asarray(g_norm2)[0], KD), "gfT": _fm(g_final, KD),
        "w_in": np.asarray(w_in, f32)[0],
        "gq_rep": np.ascontiguousarray(np.broadcast_to(np.asarray(g_q_lora, f32)[0][None, :], (128, 1024))),
        "gkv_rep": np.ascontiguousarray(np.broadcast_to(np.asarray(g_kv_lora, f32)[0][None, :], (128, 512))),
        "w_uq": np.asarray(w_uq, f32)[0],
        "w_uk": np.asarray(w_uk, f32)[0].reshape(512, 4096), "w_uv": np.asarray(w_uv, f32)[0].reshape(512, 4096),
        "w_out": np.asarray(w_out, f32)[0], "w_up": np.asarray(w_up, f32)[0], "w_down": np.asarray(w_down, f32)[0],
        "cs_mla_all": _rope_table(np.arange(SEQ), 10000.0, 64), "cs_dsa_all": _rope_table(np.arange(SEQ), 500000.0, 32),
        "ident": np.eye(128, dtype=f32).astype(ml_dtypes.bfloat16), "ones": np.ones((128, 128), f32).astype(ml_dtypes.bfloat16),
    }
    in_maps = []
    own_idx = []
    csf = np.asarray(c_sample, f32)
    for pc in range(NPHYS):
        xo_l, csm_l, csd_l, mb_l, adm_l = [], [], [], [], []
        bs = []
        for v in range(V):
            i = pc * V + v
            chunks = [8 * m + i for m in range(16)]
            tok = np.concatenate([np.arange(64 * c, 64 * c + 64) for c in chunks])
            own_idx.append(tok)
            xo = np.concatenate([x_prompt[0][tok], x_sample[2 * i], x_sample[2 * i + 1]], axis=0)
            xo_l.append(np.ascontiguousarray(xo.T))
            pos = np.concatenate([tok, PAST + np.arange(16), PAST + np.arange(16)])
            csm_l.append(_rope_table(pos, 10000.0, 64))
            csd_l.append(_rope_table(pos, 500000.0, 32))
            maskB = np.zeros((128, 4), f32)
            for r in range(4):
                for a_ in range(2):
                    maskB[a_ * 64:(a_ + 1) * 64, r] = 1.0 if i >= 2 * r + a_ else 0.0
            admA = np.full((128, 1024), NEG, f32)
            admA[0:64, 0:64 * (i + 1)] = 0.0
            admA[64:128, 0:64 * (8 + i + 1)] = 0.0
            mb_l.append(maskB)
            adm_l.append(admA)
            bs += [2 * i, 2 * i + 1]
        cc = np.stack([np.asarray(c_prompt, f32)[0]] + [csf[b_] for b_ in bs], axis=1)
        cTm = np.ascontiguousarray(cc.reshape(KD, 128, NB).transpose(1, 0, 2).reshape(128, KD * NB))
        m = dict(shared)
        m.update({
            "xT_own": np.stack(xo_l), "cT": cTm,
            "cs_mla_own": np.stack(csm_l), "cs_dsa_own": np.stack(csd_l),
            "c_ckvT": np.ascontiguousarray(np.asarray(cache_mla_ckv, f32)[0][bs].transpose(0, 2, 1)),
            "c_krT": np.ascontiguousarray(np.asarray(cache_mla_krope, f32)[0][bs].transpose(0, 2, 1)),
            "c_kT": np.ascontiguousarray(np.asarray(cache_dsa_k, f32)[0][bs].transpose(0, 2, 3, 1)),
            "c_v": np.ascontiguousarray(np.asarray(cache_dsa_v, f32)[0][bs].reshape(2 * V, PAST, 1024)),
            "c_kiT": np.ascontiguousarray(np.asarray(cache_idx_k, f32)[0][bs].transpose(0, 2, 1)),
            "maskB": np.ascontiguousarray(np.concatenate(mb_l, axis=1)), "admA": np.ascontiguousarray(np.concatenate(adm_l, axis=1)),
        })
        in_maps.append(m)
    if "nc" not in _NC_CACHE:
        _NC_CACHE["nc"] = build_program()
    nc = _NC_CACHE["nc"]
    res = run_bass_kernel_spmd(nc, in_maps, core_ids=list(range(NPHYS)))
    R = res.results
    y_prompt = np.zeros((1, SEQ, D), f32)
    y_sample = np.zeros((16, 16, D), f32)
    outs_p = {k: np.zeros((1, 1, SEQ, w), f32) for k, w in (("o_ckv", 512), ("o_kr", 64), ("o_k", 1024), ("o_v", 1024), ("o_ki", 128))}
    outs_s = {k: np.zeros((1, 16, 16, w), f32) for k, w in (("o_ckv", 512), ("o_kr", 64), ("o_k", 1024), ("o_v", 1024), ("o_ki", 128))}
    for i in range(NCORES):
        r, v = R[i // V], i % V
        y = np.asarray(r["yT"])[v].T
        y_prompt[0][own_idx[i]] = y[0:NP]
        y_sample[2 * i] = y[NP:NP + 16]
        y_sample[2 * i + 1] = y[NP + 16:NT]
        for k in outs_p:
            a = np.asarray(r[k])[v]
            outs_p[k][0, 0][own_idx[i]] = a[0:NP]
            outs_s[k][0, 2 * i] = a[NP:NP + 16]
            outs_s[k][0, 2 * i + 1] = a[NP + 16:NT]
    if DEBUG:
        kernel.debug = [np.asarray(R[i // V]["dbg_mg"])[i % V] for i in range(NCORES)]
        kernel.own_idx = own_idx
    return (y_prompt, y_sample,
            outs_p["o_ckv"], outs_p["o_kr"], outs_p["o_k"].reshape(1, 1, SEQ, 8, 128), outs_p["o_v"].reshape(1, 1, SEQ, 8, 128), outs_p["o_ki"],
            outs_s["o_ckv"], outs_s["o_kr"], outs_s["o_k"].reshape(1, 16, 16, 8, 128), outs_s["o_v"].reshape(1, 16, 16, 8, 128), outs_s["o_ki"])
```

```python
import numpy as np
import ml_dtypes
from contextlib import ExitStack
import concourse.bass as bass
import concourse.mybir as mybir
from concourse.bass_utils import run_bass_kernel_spmd

F32 = mybir.dt.float32
BF16 = mybir.dt.bfloat16
AF = mybir.ActivationFunctionType
ALU = mybir.AluOpType
AX = mybir.AxisListType

NCORES = 8
NPHYS = 8
V = NCORES // NPHYS
NB = 1 + 2 * V
D = 4096
KD = 32
SEQ = 8192
NP = 1024
NS = 32
NT = NP + NS
PAST = 1024
LS = PAST + 16
EPS = 1e-6
NEG = -1e30
D_IN = 20192
O_QL, O_CKV, O_KR, O_QB, O_KB, O_VB, O_QI, O_KI, O_WI, O_GA, O_GB = (
    0, 1024, 1536, 1600, 5696, 6720, 7744, 11840, 11968, 12000, 16096)
MLA_SCALE = 192.0 ** -0.5
DSA_SCALE = 128.0 ** -0.5
IDX_W_SCALE = 32.0 ** -0.5
TOPK = 256
TT = [(i * 128, 128) for i in range(8)] + [(1024, 32)]
TGRP = [(0, 512), (512, 512), (1024, 32)]
DEBUG = False


class Sched:
    ENG = ("pe", "act", "dve", "pool", "sp")

    def __init__(self, nc, es, nphase=22, ndma=8):
        self.nc = nc
        self.eng = {"pe": nc.tensor, "act": nc.scalar, "dve": nc.vector, "pool": nc.gpsimd, "sp": nc.sync}
        self.psems = [{e: es.enter_context(nc.semaphore(f"p{p}_{e}")) for e in ("pe", "act", "dve")} for p in range(nphase)]
        self.phase = 0
        self.cnt = {e: 0 for e in self.ENG}
        self.seen = {e: {} for e in self.ENG}
        self.bufs = {}
        self.dsem = {}
        for q in ("sp", "pool"):
            self.dsem[q] = [[es.enter_context(nc.semaphore(f"d_{q}{i}")), 0] for i in range(ndma)]
        self.drr = {q: 0 for q in self.dsem}
        self.nins = 0

    def _deps(self, reads, writes):
        deps = {}

        def add(ev):
            k, s, v = ev
            if k not in deps or deps[k][1] < v:
                deps[k] = (s, v)
        for b in reads:
            st = self.bufs.get(b)
            if st and st[0]:
                add(st[0])
        for b in writes:
            st = self.bufs.get(b)
            if st:
                if st[0]:
                    add(st[0])
                for ev in st[1].values():
                    add(ev)
        return deps

    def _wait(self, e, deps):
        for k, (s, v) in deps.items():
            if e == "pe" and k == ("p", self.phase, "pe"):
                continue
            if self.seen[e].get(k, 0) < v:
                self.eng[e].wait_ge(s, v)
                self.seen[e][k] = v

    def _update(self, ev, reads, writes):
        for b in reads:
            st = self.bufs.setdefault(b, [None, {}])
            st[1][ev[0]] = ev
        for b in writes:
            self.bufs[b] = [ev, {}]

    def op(self, e, fn, reads=(), writes=()):
        self._wait(e, self._deps(reads, writes))
        ins = fn(self.eng[e])
        self.cnt[e] += 1
        sem = self.psems[self.phase][e]
        ins.then_inc(sem, 1)
        ev = (("p", self.phase, e), sem, self.cnt[e])
        self._update(ev, reads, writes)
        self.nins += 1
        return ev

    def dma(self, q, out, in_, reads=(), writes=()):
        self._wait(q, self._deps(reads, writes))
        i = self.drr[q]
        self.drr[q] = (i + 1) % len(self.dsem[q])
        ent = self.dsem[q][i]
        k = ("d", q, i)
        if ent[1] > 0 and self.seen[q].get(k, 0) < ent[1]:
            self.eng[q].wait_ge(ent[0], ent[1])
            self.seen[q][k] = ent[1]
        self.eng[q].dma_start(out=out, in_=in_).then_inc(ent[0], 16)
        ent[1] += 16
        ev = (k, ent[0], ent[1])
        self._update(ev, reads, writes)
        self.nins += 1
        return ev

    def barrier(self):
        for e in self.ENG:
            for e2 in ("pe", "act", "dve"):
                if self.cnt[e2] > 0:
                    self.eng[e].wait_ge(self.psems[self.phase][e2], self.cnt[e2])
            for q, lst in self.dsem.items():
                for i, ent in enumerate(lst):
                    if ent[1] > 0 and self.seen[e].get(("d", q, i), 0) < ent[1]:
                        self.eng[e].wait_ge(ent[0], ent[1])
                        self.seen[e][("d", q, i)] = ent[1]
        self.phase += 1
        assert self.phase < len(self.psems)
        self.cnt = {e: 0 for e in self.ENG}
        for e in self.ENG:
            self.seen[e] = {k: v for k, v in self.seen[e].items() if k[0] == "d"}
        self.bufs = {}


def build_program():
    nc = bass.Bass("TRN2", target_bir_lowering=False)
    es = ExitStack()

    def din(name, shape, dt=F32):
        return nc.dram_tensor(name, list(shape), dt, kind="ExternalInput").ap()

    def dout(name, shape, dt=F32):
        return nc.dram_tensor(name, list(shape), dt, kind="ExternalOutput").ap()

    def dscr(name, shape, dt=BF16):
        return nc.dram_tensor(name, list(shape), dt).ap()

    xT_own = din("xT_own", [V, D, NT])
    xT_all = din("xT_all", [D, SEQ])
    cT = din("cT", [128, KD * NB])
    w_ada = din("w_ada", [D, 6 * D])
    b_adaT = din("b_adaT", [128, 192])
    g1T = din("g1T", [128, KD])
    g2T = din("g2T", [128, KD])
    gfT = din("gfT", [128, KD])
    w_in = din("w_in", [D, D_IN])
    gq_rep = din("gq_rep", [128, 1024])
    gkv_rep = din("gkv_rep", [128, 512])
    w_uq = din("w_uq", [1024, 6144])
    w_uk = din("w_uk", [512, 4096])
    w_uv = din("w_uv", [512, 4096])
    w_out = din("w_out", [D, D])
    w_up = din("w_up", [D, 4 * D])
    w_down = din("w_down", [4 * D, D])
    cs_mla_own = din("cs_mla_own", [V, NT, 64])
    cs_dsa_own = din("cs_dsa_own", [V, NT, 32])
    cs_mla_all = din("cs_mla_all", [SEQ, 64])
    cs_dsa_all = din("cs_dsa_all", [SEQ, 32])
    c_ckvT = din("c_ckvT", [2 * V, 512, PAST])
    c_krT = din("c_krT", [2 * V, 64, PAST])
    c_kT = din("c_kT", [2 * V, 8, 128, PAST])
    c_v = din("c_v", [2 * V, PAST, 1024])
    c_kiT = din("c_kiT", [2 * V, 128, PAST])
    maskB_in = din("maskB", [128, V * 4])
    admA_in = din("admA", [128, V * 1024])
    ident_in = din("ident", [128, 128], BF16)
    ones_in = din("ones", [128, 128], BF16)
    yT = dout("yT", [V, D, NT])
    o_ckv = dout("o_ckv", [V, NT, 512])
    o_kr = dout("o_kr", [V, NT, 64])
    o_k = dout("o_k", [V, NT, 1024])
    o_v = dout("o_v", [V, NT, 1024])
    o_ki = dout("o_ki", [V, NT, 128])
    if DEBUG:
        dbg_mg = dout("dbg_mg", [V, 32, 128, NT], BF16)
    QBT = dscr("QBT", [V, 32, 128, NT])
    QIT = dscr("QIT", [V, 32, 128, NT])
    QNT = dscr("QNT", [V, 32, 128, NT])
    QRT = dscr("QRT", [V, 16, 128, NT])
    GAT = dscr("GAT", [V, 32, 128, NT])
    GBT = dscr("GBT", [V, 32, 128, NT])
    MAT = dscr("MAT", [V, 32, 128, NT])
    MGT = dscr("MGT", [V, 32, 128, NT])
    WKB = dscr("WKB", [128, KD, 2752])
    CKVT = dscr("CKVT", [4, 128, SEQ])
    KRT = dscr("KRT", [128, SEQ])
    KBT = dscr("KBT", [8, 128, SEQ])
    KIT = dscr("KIT", [128, SEQ])
    VB = dscr("VB", [SEQ, 1024])
    SCKVT = dscr("SCKVT", [2 * V, 4, 128, LS])
    SKRT = dscr("SKRT", [2 * V, 128, LS])
    SKBT = dscr("SKBT", [2 * V, 8, 128, LS])
    SKIT = dscr("SKIT", [2 * V, 128, LS])
    SVB = dscr("SVB", [2 * V, LS + 112, 1024])
    MASKT = dscr("MASKT", [V, 64, 128, NP])
    SMASKT = dscr("SMASKT", [2 * V, 9, 128, 16])
    X1T = dscr("X1T", [V, KD, 128, NT], F32)
    X2T = dscr("X2T", [V, KD, 128, NT], F32)

    S = Sched(nc, es)

    def sb(name, shape, dt=F32):
        return es.enter_context(nc.sbuf_tensor("sb_" + name, list(shape), dt))

    psbig = es.enter_context(nc.psum_tensor("psbig", [128, 4096], F32))
    ps = [psbig[:, i * 512:(i + 1) * 512] for i in range(8)]

    def psk(i):
        return ("ps", i)

    ident = sb("ident", [128, 128], BF16)
    ones = sb("ones", [128, 128], BF16)
    epsT = sb("epsT", [128, 1])
    adaT = sb("adaT", [128, 192, NB])
    A1 = sb("A1", [128, KD, NB])
    A2 = sb("A2", [128, KD, NB])
    gf = sb("gf", [128, KD])
    WI = sb("WI", [128, V, 9, 32])
    maskB = sb("maskB", [128, V, 4])
    admA = sb("admA", [128, V, 1024])
    S.dma("sp", ident[:], ident_in[:, :], writes=["ident"])
    S.dma("sp", ones[:], ones_in[:, :], writes=["ones"])
    S.dma("sp", gf[:], gfT[:, :], writes=["gf"])
    S.dma("sp", maskB[:].rearrange("p v r -> p (v r)"), maskB_in[:, :], writes=["maskB"])
    S.dma("sp", admA[:].rearrange("p v r -> p (v r)"), admA_in[:, :], writes=["admA"])
    S.op("dve", lambda e: e.memset(epsT[:], EPS), writes=["epsT"])

    def mm(bank, cols, pairs, reads):
        m = pairs[0][0].shape[-1] if len(pairs[0][0].shape) == 2 else None
        n = len(pairs)

        def fn(pe):
            ins = None
            for i, (l, r) in enumerate(pairs):
                mrows = l.shape[1]
                ins = pe.matmul(ps[bank][0:mrows, cols[0]:cols[1]], lhsT=l, rhs=r, start=(i == 0), stop=(i == n - 1))
            return ins
        return S.op("pe", fn, reads=reads, writes=[psk(bank)])

    def transpose_to(bank, col0, src, nrow, reads):
        pv = ps[bank][:].bitcast(BF16)

        def fn(pe):
            return pe.transpose(out=pv[0:src.shape[1], col0:col0 + nrow], in_=src, identity=ident[0:nrow, 0:nrow])
        return S.op("pe", fn, reads=list(reads) + ["ident"], writes=[psk(bank)])

    def psbf(bank):
        return ps[bank][:].bitcast(BF16)

    with ExitStack() as ph:
        def psb(name, shape, dt=F32):
            return ph.enter_context(nc.sbuf_tensor(name, list(shape), dt))
        c_sb = psb("c_sb", [128, KD * NB])
        sT = psb("sT", [128, KD, NB], BF16)
        bada = psb("bada", [128, 192])
        g1 = psb("g1", [128, KD])
        g2 = psb("g2", [128, KD])
        tmpA = psb("tmpA", [128, KD, NB])
        wp = [psb(f"wpa{i}", [128, KD, 512], BF16) for i in range(3)]
        S.dma("sp", c_sb[:], cT[:, :], writes=["c_sb"])
        S.dma("sp", bada[:], b_adaT[:, :], writes=["bada"])
        S.dma("sp", g1[:], g1T[:, :], writes=["g1"])
        S.dma("sp", g2[:], g2T[:, :], writes=["g2"])
        S.op("act", lambda e: e.activation(out=sT[:].rearrange("p k b -> p (k b)"), in_=c_sb[:], func=AF.Silu),
             reads=["c_sb"], writes=["sT"])
        wav = w_ada.rearrange("(k p) c -> p k c", p=128)
        NPA = 48
        for pn in range(min(3, NPA)):
            S.dma("pool", wp[pn % 3][:], wav[:, :, pn * 512:(pn + 1) * 512], writes=[("wpa", pn % 3)])
        for pn in range(NPA):
            buf = wp[pn % 3]
            for cc in range(4):
                j = pn * 4 + cc
                bank = j % 8
                mm(bank, (0, NB), [(buf[:, k, cc * 128:(cc + 1) * 128], sT[:, k, :]) for k in range(KD)],
                   reads=[("wpa", pn % 3), "sT"])
                S.op("dve", lambda e, j=j, bank=bank: e.tensor_scalar(
                    out=adaT[:, j, :], in0=ps[bank][:, 0:NB], scalar1=bada[:, j:j + 1], scalar2=None, op0=ALU.add),
                    reads=[psk(bank), "bada"], writes=[("adaT", j)])
            if pn + 3 < NPA:
                S.dma("pool", wp[pn % 3][:], wav[:, :, (pn + 3) * 512:(pn + 4) * 512], writes=[("wpa", pn % 3)])
        for (A, g, gname, off) in ((A1, g1, "g1", 32), (A2, g2, "g2", 128)):
            S.op("dve", lambda e, off=off: e.tensor_scalar(out=tmpA[:], in0=adaT[:, off:off + 32, :], scalar1=1.0,
                                                          scalar2=None, op0=ALU.add),
                 reads=[("adaT", j) for j in range(off, off + 32)], writes=["tmpA"])
            S.op("dve", lambda e, A=A, g=g: e.tensor_tensor(out=A[:], in0=tmpA[:],
                                                            in1=g[:].unsqueeze(2).to_broadcast([128, KD, NB]), op=ALU.mult),
                 reads=["tmpA", gname], writes=["Amod"])
        S.barrier()
    SH1, GT1, SH2, GT2 = 0, 64, 96, 160

    def bcols(v):
        return [(0, 1024, 0), (1024, 1040, 1 + 2 * v), (1040, 1056, 2 + 2 * v)]

    def norm_phase(src, A, Boff, hT, final=False, tag="n", v=0):
        with ExitStack() as ph:
            def psb(name, shape, dt=F32):
                return ph.enter_context(nc.sbuf_tensor(f"{tag}_{name}", list(shape), dt))
            xb = [psb(f"xb{i}", [128, NT]) for i in range(3)]
            sq = [psb(f"sq{i}", [128, NT], BF16) for i in range(2)]
            rstd = psb("rstd", [128, NT])
            tmp = [psb(f"tmp{i}", [128, NT]) for i in range(2)]
            for k in range(KD):
                S.dma("sp", xb[k % 3][:], src[k], writes=[("xb", k % 3)])
                S.op("act", lambda e, k=k: e.activation(out=sq[k % 2][:], in_=xb[k % 3][:], func=AF.Square),
                     reads=[("xb", k % 3)], writes=[("sq", k % 2)])

                def fn(pe, k=k):
                    ins = None
                    for gi, (c0, n) in enumerate(TGRP):
                        ins = pe.matmul(ps[gi][:, 0:n], lhsT=ones[:, :], rhs=sq[k % 2][:, c0:c0 + n],
                                        start=(k == 0), stop=(k == KD - 1))
                    return ins
                S.op("pe", fn, reads=[("sq", k % 2), "ones"], writes=[psk(0), psk(1), psk(2)])
            for gi, (c0, n) in enumerate(TGRP):
                S.op("act", lambda e, gi=gi, c0=c0, n=n: e.activation(
                    out=rstd[:, c0:c0 + n], in_=ps[gi][:, 0:n], func=AF.Sqrt, bias=epsT[:, 0:1], scale=1.0 / D),
                    reads=[psk(gi), "epsT"], writes=[("rstd", gi)])
                S.op("dve", lambda e, c0=c0, n=n: e.reciprocal(out=rstd[:, c0:c0 + n], in_=rstd[:, c0:c0 + n]),
                     reads=[("rstd", gi)], writes=[("rstd", gi)])
            rk = [("rstd", gi) for gi in range(3)]
            for k in range(KD):
                S.dma("sp", xb[k % 3][:], src[k], writes=[("xb", k % 3)])
                if final:
                    S.op("dve", lambda e, k=k: e.scalar_tensor_tensor(
                        out=tmp[k % 2][:], in0=xb[k % 3][:], scalar=gf[:, k:k + 1], in1=rstd[:],
                        op0=ALU.mult, op1=ALU.mult), reads=[("xb", k % 3), "gf"] + rk, writes=[("tmp", k % 2)])
                    S.dma("sp", yT[v, k * 128:(k + 1) * 128, :], tmp[k % 2][:], reads=[("tmp", k % 2)], writes=[("yT", k)])
                else:
                    S.op("dve", lambda e, k=k: e.tensor_tensor(out=tmp[k % 2][:], in0=xb[k % 3][:], in1=rstd[:], op=ALU.mult),
                         reads=[("xb", k % 3)] + rk, writes=[("tmp", k % 2)])
                    for (c0, c1, b) in bcols(v):
                        S.op("act", lambda e, k=k, c0=c0, c1=c1, b=b: e.activation(
                            out=hT[:, k, c0:c1], in_=tmp[k % 2][:, c0:c1], func=AF.Identity,
                            bias=adaT[:, Boff + k, b:b + 1], scale=A[:, k, b:b + 1]),
                            reads=[("tmp", k % 2)], writes=[("hT", k)])
            S.barrier()

    def rope_inplace(e_tag, z3, n, H, half, cs, tmps, zkey, cskey="cs"):
        cb = cs[0:n, 0:half].unsqueeze(1).to_broadcast([n, H, half])
        sn = cs[0:n, half:2 * half].unsqueeze(1).to_broadcast([n, H, half])
        x1 = z3[:, :, 0:half]
        x2 = z3[:, :, half:2 * half]
        t = [tm[0:n, 0:H * half].rearrange("p (h d) -> p h d", d=half) for tm in tmps]
        tk = [("ropetmp", e_tag, i) for i in range(4)]
        S.op("dve", lambda e: e.tensor_tensor(out=t[0], in0=x1, in1=cb, op=ALU.mult), reads=[zkey, cskey], writes=[tk[0]])
        S.op("dve", lambda e: e.tensor_tensor(out=t[1], in0=x2, in1=sn, op=ALU.mult), reads=[zkey, cskey], writes=[tk[1]])
        S.op("dve", lambda e: e.tensor_tensor(out=t[2], in0=x2, in1=cb, op=ALU.mult), reads=[zkey, cskey], writes=[tk[2]])
        S.op("dve", lambda e: e.tensor_tensor(out=t[3], in0=x1, in1=sn, op=ALU.mult), reads=[zkey, cskey], writes=[tk[3]])
        S.op("dve", lambda e: e.tensor_tensor(out=x1, in0=t[0], in1=t[1], op=ALU.subtract), reads=[tk[0], tk[1]], writes=[zkey])
        S.op("dve", lambda e: e.tensor_tensor(out=x2, in0=t[2], in1=t[3], op=ALU.add), reads=[tk[2], tk[3]], writes=[zkey])

    def rms_rows(z, n, width, grep, out, zkey, gkey, ss, outkey, tag):
        S.op("act", lambda e: e.activation(out=ss[1][0:n, 0:width], in_=z, func=AF.Square, accum_out=ss[0][0:n, 0:1]),
             reads=[zkey], writes=[("ss", tag)])
        S.op("act", lambda e: e.activation(out=ss[0][0:n, 0:1], in_=ss[0][0:n, 0:1], func=AF.Sqrt,
                                           bias=epsT[0:n, 0:1], scale=1.0 / width),
             reads=[("ss", tag), "epsT"], writes=[("ss", tag)])
        S.op("dve", lambda e: e.reciprocal(out=ss[0][0:n, 0:1], in_=ss[0][0:n, 0:1]), reads=[("ss", tag)], writes=[("ss", tag)])
        S.op("dve", lambda e: e.scalar_tensor_tensor(out=out, in0=z, scalar=ss[0][0:n, 0:1], in1=grep[0:n, 0:width],
                                                     op0=ALU.mult, op1=ALU.mult),
             reads=[zkey, ("ss", tag), gkey], writes=[outkey])

    for v in range(V):
        ph12 = ExitStack()
        hT = ph12.enter_context(nc.sbuf_tensor(f"hT{v}", [128, KD, NT], BF16))
        qlnT = ph12.enter_context(nc.sbuf_tensor(f"qlnT{v}", [128, 8, NT], BF16))
        norm_phase(xT_own[v].rearrange("(k p) t -> k p t", p=128), A1, SH1, hT, tag=f"n1{v}", v=v)
        hkeys = [("hT", k) for k in range(KD)]

        with ExitStack() as ph:
            def psb(name, shape, dt=F32):
                return ph.enter_context(nc.sbuf_tensor(f"p2{v}_{name}", list(shape), dt))
            wp = [psb(f"wp{i}", [128, KD, 512], BF16) for i in range(2)]
            zt = [psb(f"z{i}", [128, 512]) for i in range(3)]
            zb = [psb(f"zb{i}", [128, 512], BF16) for i in range(2)]
            rt = [psb(f"rt{i}", [128, 128]) for i in range(4)]
            ssq = [psb("ss0", [128, 1]), psb("ssj", [128, 1024], BF16)]
            ql = psb("ql", [128, 1024])
            stg = [psb(f"stg{i}", [128, 4, NT], BF16) for i in range(2)]
            csm = psb("csm", [128, 9, 64])
            csd = psb("csd", [128, 9, 32])
            gq = psb("gq", [128, 1024])
            gkv = psb("gkv", [128, 512])
            sstg = psb("sstg", [128, 14, 32], BF16)
            S.dma("sp", gq[:], gq_rep[:, :], writes=["gq"])
            S.dma("sp", gkv[:], gkv_rep[:, :], writes=["gkv"])
            for ti, (t0, n) in enumerate(TT):
                S.dma("sp", csm[0:n, ti, :], cs_mla_own[v, t0:t0 + n, :], writes=["cs"])
                S.dma("sp", csd[0:n, ti, :], cs_dsa_own[v, t0:t0 + n, :], writes=["cs"])
            wv = w_in.rearrange("(k p) c -> p k c", p=128)
            cnt = {"pn": 0, "z": 0, "bank": 0, "tb": 0, "stg": 0, "zb": 0}

            def load_panel(c0, w):
                i = cnt["pn"] % 2
                cnt["pn"] += 1
                S.dma("pool", wp[i][:, :, 0:w], wv[:, :, c0:c0 + w], writes=[("wp", i)])
                return i

            def gemm_tok(pi, w, ti, kk=KD, lhs=None, lkeys=None):
                t0, n = TT[ti]
                bank = cnt["bank"] % 4
                cnt["bank"] += 1
                src = hT if lhs is None else lhs
                mm(bank, (0, w), [(src[:, k, t0:t0 + n], wp[pi][:, k, 0:w]) for k in range(kk)],
                   reads=[("wp", pi)] + (hkeys if lkeys is None else lkeys))
                flush()
                return bank

            pending = []

            def flush():
                while pending:
                    pending.pop(0)()

            def defer(fn):
                pending.append(fn)

            def evac_z(bank, n, w):
                zi = cnt["z"] % 3
                cnt["z"] += 1
                S.op("act", lambda e: e.activation(out=zt[zi][0:n, 0:w], in_=ps[bank][0:n, 0:w], func=AF.Copy),
                     reads=[psk(bank)], writes=[("z", zi)])
                return zi

            def to_bf(zi, n, w):
                bi = cnt["zb"] % 2
                cnt["zb"] += 1
                S.op("act", lambda e: e.activation(out=zb[bi][0:n, 0:w], in_=zt[zi][0:n, 0:w], func=AF.Copy),
                     reads=[("z", zi)], writes=[("zb", bi)])
                return bi

            def tr_blocks(*args):
                defer(lambda: tr_blocks_now(*args))

            def tr_blocks_now(src_tile, skey, n, blocks, dst_fn, dkey):
                for bi_, (c0, wdt) in enumerate(blocks):
                    tb = 4 + cnt["tb"] % 4
                    cnt["tb"] += 1
                    transpose_to(tb, 0, src_tile[0:n, c0:c0 + wdt], n, reads=[skey])
                    S.op("dve", lambda e, tb=tb, bi_=bi_, wdt=wdt: e.tensor_copy(out=dst_fn(bi_), in_=psbf(tb)[0:wdt, 0:n]),
                         reads=[psk(tb)], writes=[dkey])

            qlb = psb("qlb", [128, 1024], BF16)
            pis = [load_panel(O_QL + pnl * 512, 512) for pnl in range(2)]
            for ti, (t0, n) in enumerate(TT):
                for pnl in range(2):
                    bank = gemm_tok(pis[pnl], 512, ti)
                    S.op("act", lambda e, bank=bank, n=n, pnl=pnl: e.activation(
                        out=ql[0:n, pnl * 512:(pnl + 1) * 512], in_=ps[bank][0:n, 0:512], func=AF.Copy),
                        reads=[psk(bank)], writes=[("ql", pnl)])
                S.op("act", lambda e, n=n: e.activation(out=ssq[1][0:n, :], in_=ql[0:n, :], func=AF.Square,
                                                        accum_out=ssq[0][0:n, 0:1]),
                     reads=[("ql", 0), ("ql", 1)], writes=["ssq"])
                S.op("act", lambda e, n=n: e.activation(out=ssq[0][0:n, 0:1], in_=ssq[0][0:n, 0:1], func=AF.Sqrt,
                                                        bias=epsT[0:n, 0:1], scale=1.0 / 1024), reads=["ssq", "epsT"], writes=["ssq"])
                S.op("dve", lambda e, n=n: e.reciprocal(out=ssq[0][0:n, 0:1], in_=ssq[0][0:n, 0:1]), reads=["ssq"], writes=["ssq"])
                S.op("dve", lambda e, n=n: e.scalar_tensor_tensor(
                    out=qlb[0:n, :], in0=ql[0:n, :], scalar=ssq[0][0:n, 0:1], in1=gq[0:n, :], op0=ALU.mult, op1=ALU.mult),
                    reads=["ssq", "gq", ("ql", 0), ("ql", 1)], writes=["qlb"])
                tr_blocks(qlb, "qlb", n, [(c * 128, 128) for c in range(8)],
                          lambda bi_, t0=t0, n=n: qlnT[:, bi_, t0:t0 + n], ("qlnT", ti))
            qkeys = [("qlnT", ti) for ti in range(9)]

            pi = load_panel(O_CKV, 512)
            for ti, (t0, n) in enumerate(TT):
                bank = gemm_tok(pi, 512, ti)
                zi = evac_z(bank, n, 512)
                zo = (zi + 1) % 3
                cnt["z"] += 1
                rms_rows(zt[zi][0:n, :], n, 512, gkv, zt[zo][0:n, :], ("z", zi), "gkv", ssq, ("z", zo), "ckv")
                S.dma("sp", o_ckv[v, t0:t0 + n, :], zt[zo][0:n, :], reads=[("z", zo)], writes=[("o_ckv", ti)])
                if ti == 8:
                    bi = to_bf(zo, n, 512)
                    tr_blocks(zb[bi], ("zb", bi), n, [(c * 128, 128) for c in range(4)],
                              lambda bi_: sstg[:, bi_, 0:32], "sstg")
            pi = load_panel(O_KR, 64)
            for ti, (t0, n) in enumerate(TT):
                bank = gemm_tok(pi, 64, ti)
                zi = evac_z(bank, n, 64)
                rope_inplace("a", zt[zi][0:n, 0:64].rearrange("p (h d) -> p h d", h=1), n, 1, 32, csm[:, ti, :], rt, ("z", zi))
                S.dma("sp", o_kr[v, t0:t0 + n, :], zt[zi][0:n, 0:64], reads=[("z", zi)], writes=[("o_kr", ti)])
                if ti == 8:
                    bi = cnt["zb"] % 2
                    cnt["zb"] += 1
                    for hh in range(2):
                        S.op("act", lambda e, hh=hh, bi=bi, zi=zi, n=n: e.activation(
                            out=zb[bi][0:n, hh * 64:(hh + 1) * 64], in_=zt[zi][0:n, 0:64], func=AF.Copy),
                            reads=[("z", zi)], writes=[("zb", bi)])
                    tr_blocks(zb[bi], ("zb", bi), n, [(0, 128)], lambda bi_: sstg[:, 4, 0:32], "sstg")
            for pnl in range(2):
                pi = load_panel(O_KB + pnl * 512, 512)
                for ti, (t0, n) in enumerate(TT):
                    bank = gemm_tok(pi, 512, ti)
                    zi = evac_z(bank, n, 512)
                    rope_inplace("a", zt[zi][0:n, :].rearrange("p (h d) -> p h d", h=4), n, 4, 16, csd[:, ti, :], rt, ("z", zi))
                    S.dma("sp", o_k[v, t0:t0 + n, pnl * 512:(pnl + 1) * 512], zt[zi][0:n, :], reads=[("z", zi)],
                          writes=[("o_k", ti, pnl)])
                    if ti == 8:
                        bi = to_bf(zi, n, 512)
                        tr_blocks(zb[bi], ("zb", bi), n, [(c * 128, 128) for c in range(4)],
                                  lambda bi_, pnl=pnl: sstg[:, 5 + pnl * 4 + bi_, 0:32], "sstg")
            for pnl in range(2):
                pi = load_panel(O_VB + pnl * 512, 512)
                for ti, (t0, n) in enumerate(TT):
                    bank = gemm_tok(pi, 512, ti)
                    zi = evac_z(bank, n, 512)
                    S.dma("sp", o_v[v, t0:t0 + n, pnl * 512:(pnl + 1) * 512], zt[zi][0:n, :], reads=[("z", zi)],
                          writes=[("o_v", ti, pnl)])
                    if ti == 8:
                        bi = to_bf(zi, n, 512)
                        for b in range(2):
                            S.dma("sp", SVB[2 * v + b, PAST:PAST + 16, pnl * 512:(pnl + 1) * 512], zb[bi][b * 16:(b + 1) * 16, :],
                                  reads=[("zb", bi)], writes=[("SVBn", b, pnl)])
            pi = load_panel(O_KI, 128)
            for ti, (t0, n) in enumerate(TT):
                bank = gemm_tok(pi, 128, ti)
                zi = evac_z(bank, n, 128)
                rope_inplace("a", zt[zi][0:n, 0:128].rearrange("p (h d) -> p h d", h=1), n, 1, 16, csd[:, ti, :], rt, ("z", zi))
                S.dma("sp", o_ki[v, t0:t0 + n, :], zt[zi][0:n, 0:128], reads=[("z", zi)], writes=[("o_ki", ti)])
                if ti == 8:
                    bi = to_bf(zi, n, 128)
                    tr_blocks(zb[bi], ("zb", bi), n, [(0, 128)], lambda bi_: sstg[:, 13, 0:32], "sstg")
            flush()
            for b in range(2):
                S.dma("sp", SCKVT[2 * v + b].rearrange("c p l -> p c l")[:, :, PAST:LS], sstg[:, 0:4, b * 16:(b + 1) * 16],
                      reads=["sstg"], writes=[("SCKVTn", b)])
                S.dma("sp", SKRT[2 * v + b][:, PAST:LS], sstg[:, 4, b * 16:(b + 1) * 16], reads=["sstg"], writes=[("SKRTn", b)])
                S.dma("sp", SKBT[2 * v + b].rearrange("c p l -> p c l")[:, :, PAST:LS], sstg[:, 5:13, b * 16:(b + 1) * 16],
                      reads=["sstg"], writes=[("SKBTn", b)])
                S.dma("sp", SKIT[2 * v + b][:, PAST:LS], sstg[:, 13, b * 16:(b + 1) * 16], reads=["sstg"], writes=[("SKITn", b)])
            pi = load_panel(O_WI, 32)
            for ti, (t0, n) in enumerate(TT):
                bank = gemm_tok(pi, 32, ti)
                S.op("act", lambda e, bank=bank, ti=ti, n=n: e.activation(out=WI[0:n, v, ti, :], in_=ps[bank][0:n, 0:32],
                                                                        func=AF.Copy, scale=IDX_W_SCALE),
                     reads=[psk(bank)], writes=[("WI", ti)])
            for (off, dst, dname) in ((O_QB, QBT[v], "QBT"), (O_QI, QIT[v], "QIT")):
                for pnl in range(8):
                    pi = load_panel(off + pnl * 512, 512)
                    si = cnt["stg"] % 2
                    cnt["stg"] += 1
                    for ti, (t0, n) in enumerate(TT):
                        bank = gemm_tok(pi, 512, ti)
                        zi = evac_z(bank, n, 512)
                        rope_inplace("a", zt[zi][0:n, :].rearrange("p (h d) -> p h d", h=4), n, 4, 16, csd[:, ti, :], rt, ("z", zi))
                        bi = to_bf(zi, n, 512)
                        tr_blocks(zb[bi], ("zb", bi), n, [(c * 128, 128) for c in range(4)],
                                  lambda bi_, si=si, t0=t0, n=n: stg[si][:, bi_, t0:t0 + n], ("stg", si))
                    defer(lambda dst=dst, pnl=pnl, si=si, dname=dname: S.dma(
                        "sp", dst[pnl * 4:(pnl + 1) * 4].rearrange("h p t -> p h t"), stg[si][:],
                        reads=[("stg", si)], writes=[(dname, pnl)]))
            flush()
            for (off, dst, dname) in ((O_GA, GAT[v], "GAT"), (O_GB, GBT[v], "GBT")):
                for pnl in range(8):
                    pi = load_panel(off + pnl * 512, 512)
                    si = cnt["stg"] % 2
                    cnt["stg"] += 1
                    for cc in range(4):
                        for gi, (c0, n) in enumerate(TGRP):
                            bank = cnt["bank"] % 4
                            cnt["bank"] += 1
                            mm(bank, (0, n), [(wp[pi][:, k, cc * 128:(cc + 1) * 128], hT[:, k, c0:c0 + n]) for k in range(KD)],
                               reads=[("wp", pi)] + hkeys)
                            S.op("act", lambda e, bank=bank, si=si, cc=cc, c0=c0, n=n: e.activation(
                                out=stg[si][:, cc, c0:c0 + n], in_=ps[bank][:, 0:n], func=AF.Sigmoid),
                                reads=[psk(bank)], writes=[("stg", si)])
                    S.dma("sp", dst[pnl * 4:(pnl + 1) * 4].rearrange("h p t -> p h t"), stg[si][:],
                          reads=[("stg", si)], writes=[(dname, pnl)])
            flush()
            wq = w_uq.rearrange("(k p) c -> p k c", p=128)
            for pr in range(16):
                i = cnt["pn"] % 2
                cnt["pn"] += 1
                S.dma("pool", wp[i][:, 0:8, 0:384], wq[:, :, pr * 384:(pr + 1) * 384], writes=[("wp", i)])
                si = cnt["stg"] % 2
                cnt["stg"] += 1
                for ti, (t0, n) in enumerate(TT):
                    bank = gemm_tok(i, 384, ti, kk=8, lhs=qlnT, lkeys=qkeys)
                    zi = evac_z(bank, n, 384)
                    rope_inplace("a", zt[zi][0:n, 0:384].rearrange("p (h d) -> p h d", h=2)[:, :, 128:192], n, 2, 32,
                                 csm[:, ti, :], rt, ("z", zi))
                    bi = cnt["zb"] % 2
                    cnt["zb"] += 1
                    for (d0, s0, wd) in ((0, 0, 128), (128, 192, 128), (256, 128, 64), (320, 320, 64)):
                        S.op("act", lambda e, d0=d0, s0=s0, wd=wd, bi=bi, zi=zi, n=n: e.activation(
                            out=zb[bi][0:n, d0:d0 + wd], in_=zt[zi][0:n, s0:s0 + wd], func=AF.Copy),
                            reads=[("z", zi)], writes=[("zb", bi)])
                    tr_blocks(zb[bi], ("zb", bi), n, [(0, 128), (128, 128), (256, 128)],
                              lambda bi_, si=si, t0=t0, n=n: stg[si][:, bi_, t0:t0 + n], ("stg", si))
                defer(lambda pr=pr, si=si: S.dma("sp", QNT[v, pr * 2:(pr + 1) * 2].rearrange("h p t -> p h t"), stg[si][:, 0:2, :],
                                                 reads=[("stg", si)], writes=[("QNT", pr)]))
                defer(lambda pr=pr, si=si: S.dma("sp", QRT[v, pr], stg[si][:, 2, :], reads=[("stg", si)], writes=[("QRT", pr)]))
            flush()
            S.barrier()
        ph12.close()

    KOFF = [(O_CKV, 512), (O_KR, 64), (O_KB, 512), (O_KB + 512, 512), (O_VB, 512), (O_VB + 512, 512), (O_KI, 128)]
    KPOS = []
    acc_ = 0
    for (o, w) in KOFF:
        KPOS.append(acc_)
        acc_ += w
    assert acc_ == 2752
    with ExitStack() as ph:
        def psb(name, shape, dt=F32):
            return ph.enter_context(nc.sbuf_tensor(f"p3_{name}", list(shape), dt))
        G = 512
        NTI = G // 128
        hg = psb("hg", [128, KD, G], BF16)
        wp = [psb(f"wp{i}", [128, KD, 512], BF16) for i in range(2)]
        sq = [psb(f"sq{i}", [128, G], BF16) for i in range(2)]
        rstd = psb("rstd", [128, G])
        tmp = [psb(f"tmp{i}", [128, G]) for i in range(2)]
        zt = [psb(f"z{i}", [128, 512]) for i in range(3)]
        zb = [psb(f"zb{i}", [128, 512], BF16) for i in range(2)]
        rt = [psb(f"rt{i}", [128, 128]) for i in range(4)]
        ssq = [psb("ss0", [128, 1]), psb("ssj", [128, 512])]
        gkv = psb("gkv", [128, 512])
        csm = psb("csm", [128, 4, 64])
        csd = psb("csd", [128, 4, 32])
        tstg = psb("tstg", [128, 14, 1024], BF16)
        vstg = psb("vstg", [128, 8, 1024], BF16)
        S.dma("sp", gkv[:], gkv_rep[:, :], writes=["gkv"])
        wv = w_in.rearrange("(k p) c -> p k c", p=128)
        for pi_, (o, w) in enumerate(KOFF):
            S.dma("pool", wp[pi_ % 2][:, :, 0:w], wv[:, :, o:o + w], writes=[("wp", pi_ % 2)])
            S.dma("sp", WKB[:, :, KPOS[pi_]:KPOS[pi_] + w], wp[pi_ % 2][:, :, 0:w], reads=[("wp", pi_ % 2)], writes=[("WKB", pi_)])
        for b in range(2 * V):
            S.dma("pool", tstg[:, 0:4, 0:PAST], c_ckvT[b].rearrange("(c p) l -> p c l", p=128), writes=["tstg"])
            S.dma("sp", SCKVT[b].rearrange("c p l -> p c l")[:, :, 0:PAST], tstg[:, 0:4, 0:PAST], reads=["tstg"], writes=[("SCKVTc", b)])
            for hh in range(2):
                S.dma("pool", tstg[hh * 64:(hh + 1) * 64, 4, 0:PAST], c_krT[b], writes=["tstg"])
            S.dma("sp", SKRT[b][:, 0:PAST], tstg[:, 4, 0:PAST], reads=["tstg"], writes=[("SKRTc", b)])
            S.dma("pool", tstg[:, 5:13, 0:PAST], c_kT[b].rearrange("g p l -> p g l"), writes=["tstg"])
            S.dma("sp", SKBT[b].rearrange("c p l -> p c l")[:, :, 0:PAST], tstg[:, 5:13, 0:PAST], reads=["tstg"], writes=[("SKBTc", b)])
            S.dma("pool", tstg[:, 13, 0:PAST], c_kiT[b], writes=["tstg"])
            S.dma("sp", SKIT[b][:, 0:PAST], tstg[:, 13, 0:PAST], reads=["tstg"], writes=[("SKITc", b)])
            S.dma("pool", vstg[:], c_v[b].rearrange("(t p) d -> p t d", p=128), writes=["vstg"])
            S.dma("sp", SVB[b, 0:PAST, :].rearrange("(t p) d -> p t d", p=128), vstg[:], reads=["vstg"], writes=[("SVBc", b)])
        S.barrier()
        xav = xT_all.rearrange("(k p) t -> p k t", p=128)
        cnt = {"z": 0, "bank": 0, "tb": 0, "zb": 0, "pn": 0}
        pend3 = []
        for g in range(SEQ // G):
            gc = g * G
            for q4 in range(4):
                S.dma("pool", hg[:, q4 * 8:(q4 + 1) * 8, :], xav[:, q4 * 8:(q4 + 1) * 8, gc:gc + G],
                      writes=[("hg", k) for k in range(q4 * 8, q4 * 8 + 8)])
            for ti in range(NTI):
                S.dma("sp", csm[:, ti, :], cs_mla_all[gc + ti * 128:gc + (ti + 1) * 128, :], writes=[("csm", ti)])
                S.dma("sp", csd[:, ti, :], cs_dsa_all[gc + ti * 128:gc + (ti + 1) * 128, :], writes=[("csd", ti)])
            for k in range(KD):
                S.op("act", lambda e, k=k: e.activation(out=sq[k % 2][:], in_=hg[:, k, :], func=AF.Square),
                     reads=[("hg", k)], writes=[("sq", k % 2)])
                S.op("pe", lambda pe, k=k: pe.matmul(ps[0][:, 0:G], lhsT=ones[:, :], rhs=sq[k % 2][:, :],
                                                     start=(k == 0), stop=(k == KD - 1)),
                     reads=[("sq", k % 2), "ones"], writes=[psk(0)])
            S.op("act", lambda e: e.activation(out=rstd[:, :], in_=ps[0][:, 0:G], func=AF.Sqrt, bias=epsT[:, 0:1], scale=1.0 / D),
                 reads=[psk(0), "epsT"], writes=["rstd"])
            S.op("dve", lambda e: e.reciprocal(out=rstd[:, :], in_=rstd[:, :]), reads=["rstd"], writes=["rstd"])
            for k in range(KD):
                S.op("dve", lambda e, k=k: e.tensor_tensor(out=tmp[k % 2][:], in0=hg[:, k, :], in1=rstd[:], op=ALU.mult),
                     reads=[("hg", k), "rstd"], writes=[("tmp", k % 2)])
                S.op("act", lambda e, k=k: e.activation(out=hg[:, k, :], in_=tmp[k % 2][:], func=AF.Identity,
                                                        bias=adaT[:, SH1 + k, 0:1], scale=A1[:, k, 0:1]),
                     reads=[("tmp", k % 2)], writes=[("hg", k)])
            hgk = [("hg", k) for k in range(KD)]
            for pi_, (o, w) in enumerate(KOFF):
                i = cnt["pn"] % 2
                cnt["pn"] += 1
                S.dma("pool", wp[i][:, :, 0:w], WKB[:, :, KPOS[pi_]:KPOS[pi_] + w],
                      reads=[("WKB", p_) for p_ in range(len(KOFF))], writes=[("wp", i)])
                for ti in range(NTI):
                    t0 = ti * 128
                    bank = 1 + cnt["bank"] % 3
                    cnt["bank"] += 1
                    mm(bank, (0, w), [(hg[:, k, t0:t0 + 128], wp[i][:, k, 0:w]) for k in range(KD)], reads=[("wp", i)] + hgk)
                    while pend3:
                        pend3.pop(0)()
                    zi = cnt["z"] % 3
                    cnt["z"] += 1
                    S.op("act", lambda e, zi=zi, bank=bank, w=w: e.activation(out=zt[zi][:, 0:w], in_=ps[bank][:, 0:w], func=AF.Copy),
                         reads=[psk(bank)], writes=[("z", zi)])
                    blocks = None
                    if pi_ == 0:
                        zo = (zi + 1) % 3
                        cnt["z"] += 1
                        rms_rows(zt[zi][:, :], 128, 512, gkv, zt[zo][:, :], ("z", zi), "gkv", ssq, ("z", zo), "ckv")
                        zi = zo
                        blocks, sbase = [(c * 128, 128) for c in range(4)], 0
                    elif pi_ == 1:
                        rope_inplace("a", zt[zi][:, 0:64].rearrange("p (h d) -> p h d", h=1), 128, 1, 32, csm[:, ti, :], rt, ("z", zi), ("csm", ti))
                    elif pi_ in (2, 3):
                        rope_inplace("a", zt[zi][:, :].rearrange("p (h d) -> p h d", h=4), 128, 4, 16, csd[:, ti, :], rt, ("z", zi), ("csd", ti))
                        blocks, sbase = [(c * 128, 128) for c in range(4)], 5 + (pi_ - 2) * 4
                    elif pi_ == 6:
                        rope_inplace("a", zt[zi][:, 0:128].rearrange("p (h d) -> p h d", h=1), 128, 1, 16, csd[:, ti, :], rt, ("z", zi), ("csd", ti))
                        blocks, sbase = [(0, 128)], 13
                    if pi_ in (4, 5):
                        S.op("act", lambda e, zi=zi, ti=ti, pi_=pi_: e.activation(
                            out=vstg[:, ti, (pi_ - 4) * 512:(pi_ - 3) * 512], in_=zt[zi][:, :], func=AF.Copy),
                            reads=[("z", zi)], writes=[("vstg", ti, pi_)])
                        continue
                    bi = cnt["zb"] % 2
                    cnt["zb"] += 1
                    if pi_ == 1:
                        for hh in range(2):
                            S.op("act", lambda e, hh=hh, bi=bi, zi=zi: e.activation(
                                out=zb[bi][:, hh * 64:(hh + 1) * 64], in_=zt[zi][:, 0:64], func=AF.Copy),
                                reads=[("z", zi)], writes=[("zb", bi)])
                        blocks, sbase = [(0, 128)], 4
                    else:
                        S.op("act", lambda e, bi=bi, zi=zi, w=w: e.activation(out=zb[bi][:, 0:w], in_=zt[zi][:, 0:w], func=AF.Copy),
                             reads=[("z", zi)], writes=[("zb", bi)])
                    def trs(blocks=blocks, bi=bi, sbase=sbase, t0=t0, ti=ti):
                        for bi_, (c0, wdt) in enumerate(blocks):
                            tb = 4 + cnt["tb"] % 4
                            cnt["tb"] += 1
                            transpose_to(tb, 0, zb[bi][:, c0:c0 + wdt], 128, reads=[("zb", bi)])
                            S.op("dve", lambda e, tb=tb, bi_=bi_: e.tensor_copy(
                                out=tstg[:, sbase + bi_, t0:t0 + 128], in_=psbf(tb)[:, 0:128]),
                                reads=[psk(tb)], writes=[("tstg", sbase + bi_, ti)])
                    pend3.append(trs)
            while pend3:
                pend3.pop(0)()
            tk = lambda lo, hi: [("tstg", s_, ti_) for s_ in range(lo, hi) for ti_ in range(NTI)]
            S.dma("sp", CKVT.rearrange("c p l -> p c l")[:, :, gc:gc + G], tstg[:, 0:4, 0:G], reads=tk(0, 4), writes=[("CKVT", g)])
            S.dma("sp", KRT[:, gc:gc + G], tstg[:, 4, 0:G], reads=tk(4, 5), writes=[("KRT", g)])
            S.dma("sp", KBT.rearrange("c p l -> p c l")[:, :, gc:gc + G], tstg[:, 5:13, 0:G], reads=tk(5, 13), writes=[("KBT", g)])
            S.dma("sp", KIT[:, gc:gc + G], tstg[:, 13, 0:G], reads=tk(13, 14), writes=[("KIT", g)])
            S.dma("sp", VB[gc:gc + G, :].rearrange("(t p) d -> p t d", p=128), vstg[:, 0:NTI, :],
                  reads=[("vstg", ti_, p_) for ti_ in range(NTI) for p_ in (4, 5)], writes=[("VB", g)])
        S.barrier()

    with ExitStack() as ph:
        def psb(name, shape, dt=F32):
            return ph.enter_context(nc.sbuf_tensor(f"p4_{name}", list(shape), dt))
        kit = psb("kit", [128, SEQ], BF16)
        skit = psb("skit", [128, 2 * V, LS], BF16)
        qi = [psb(f"qi{i}", [128, 32, 128], BF16) for i in range(2)]
        SAs = [psb("SA0", [128, SEQ]), psb("SA1", [128, SEQ])]
        Wk = psb("Wk", [128, SEQ])
        mk = psb("mk", [128, SEQ], BF16)
        rb = [psb(f"rb{i}", [128, 512], BF16) for i in range(4)]
        dg = [psb(f"dg{i}", [128, 32, 128], BF16) for i in range(2)]
        m8 = psb("m8", [128, 8])
        thr = psb("thr", [128, 1])
        mstg = psb("mstg", [128, 64, 128], BF16)
        S.dma("sp", kit[:], KIT[:, :], writes=["kit"])
        for b in range(2 * V):
            S.dma("sp", skit[:, b, :], SKIT[b], writes=["kit"])
        cnt = {"bank": 0, "rr": 0, "tb": 0, "ab": 0, "tile": 0}

        def index_tile(v, j, qcols, nq, keys_ap_fn, L, wi_ap, adm, out_fn, wkey):
            slot = cnt["tile"] % 2
            cnt["tile"] += 1
            SA = SAs[slot]
            qb_ = qi[j % 2]
            S.dma("sp", qb_[:, :, 0:nq], QIT[v].rearrange("h p t -> p h t")[:, :, qcols:qcols + nq], writes=[("qi", j % 2)])
            blocks = [(c0, min(512, L - c0)) for c0 in range(0, L, 512)]
            dgt = dg[j % 2]
            for h in range(32):
                S.op("dve", lambda e, h=h: e.tensor_scalar(out=dgt[0:nq, h, 0:nq], in0=ident[0:nq, 0:nq], scalar1=wi_ap[:, h:h + 1],
                                                          scalar2=None, op0=ALU.mult),
                     reads=["ident", wkey], writes=[("dg", j % 2)])
            for (c0, w) in blocks:
                ab = 4 + cnt["ab"] % 2
                cnt["ab"] += 1

                def s_mm(h, c0=c0, w=w):
                    bank = cnt["bank"] % 4
                    cnt["bank"] += 1
                    mm(bank, (0, w), [(qb_[:, h, 0:nq], keys_ap_fn(c0, w))], reads=[("qi", j % 2), "kit"])
                    return bank
                nxt = s_mm(0)
                for h in range(32):
                    bank = nxt
                    if h + 1 < 32:
                        nxt = s_mm(h + 1)
                    ri = cnt["rr"] % 4
                    cnt["rr"] += 1
                    S.op("act", lambda e, bank=bank, ri=ri, w=w: e.activation(out=rb[ri][0:nq, 0:w], in_=ps[bank][0:nq, 0:w], func=AF.Relu),
                         reads=[psk(bank)], writes=[("rb", ri)])
                    S.op("pe", lambda pe, h=h, ri=ri, w=w, ab=ab: pe.matmul(ps[ab][0:nq, 0:w], lhsT=dgt[0:nq, h, 0:nq], rhs=rb[ri][0:nq, 0:w],
                                                                      start=(h == 0), stop=(h == 31)),
                         reads=[("rb", ri), ("dg", j % 2)], writes=[psk(ab)])
                S.op("act", lambda e, ab=ab, c0=c0, w=w: e.activation(out=SA[0:nq, c0:c0 + w], in_=ps[ab][0:nq, 0:w], func=AF.Copy),
                     reads=[psk(ab)], writes=[("SA", slot, c0)])
            def part_b():
                sak = [("SA", slot, c0) for (c0, w) in blocks]
                if adm:
                    S.op("dve", lambda e: e.tensor_tensor(out=SA[0:nq, L - 1024:L], in0=SA[0:nq, L - 1024:L], in1=admA[0:nq, v, :], op=ALU.add),
                         reads=sak + ["admA"], writes=sak)
                cur = SA
                for r in range(TOPK // 8):
                    S.op("dve", lambda e, cur=cur: e.max(out=m8[0:nq, :], in_=cur[0:nq, 0:L]), reads=sak + ["Wk"], writes=["m8"])
                    if r < TOPK // 8 - 1:
                        S.op("dve", lambda e, cur=cur: e.match_replace(out=Wk[0:nq, 0:L], in_to_replace=m8[0:nq, :],
                                                                       in_values=cur[0:nq, 0:L], imm_value=NEG),
                             reads=sak + ["m8", "Wk"], writes=["Wk"])
                        cur = Wk
                S.op("dve", lambda e: e.tensor_reduce(out=thr[0:nq, :], in_=m8[0:nq, :], axis=AX.X, op=ALU.min), reads=["m8"], writes=["thr"])
                S.op("dve", lambda e: e.tensor_scalar(out=thr[0:nq, :], in0=thr[0:nq, :], scalar1=-1e29, scalar2=None, op0=ALU.max),
                     reads=["thr"], writes=["thr"])
                S.op("dve", lambda e: e.tensor_scalar(out=mk[0:nq, 0:L], in0=SA[0:nq, 0:L], scalar1=thr[0:nq, 0:1], scalar2=None, op0=ALU.is_ge),
                     reads=sak + ["thr"], writes=["mk"])
                nblk = (L + 127) // 128
                for kb in range(nblk):
                    kw = min(128, L - kb * 128)
                    tb = 6 + cnt["tb"] % 2
                    cnt["tb"] += 1
                    transpose_to(tb, 0, mk[0:nq, kb * 128:kb * 128 + kw], nq, reads=["mk"])
                    S.op("act", lambda e, tb=tb, kb=kb, kw=kw: e.activation(out=mstg[0:kw, kb, 0:nq], in_=psbf(tb)[0:kw, 0:nq], func=AF.Copy),
                         reads=[psk(tb)], writes=[("mstg", kb)])
                out_fn(nblk, [("mstg", kb) for kb in range(nblk)])
            return part_b

        wis = [psb(f"wis{b}", [16, 32]) for b in range(2)]
        pend_b = []

        def run_b():
            while pend_b:
                pend_b.pop(0)()
        for v in range(V):
            for j in range(8):
                L = 1024 * (j + 1)
                pb_ = index_tile(v, j, j * 128, 128, lambda c0, w: kit[:, c0:c0 + w], L, WI[:, v, j, :], True,
                           lambda nblk, keys, j=j, v=v: S.dma("sp", MASKT[v, 0:nblk].rearrange("k p q -> p k q")[:, :, j * 128:(j + 1) * 128],
                                                             mstg[:, 0:nblk, :], reads=keys, writes=[("MASKT", j)]), "WIp")
                run_b()
                pend_b.append(pb_)
            for b in range(2):
                def outs(nblk, keys, b=b, v=v):
                    S.dma("sp", SMASKT[2 * v + b, 0:8].rearrange("k p q -> p k q"), mstg[:, 0:8, 0:16], reads=keys, writes=[("SMASKT", b, 0)])
                    S.dma("sp", SMASKT[2 * v + b, 8, 0:16, :], mstg[0:16, 8, 0:16], reads=keys, writes=[("SMASKT", b, 1)])
                S.dma("sp", wis[b][:], WI[b * 16:(b + 1) * 16, v, 8, :], writes=[("wis", b)])
                pb_ = index_tile(v, 8 + b, 1024 + b * 16, 16, lambda c0, w, b=b, v=v: skit[:, 2 * v + b, c0:c0 + w], LS, wis[b], False, outs, ("wis", b))
                run_b()
                pend_b.append(pb_)
        run_b()
        S.barrier()

    def attention(tagp, sbufs, kparts_fn, v_fn, kblocks, q0_fn, QA, QB, scale, mask_fn, rdeps, fin_fn):
        pT = sbufs["pT"]
        nkb = len(kblocks)
        if QB - QA <= 16 and nkb * (QB - QA) <= 512:
            nq = QB - QA
            pt, ptk = pT[0], ("pT", 0)

            def fs(pe):
                ins = None
                for i, (kb, kw) in enumerate(kblocks):
                    parts = kparts_fn(kb, kw)
                    for pi_, (l, rfn) in enumerate(parts):
                        ins = pe.matmul(ps[0][0:kw, i * nq:(i + 1) * nq], lhsT=l, rhs=rfn(QA, QB),
                                        start=(pi_ == 0), stop=(pi_ == len(parts) - 1))
                return ins
            S.op("pe", fs, reads=rdeps, writes=[psk(0)])
            i0 = 0
            while i0 < nkb:
                i1 = i0
                while i1 < nkb and kblocks[i1][1] == kblocks[i0][1]:
                    i1 += 1
                kw = kblocks[i0][1]
                S.op("act", lambda e, i0=i0, i1=i1, kw=kw: e.activation(out=pt[0:kw, i0 * nq:i1 * nq], in_=ps[0][0:kw, i0 * nq:i1 * nq],
                                                                        func=AF.Exp, scale=scale), reads=[psk(0)], writes=[ptk])
                i0 = i1
            for i, (kb, kw) in enumerate(kblocks):
                mask_fn(kb, kw, 0, pt[:, i * nq:(i + 1) * nq], ptk)

            def fpv(pe):
                ins = None
                for i, (kb, kw) in enumerate(kblocks):
                    pe.matmul(ps[4][:, 0:nq], lhsT=v_fn(kb, kw), rhs=pt[0:kw, i * nq:(i + 1) * nq], start=(i == 0), stop=(i == nkb - 1))
                for i, (kb, kw) in enumerate(kblocks):
                    ins = pe.matmul(ps[6][:, 0:nq], lhsT=ones[0:kw, :], rhs=pt[0:kw, i * nq:(i + 1) * nq], start=(i == 0), stop=(i == nkb - 1))
                return ins
            S.op("pe", fpv, reads=[ptk, "ones"] + rdeps, writes=[psk(4), psk(6)])
            rec = sbufs["rec"]
            S.op("dve", lambda e: e.reciprocal(out=rec[:, 0:nq], in_=ps[6][:, 0:nq]), reads=[psk(6)], writes=[("rec", 0)])
            fin_fn(0, QA, nq, rec)
            return

        def s_mm(i):
            kb, kw = kblocks[i]
            q0 = q0_fn(kb)
            par = (i % 2) * 2
            parts = kparts_fn(kb, kw)
            for gi, (c0, c1) in enumerate(((QA, QA + 512), (QA + 512, QB))):
                a, bb = max(c0, q0), min(c1, QB)
                if a >= bb:
                    continue

                def fn(pe, a=a, bb=bb, gi=gi):
                    ins = None
                    for pi_, (l, rfn) in enumerate(parts):
                        ins = pe.matmul(ps[par + gi][0:kw, a - c0:bb - c0], lhsT=l, rhs=rfn(a, bb),
                                        start=(pi_ == 0), stop=(pi_ == len(parts) - 1))
                    return ins
                S.op("pe", fn, reads=rdeps, writes=[psk(par + gi)])

        s_mm(0)
        for i, (kb, kw) in enumerate(kblocks):
            if i + 1 < nkb:
                s_mm(i + 1)
            q0 = q0_fn(kb)
            par = (i % 2) * 2
            pt = pT[i % 2]
            ptk = ("pT", i % 2)
            if q0 < QA + 512 < QB:
                S.op("act", lambda e, par=par, pt=pt, kw=kw, q0=q0: e.activation(
                    out=pt[0:kw, q0 - QA:QB - QA], in_=psbig[0:kw, par * 512 + q0 - QA:par * 512 + QB - QA], func=AF.Exp, scale=scale),
                    reads=[psk(par), psk(par + 1)], writes=[ptk])
            else:
                for gi, (c0, c1) in enumerate(((QA, QA + 512), (QA + 512, QB))):
                    a, bb = max(c0, q0), min(c1, QB)
                    if a >= bb:
                        continue
                    S.op("act", lambda e, a=a, bb=bb, c0=c0, gi=gi, par=par, pt=pt, kw=kw: e.activation(
                        out=pt[0:kw, a - QA:bb - QA], in_=ps[par + gi][0:kw, a - c0:bb - c0], func=AF.Exp, scale=scale),
                        reads=[psk(par + gi)], writes=[ptk])
            mask_fn(kb, kw, q0, pt, ptk)
            for gi, (c0, c1) in enumerate(((QA, QA + 512), (QA + 512, QB))):
                a, bb = max(c0, q0), min(c1, QB)
                if a >= bb:
                    continue

                def fn(pe, a=a, bb=bb, c0=c0, gi=gi, kb=kb, kw=kw, i=i, pt=pt):
                    pe.matmul(ps[4 + gi][:, a - c0:bb - c0], lhsT=v_fn(kb, kw), rhs=pt[0:kw, a - QA:bb - QA],
                              start=(i == 0), stop=(i == nkb - 1))
                    return pe.matmul(ps[6 + gi][:, a - c0:bb - c0], lhsT=ones[0:kw, :], rhs=pt[0:kw, a - QA:bb - QA],
                                     start=(i == 0), stop=(i == nkb - 1))
                S.op("pe", fn, reads=[ptk, "ones"] + rdeps, writes=[psk(4 + gi), psk(6 + gi)])
        rec = sbufs["rec"]
        for gi, (c0, c1) in enumerate(((QA, QA + 512), (QA + 512, QB))):
            if c0 >= QB:
                continue
            n = min(c1, QB) - c0
            S.op("dve", lambda e, gi=gi, n=n, c0=c0: e.reciprocal(out=rec[:, c0 - QA:c0 - QA + n], in_=ps[6 + gi][:, 0:n]),
                 reads=[psk(6 + gi)], writes=[("rec", gi)])
            fin_fn(gi, c0, n, rec)

    with ExitStack() as ph:
        def psb(name, shape, dt=F32):
            return ph.enter_context(nc.sbuf_tensor(f"p5_{name}", list(shape), dt))
        ckvT = psb("ckvT", [128, 4, SEQ], BF16)
        krT = psb("krT", [128, SEQ], BF16)
        sckvT = psb("sckvT", [128, 2 * V, 4, LS], BF16)
        skrT = psb("skrT", [128, 2 * V, LS], BF16)
        kn = psb("kn", [128, SEQ], BF16)
        vh = psb("vh", [128, 64, 128], BF16)
        wuk = [psb(f"wuk{i}", [128, 4, 128], BF16) for i in range(2)]
        wuv = [psb(f"wuv{i}", [128, 4, 128], BF16) for i in range(2)]
        qn = [psb(f"qn{i}", [128, NT], BF16) for i in range(V)]
        qr = [psb(f"qr{i}", [128, NT], BF16) for i in range(V)]
        ga = [psb(f"ga{i}", [128, NT], BF16) for i in range(V)]
        pT = [psb(f"pT{i}", [128, 1024], BF16) for i in range(2)]
        rec = psb("rec", [128, 1024])
        otmp = psb("otmp", [128, 512])
        mo = [psb(f"mo{i}", [128, NT], BF16) for i in range(V)]
        for c in range(4):
            S.dma("sp", ckvT[:, c, :], CKVT[c], writes=[("ckvT", c)])
        S.dma("sp", krT[:], KRT[:, :], writes=["krT"])
        for b in range(2 * V):
            S.dma("sp", sckvT[:, b, :, :], SCKVT[b].rearrange("c p l -> p c l"), writes=[("sckvT", b)])
            S.dma("sp", skrT[:, b, :], SKRT[b], writes=[("skrT", b)])
        ckeys = [("ckvT", c) for c in range(4)] + [("sckvT", b) for b in range(2 * V)]
        krkeys = ["krT"] + [("skrT", b) for b in range(2 * V)]
        wukv = w_uk.rearrange("(c p) n -> p c n", p=128)
        wuvv = w_uv.rearrange("(c p) n -> p c n", p=128)
        bufs = {"pT": pT, "rec": rec}
        PBLK = [(kb, 128) for kb in range(64)]
        SBLK = [(kb, 128) for kb in range(8)] + [(8, 16)]
        for h in range(32):
            hb = h % 2
            rp = (h % 2) * 64
            S.dma("pool", wuk[hb][:], wukv[:, :, h * 128:(h + 1) * 128], writes=[("wuk", hb)])
            S.dma("pool", wuv[hb][:], wuvv[:, :, h * 128:(h + 1) * 128], writes=[("wuv", hb)])
            for v in range(V):
                S.dma("sp", qn[v][:], QNT[v, h], writes=[("qn", v)])
                S.dma("sp", qr[v][:], QRT[v, h // 2], writes=[("qr", v)])
                S.dma("sp", ga[v][:], GAT[v, h], writes=[("ga", v)])

            def materialize(cfn, L, kblocks):
                for c0 in range(0, L, 512):
                    w = min(512, L - c0)
                    bank = (c0 // 512) % 4
                    mm(bank, (0, w), [(wuk[hb][:, c, :], cfn(c, c0, w)) for c in range(4)], reads=[("wuk", hb)] + ckeys)
                    S.op("act", lambda e, bank=bank, c0=c0, w=w: e.activation(out=kn[:, c0:c0 + w], in_=ps[bank][:, 0:w], func=AF.Copy),
                         reads=[psk(bank)], writes=["kn"])
                for k4 in range(0, len(kblocks), 4):
                    bank = 4 + (k4 // 4) % 4
                    blk = kblocks[k4:k4 + 4]

                    def fn(pe, blk=blk, bank=bank):
                        ins = None
                        for bi_, (kb, kw) in enumerate(blk):
                            for c in range(4):
                                ins = pe.matmul(ps[bank][0:kw, bi_ * 128:(bi_ + 1) * 128], lhsT=cfn(c, kb * 128, kw), rhs=wuv[hb][:, c, :],
                                                start=(c == 0), stop=(c == 3))
                        return ins
                    S.op("pe", fn, reads=[("wuv", hb)] + ckeys, writes=[psk(bank)])
                    for bi_, (kb, kw) in enumerate(blk):
                        S.op("dve", lambda e, bank=bank, bi_=bi_, kb=kb, kw=kw: e.tensor_copy(
                            out=vh[0:kw, kb, :], in_=ps[bank][0:kw, bi_ * 128:(bi_ + 1) * 128]), reads=[psk(bank)], writes=["vh"])

            def run_set(v, is_prompt, kr_src, QA, QB, kblocks, q0_fn):
                def kparts(kb, kw):
                    return [(kn[:, kb * 128:kb * 128 + kw], lambda a, bb: qn[v][:, a:bb]),
                            (kr_src(kb * 128, kw), lambda a, bb: qr[v][rp:rp + 64, a:bb])]

                def mask_fn(kb, kw, q0, pt, ptk):
                    if is_prompt:
                        S.op("dve", lambda e: e.tensor_scalar(out=pt[:, q0:q0 + 64], in0=pt[:, q0:q0 + 64],
                                                             scalar1=maskB[:, v, kb % 4:kb % 4 + 1], scalar2=None, op0=ALU.mult),
                             reads=[ptk, "maskB"], writes=[ptk])

                def fin(gi, c0, n, rec_):
                    S.op("dve", lambda e: e.tensor_tensor(out=otmp[:, 0:n], in0=ps[4 + gi][:, 0:n], in1=rec_[:, c0 - QA:c0 - QA + n], op=ALU.mult),
                         reads=[psk(4 + gi), ("rec", gi)], writes=["otmp"])
                    S.op("dve", lambda e: e.tensor_tensor(out=mo[v][:, c0:c0 + n], in0=otmp[:, 0:n], in1=ga[v][:, c0:c0 + n], op=ALU.mult),
                         reads=["otmp", ("ga", v)], writes=[("mo", v, c0)])
                    mokeys[v].append(("mo", v, c0))
                attention("mla", bufs, kparts, lambda kb, kw: vh[0:kw, kb, :], kblocks, q0_fn, QA, QB, MLA_SCALE, mask_fn,
                          ["kn", "vh", ("qn", v), ("qr", v)] + krkeys, fin)

            mokeys = [[] for _ in range(V)]
            materialize(lambda c, a, w: ckvT[:, c, a:a + w], SEQ, PBLK)
            for v in range(V):
                run_set(v, True, lambda a, w: krT[rp:rp + 64, a:a + w], 0, NP, PBLK, lambda kb: 64 * (kb // 4))
            for v in range(V):
                for b in range(2):
                    sb_ = 2 * v + b
                    materialize(lambda c, a, w, sb_=sb_: sckvT[:, sb_, c, a:a + w], LS, SBLK)
                    run_set(v, False, lambda a, w, sb_=sb_: skrT[rp:rp + 64, sb_, a:a + w], NP + 16 * b, NP + 16 * b + 16, SBLK, lambda kb: 0)
            for v in range(V):
                S.dma("sp", MAT[v, h], mo[v][:], reads=mokeys[v], writes=[("MAT", v, h)])
        S.barrier()

    with ExitStack() as ph:
        def psb(name, shape, dt=F32):
            return ph.enter_context(nc.sbuf_tensor(f"p6_{name}", list(shape), dt))
        kg = [psb(f"kg{i}", [128, SEQ], BF16) for i in range(2)]
        vg = [psb(f"vg{i}", [128, 64, 128], BF16) for i in range(2)]
        skg = [psb(f"skg{i}", [128, 2, LS], BF16) for i in range(2)]
        svg = [psb(f"svg{i}", [128, 2, 9, 128], BF16) for i in range(2)]
        MOFF = []
        mo_ = 0
        for kb in range(64):
            MOFF.append(mo_)
            mo_ += NP - 64 * (kb // 4)
        mT = psb("mT", [128, mo_], BF16)
        smT = psb("smT", [128, 2, 9, 16], BF16)
        qb = [psb(f"qb{i}", [128, NT], BF16) for i in range(2)]
        gb = [psb(f"gb{i}", [128, NT], BF16) for i in range(2)]
        ma = [psb(f"ma{i}", [128, NT], BF16) for i in range(2)]
        pT = [psb(f"pT{i}", [128, 1024], BF16) for i in range(2)]
        rec = psb("rec", [128, 1024])
        otmp = psb("otmp", [128, 512])
        mo = [psb(f"mo{i}", [128, NT], BF16) for i in range(2)]
        bufs = {"pT": pT, "rec": rec}
        PBLK = [(kb, 128) for kb in range(64)]
        SBLK = [(kb, 128) for kb in range(8)] + [(8, 16)]
        gcount = 0
        for v in range(V):
            for kb in range(64):
                q0 = 64 * (kb // 4)
                S.dma("sp", mT[:, MOFF[kb]:MOFF[kb] + NP - q0], MASKT[v, kb][:, q0:NP], writes=[("mT", kb)])
            for b in range(2):
                S.dma("sp", smT[:, b, :, :], SMASKT[2 * v + b].rearrange("k p q -> p k q"), writes=[("smT", b)])
            for g in range(8):
                gbi = gcount % 2
                gcount += 1
                S.dma("sp", kg[gbi][:], KBT[g], writes=[("kg", gbi)])
                S.dma("sp", vg[gbi][:], VB[:, g * 128:(g + 1) * 128].rearrange("(k p) d -> p k d", p=128), writes=[("vg", gbi)])
                for b in range(2):
                    S.dma("sp", skg[gbi][:, b, :], SKBT[2 * v + b, g], writes=[("skg", gbi, b)])
                    S.dma("sp", svg[gbi][:, b, :, :], SVB[2 * v + b, 0:9 * 128, g * 128:(g + 1) * 128].rearrange("(k p) d -> p k d", p=128),
                          writes=[("svg", gbi, b)])
                kvkeys = [("kg", gbi), ("vg", gbi)] + [("skg", gbi, b) for b in range(2)] + [("svg", gbi, b) for b in range(2)]
                for hh in range(4):
                    h = g * 4 + hh
                    hb = h % 2
                    S.dma("sp", qb[hb][:], QBT[v, h], writes=[("qb", hb)])
                    S.dma("sp", gb[hb][:], GBT[v, h], writes=[("gb", hb)])
                    S.dma("sp", ma[hb][:], MAT[v, h], writes=[("ma", hb)])
                    mokeys = []
                    for (sname, kfn, vfn, QA, QB, kblocks, q0_fn) in (
                        [("p", lambda a, w: kg[gbi][:, a:a + w], lambda kb, kw: vg[gbi][0:kw, kb, :], 0, NP, PBLK, lambda kb: 64 * (kb // 4))] +
                        [(f"s{b}", lambda a, w, b=b: skg[gbi][:, b, a:a + w], lambda kb, kw, b=b: svg[gbi][0:kw, b, kb, :],
                          NP + 16 * b, NP + 16 * b + 16, SBLK, lambda kb: 0) for b in range(2)]):
                        def kparts(kb, kw, kfn=kfn):
                            return [(kfn(kb * 128, kw), lambda a, bb: qb[hb][:, a:bb])]

                        def mask_fn(kb, kw, q0, pt, ptk, sname=sname):
                            if sname == "p":
                                S.op("dve", lambda e: e.tensor_tensor(out=pt[:, q0:NP], in0=pt[:, q0:NP], in1=mT[:, MOFF[kb]:MOFF[kb] + NP - q0], op=ALU.mult),
                                     reads=[ptk, ("mT", kb)], writes=[ptk])
                            else:
                                b = int(sname[1])
                                S.op("dve", lambda e: e.tensor_tensor(out=pt[0:kw, 0:16], in0=pt[0:kw, 0:16], in1=smT[0:kw, b, kb, :], op=ALU.mult),
                                     reads=[ptk, ("smT", b)], writes=[ptk])

                        def fin(gi, c0, n, rec_, QA=QA):
                            S.op("dve", lambda e: e.tensor_tensor(out=otmp[:, 0:n], in0=ps[4 + gi][:, 0:n], in1=rec_[:, c0 - QA:c0 - QA + n], op=ALU.mult),
                                 reads=[psk(4 + gi), ("rec", gi)], writes=["otmp"])
                            S.op("dve", lambda e: e.tensor_tensor(out=otmp[:, 0:n], in0=otmp[:, 0:n], in1=gb[hb][:, c0:c0 + n], op=ALU.mult),
                                 reads=["otmp", ("gb", hb)], writes=["otmp"])
                            S.op("dve", lambda e: e.tensor_tensor(out=mo[hb][:, c0:c0 + n], in0=otmp[:, 0:n], in1=ma[hb][:, c0:c0 + n], op=ALU.add),
                                 reads=["otmp", ("ma", hb)], writes=[("mo", hb, c0)])
                            mokeys.append(("mo", hb, c0))
                        attention("dsa", bufs, kparts, vfn, kblocks, q0_fn, QA, QB, DSA_SCALE, mask_fn, kvkeys + [("qb", hb)], fin)
                    S.dma("sp", MGT[v, h], mo[hb][:], reads=mokeys, writes=[("MGT", v, h)])
                    if DEBUG:
                        S.dma("sp", dbg_mg[v, h], mo[hb][:], reads=mokeys, writes=[("dbg", v, h)])
        S.barrier()

    with ExitStack() as ph:
        def psb(name, shape, dt=F32):
            return ph.enter_context(nc.sbuf_tensor(f"p7_{name}", list(shape), dt))
        mg = psb("mg", [128, KD, NT], BF16)
        wp = [psb(f"wp{i}", [128, KD, 512], BF16) for i in range(2)]
        xb = [psb(f"xb{i}", [128, NT]) for i in range(2)]
        x1 = [psb(f"x1{i}", [128, NT]) for i in range(2)]
        wv = w_out.rearrange("(k p) c -> p k c", p=128)
        pcount = 0
        for v in range(V):
            xov = xT_own[v].rearrange("(k p) t -> k p t", p=128)
            S.dma("sp", mg[:], MGT[v].rearrange("h p t -> p h t"), writes=["mg"])
            for pn in range(8):
                pw = pcount % 2
                pcount += 1
                S.dma("pool", wp[pw][:], wv[:, :, pn * 512:(pn + 1) * 512], writes=[("wp", pw)])
                for cc in range(4):
                    oc = pn * 4 + cc
                    S.dma("sp", xb[oc % 2][:], xov[oc], writes=[("xb", oc % 2)])
                    for gi, (c0, n) in enumerate(TGRP):
                        bank = (oc % 2) * 3 + gi
                        mm(bank, (0, n), [(wp[pw][:, j, cc * 128:(cc + 1) * 128], mg[:, j, c0:c0 + n]) for j in range(KD)],
                           reads=[("wp", pw), "mg"])
                    for bi_, (c0, c1, b) in enumerate(bcols(v)):
                        if c0 == 0:
                            for g2_ in range(2):
                                S.op("dve", lambda e, oc=oc, g2_=g2_: e.scalar_tensor_tensor(
                                    out=x1[oc % 2][:, g2_ * 512:(g2_ + 1) * 512], in0=ps[(oc % 2) * 3 + g2_][:, :], scalar=adaT[:, GT1 + oc, 0:1],
                                    in1=xb[oc % 2][:, g2_ * 512:(g2_ + 1) * 512], op0=ALU.mult, op1=ALU.add),
                                    reads=[psk((oc % 2) * 3 + g2_), ("xb", oc % 2)], writes=[("x1", oc % 2, g2_)])
                        else:
                            S.op("dve", lambda e, oc=oc, c0=c0, c1=c1, b=b: e.scalar_tensor_tensor(
                                out=x1[oc % 2][:, c0:c1], in0=ps[(oc % 2) * 3 + 2][:, c0 - 1024:c1 - 1024], scalar=adaT[:, GT1 + oc, b:b + 1],
                                in1=xb[oc % 2][:, c0:c1], op0=ALU.mult, op1=ALU.add),
                                reads=[psk((oc % 2) * 3 + 2), ("xb", oc % 2)], writes=[("x1", oc % 2, 1 + bi_)])
                    S.dma("sp", X1T[v, oc], x1[oc % 2][:], reads=[("x1", oc % 2, i_) for i_ in range(4)], writes=[("X1T", v, oc)])
        S.barrier()

    H2T = dscr("H2T", [V, KD, 128, NT])
    for v in range(V):
        with ExitStack() as ph7:
            h2T = ph7.enter_context(nc.sbuf_tensor(f"h2T{v}", [128, KD, NT], BF16))
            norm_phase(X1T[v], A2, SH2, h2T, tag=f"n2{v}", v=v)
            S.dma("sp", H2T[v].rearrange("k p t -> p k t"), h2T[:], writes=["H2T"])
            S.barrier()
    with ExitStack() as ph:
        def psb(name, shape, dt=F32):
            return ph.enter_context(nc.sbuf_tensor(f"p8_{name}", list(shape), dt))
        HN = NT // 2
        FB = 8
        h2h = psb("h2h", [128, KD, HN], BF16)
        acc = psb("acc", [128, KD, HN])
        uT = psb("uT", [128, FB, HN], BF16)
        wu = [psb(f"wu{i}", [128, KD, 256], BF16) for i in range(2)]
        wd = [psb(f"wd{i}", [128, FB, 512], BF16) for i in range(2)]
        ur = [psb(f"ur{i}", [128, HN]) for i in range(2)]
        xb = [psb(f"xb{i}", [128, HN]) for i in range(2)]
        xo = [psb(f"xo{i}", [128, HN]) for i in range(2)]
        wuv_ = w_up.rearrange("(k p) c -> p k c", p=128)
        wdv_ = w_down.rearrange("(fb fc p) c -> fb p fc c", fc=FB, p=128)
        HG = [(0, 512), (512, HN - 512)]
        cnt = {"wu": 0, "wd": 0, "bank": 0}
        for vh_ in range(2 * V):
            v, half = vh_ // 2, vh_ % 2
            hc = half * HN
            S.dma("sp", h2h[:], H2T[v].rearrange("k p t -> p k t")[:, :, hc:hc + HN], writes=["h2h"])
            for fb in range(128 // FB):
                for f2 in range(FB // 2):
                    i = cnt["wu"] % 2
                    cnt["wu"] += 1
                    fcol = (fb * FB + f2 * 2) * 128
                    S.dma("pool", wu[i][:], wuv_[:, :, fcol:fcol + 256], writes=[("wu", i)])
                    for fl in range(2):
                        fci = f2 * 2 + fl
                        bp = (cnt["bank"] % 2) * 2
                        cnt["bank"] += 1
                        for gi, (c0, n) in enumerate(HG):
                            mm(bp + gi, (0, n), [(wu[i][:, k, fl * 128:(fl + 1) * 128], h2h[:, k, c0:c0 + n]) for k in range(KD)],
                               reads=[("wu", i), "h2h"])
                        ui = fci % 2
                        for gi, (c0, n) in enumerate(HG):
                            S.op("act", lambda e, bp=bp, gi=gi, c0=c0, n=n, ui=ui: e.activation(
                                out=ur[ui][:, c0:c0 + n], in_=ps[bp + gi][:, 0:n], func=AF.Relu), reads=[psk(bp + gi)], writes=[("ur", ui, gi)])
                        S.op("dve", lambda e, ui=ui, fci=fci: e.tensor_tensor(out=uT[:, fci, :], in0=ur[ui][:], in1=ur[ui][:], op=ALU.mult),
                             reads=[("ur", ui, 0), ("ur", ui, 1)], writes=[("uT", fci)])
                utk = [("uT", f_) for f_ in range(FB)]
                for og in range(8):
                    i = cnt["wd"] % 2
                    cnt["wd"] += 1
                    S.dma("pool", wd[i][:], wdv_[fb][:, :, og * 512:(og + 1) * 512], writes=[("wd", i)])
                    for ocl in range(4):
                        oc = og * 4 + ocl
                        bp = 4 + (oc % 2) * 2
                        for gi, (c0, n) in enumerate(HG):
                            mm(bp + gi, (0, n), [(wd[i][:, fc, ocl * 128:(ocl + 1) * 128], uT[:, fc, c0:c0 + n]) for fc in range(FB)],
                               reads=[("wd", i)] + utk)
                        for gi, (c0, n) in enumerate(HG):
                            if fb == 0:
                                S.op("dve", lambda e, bp=bp, gi=gi, c0=c0, n=n, oc=oc: e.tensor_copy(out=acc[:, oc, c0:c0 + n], in_=ps[bp + gi][:, 0:n]),
                                     reads=[psk(bp + gi)], writes=[("acc", oc, gi)])
                            else:
                                S.op("dve", lambda e, bp=bp, gi=gi, c0=c0, n=n, oc=oc: e.tensor_tensor(
                                    out=acc[:, oc, c0:c0 + n], in0=ps[bp + gi][:, 0:n], in1=acc[:, oc, c0:c0 + n], op=ALU.add),
                                    reads=[psk(bp + gi), ("acc", oc, gi)], writes=[("acc", oc, gi)])
            for oc in range(KD):
                S.dma("sp", xb[oc % 2][:], X1T[v, oc][:, hc:hc + HN], writes=[("xb", oc % 2)])
                wk_ = []
                for bi_, (c0, c1, b) in enumerate(bcols(v)):
                    a, bb = max(c0, hc), min(c1, hc + HN)
                    if a >= bb:
                        continue
                    S.op("dve", lambda e, oc=oc, a=a, bb=bb, b=b: e.scalar_tensor_tensor(
                        out=xo[oc % 2][:, a - hc:bb - hc], in0=acc[:, oc, a - hc:bb - hc], scalar=adaT[:, GT2 + oc, b:b + 1],
                        in1=xb[oc % 2][:, a - hc:bb - hc], op0=ALU.mult, op1=ALU.add),
                        reads=[("acc", oc, 0), ("acc", oc, 1), ("xb", oc % 2)], writes=[("xo", oc % 2, bi_)])
                    wk_.append(("xo", oc % 2, bi_))
                S.dma("sp", X2T[v, oc][:, hc:hc + HN], xo[oc % 2][:], reads=wk_, writes=[("X2T", v, oc, half)])
        S.barrier()

    for v in range(V):
        norm_phase(X2T[v], None, 0, None, final=True, tag=f"n3{v}", v=v)
    es.close()
    return nc


def _rope_table(pos, theta, rot):
    half = rot // 2
    inv = (np.float32(theta) ** (-np.arange(half, dtype=np.float32) / np.float32(half))).astype(np.float32)
    ang = pos.astype(np.float32)[:, None] * inv[None, :]
    return np.concatenate([np.cos(ang).astype(np.float32), np.sin(ang).astype(np.float32)], axis=1)


def _fm(v, nchunk):
    return np.ascontiguousarray(np.asarray(v, np.float32).reshape(nchunk, 128).T)


_NC_CACHE = {}


def kernel(x_prompt, x_sample, c_prompt, c_sample, cache_mla_ckv, cache_mla_krope, cache_dsa_k, cache_dsa_v,
           cache_idx_k, w_ada, b_ada, g_norm1, w_in, g_q_lora, w_uq, g_kv_lora, w_uk, w_uv, w_out, g_norm2,
           w_up, w_down, g_final):
    f32 = np.float32
    x_prompt = np.asarray(x_prompt, f32)
    x_sample = np.asarray(x_sample, f32)
    xT_all = np.ascontiguousarray(x_prompt[0].T)
    shared = {
        "xT_all": xT_all,
        "w_ada": np.asarray(w_ada, f32)[0], "b_adaT": _fm(np.asarray(b_ada)[0], 192),
        "g1T": _fm(np.asarray(g_norm1)[0], KD), "g2T": _fm(np.asarray(g_norm2)[0], KD), "gfT": _fm(g_final, KD),
        "w_in": np.asarray(w_in, f32)[0],
        "gq_rep": np.ascontiguousarray(np.broadcast_to(np.asarray(g_q_lora, f32)[0][None, :], (128, 1024))),
        "gkv_rep": np.ascontiguousarray(np.broadcast_to(np.asarray(g_kv_lora, f32)[0][None, :], (128, 512))),
        "w_uq": np.asarray(w_uq, f32)[0],
        "w_uk": np.asarray(w_uk, f32)[0].reshape(512, 4096), "w_uv": np.asarray(w_uv, f32)[0].reshape(512, 4096),
        "w_out": np.asarray(w_out, f32)[0], "w_up": np.asarray(w_up, f32)[0], "w_down": np.asarray(w_down, f32)[0],
        "cs_mla_all": _rope_table(np.arange(SEQ), 10000.0, 64), "cs_dsa_all": _rope_table(np.arange(SEQ), 500000.0, 32),
        "ident": np.eye(128, dtype=f32).astype(ml_dtypes.bfloat16), "ones": np.ones((128, 128), f32).astype(ml_dtypes.bfloat16),
    }
    in_maps = []
    own_idx = []
    csf = np.asarray(c_sample, f32)
    for pc in range(NPHYS):
        xo_l, csm_l, csd_l, mb_l, adm_l = [], [], [], [], []
        bs = []
        for v in range(V):
            i = pc * V + v
            chunks = [8 * m + i for m in range(16)]
            tok = np.concatenate([np.arange(64 * c, 64 * c + 64) for c in chunks])
            own_idx.append(tok)
            xo = np.concatenate([x_prompt[0][tok], x_sample[2 * i], x_sample[2 * i + 1]], axis=0)
            xo_l.append(np.ascontiguousarray(xo.T))
            pos = np.concatenate([tok, PAST + np.arange(16), PAST + np.arange(16)])
            csm_l.append(_rope_table(pos, 10000.0, 64))
            csd_l.append(_rope_table(pos, 500000.0, 32))
            maskB = np.zeros((128, 4), f32)
            for r in range(4):
                for a_ in range(2):
                    maskB[a_ * 64:(a_ + 1) * 64, r] = 1.0 if i >= 2 * r + a_ else 0.0
            admA = np.full((128, 1024), NEG, f32)
            admA[0:64, 0:64 * (i + 1)] = 0.0
            admA[64:128, 0:64 * (8 + i + 1)] = 0.0
            mb_l.append(maskB)
            adm_l.append(admA)
            bs += [2 * i, 2 * i + 1]
        cc = np.stack([np.asarray(c_prompt, f32)[0]] + [csf[b_] for b_ in bs], axis=1)
        cTm = np.ascontiguousarray(cc.reshape(KD, 128, NB).transpose(1, 0, 2).reshape(128, KD * NB))
        m = dict(shared)
        m.update({
            "xT_own": np.stack(xo_l), "cT": cTm,
            "cs_mla_own": np.stack(csm_l), "cs_dsa_own": np.stack(csd_l),
            "c_ckvT": np.ascontiguousarray(np.asarray(cache_mla_ckv, f32)[0][bs].transpose(0, 2, 1)),
            "c_krT": np.ascontiguousarray(np.asarray(cache_mla_krope, f32)[0][bs].transpose(0, 2, 1)),
            "c_kT": np.ascontiguousarray(np.asarray(cache_dsa_k, f32)[0][bs].transpose(0, 2, 3, 1)),
            "c_v": np.ascontiguousarray(np.asarray(cache_dsa_v, f32)[0][bs].reshape(2 * V, PAST, 1024)),
            "c_kiT": np.ascontiguousarray(np.asarray(cache_idx_k, f32)[0][bs].transpose(0, 2, 1)),
            "maskB": np.ascontiguousarray(np.concatenate(mb_l, axis=1)), "admA": np.ascontiguousarray(np.concatenate(adm_l, axis=1)),
        })
        in_maps.append(m)
    if "nc" not in _NC_CACHE:
        _NC_CACHE["nc"] = build_program()
    nc = _NC_CACHE["nc"]
    res = run_bass_kernel_spmd(nc, in_maps, core_ids=list(range(NPHYS)))
    R = res.results
    y_prompt = np.zeros((1, SEQ, D), f32)
    y_sample = np.zeros((16, 16, D), f32)
    outs_p = {k: np.zeros((1, 1, SEQ, w), f32) for k, w in (("o_ckv", 512), ("o_kr", 64), ("o_k", 1024), ("o_v", 1024), ("o_ki", 128))}
    outs_s = {k: np.zeros((1, 16, 16, w), f32) for k, w in (("o_ckv", 512), ("o_kr", 64), ("o_k", 1024), ("o_v", 1024), ("o_ki", 128))}
    for i in range(NCORES):
        r, v = R[i // V], i % V
        y = np.asarray(r["yT"])[v].T
        y_prompt[0][own_idx[i]] = y[0:NP]
        y_sample[2 * i] = y[NP:NP + 16]
        y_sample[2 * i + 1] = y[NP + 16:NT]
        for k in outs_p:
            a = np.asarray(r[k])[v]
            outs_p[k][0, 0][own_idx[i]] = a[0:NP]
            outs_s[k][0, 2 * i] = a[NP:NP + 16]
            outs_s[k][0, 2 * i + 1] = a[NP + 16:NT]
    if DEBUG:
        kernel.debug = [np.asarray(R[i // V]["dbg_mg"])[i % V] for i in range(NCORES)]
        kernel.own_idx = own_idx
    return (y_prompt, y_sample,
            outs_p["o_ckv"], outs_p["o_kr"], outs_p["o_k"].reshape(1, 1, SEQ, 8, 128), outs_p["o_v"].reshape(1, 1, SEQ, 8, 128), outs_p["o_ki"],
            outs_s["o_ckv"], outs_s["o_kr"], outs_s["o_k"].reshape(1, 16, 16, 8, 128), outs_s["o_v"].reshape(1, 16, 16, 8, 128), outs_s["o_ki"])
```

```python
import numpy as np
import ml_dtypes
from contextlib import ExitStack
import concourse.bass as bass
import concourse.mybir as mybir
from concourse.bass_utils import run_bass_kernel_spmd

F32 = mybir.dt.float32
BF16 = mybir.dt.bfloat16
AF = mybir.ActivationFunctionType
ALU = mybir.AluOpType
AX = mybir.AxisListType

NCORES = 8
NPHYS = 8
V = NCORES // NPHYS
NB = 1 + 2 * V
D = 4096
KD = 32
SEQ = 8192
NP = 1024
NS = 32
NT = NP + NS
PAST = 1024
LS = PAST + 16
EPS = 1e-6
NEG = -1e30
D_IN = 20192
O_QL, O_CKV, O_KR, O_QB, O_KB, O_VB, O_QI, O_KI, O_WI, O_GA, O_GB = (
    0, 1024, 1536, 1600, 5696, 6720, 7744, 11840, 11968, 12000, 16096)
MLA_SCALE = 192.0 ** -0.5
DSA_SCALE = 128.0 ** -0.5
IDX_W_SCALE = 32.0 ** -0.5
TOPK = 256
TT = [(i * 128, 128) for i in range(8)] + [(1024, 32)]
TGRP = [(0, 512), (512, 512), (1024, 32)]
DEBUG = False


class Sched:
    ENG = ("pe", "act", "dve", "pool", "sp")

    def __init__(self, nc, es, nphase=22, ndma=8):
        self.nc = nc
        self.eng = {"pe": nc.tensor, "act": nc.scalar, "dve": nc.vector, "pool": nc.gpsimd, "sp": nc.sync}
        self.psems = [{e: es.enter_context(nc.semaphore(f"p{p}_{e}")) for e in ("pe", "act", "dve")} for p in range(nphase)]
        self.phase = 0
        self.cnt = {e: 0 for e in self.ENG}
        self.seen = {e: {} for e in self.ENG}
        self.bufs = {}
        self.dsem = {}
        for q in ("sp", "pool"):
            self.dsem[q] = [[es.enter_context(nc.semaphore(f"d_{q}{i}")), 0] for i in range(ndma)]
        self.drr = {q: 0 for q in self.dsem}
        self.nins = 0

    def _deps(self, reads, writes):
        deps = {}

        def add(ev):
            k, s, v = ev
            if k not in deps or deps[k][1] < v:
                deps[k] = (s, v)
        for b in reads:
            st = self.bufs.get(b)
            if st and st[0]:
                add(st[0])
        for b in writes:
            st = self.bufs.get(b)
            if st:
                if st[0]:
                    add(st[0])
                for ev in st[1].values():
                    add(ev)
        return deps

    def _wait(self, e, deps):
        for k, (s, v) in deps.items():
            if e == "pe" and k == ("p", self.phase, "pe"):
                continue
            if self.seen[e].get(k, 0) < v:
                self.eng[e].wait_ge(s, v)
                self.seen[e][k] = v

    def _update(self, ev, reads, writes):
        for b in reads:
            st = self.bufs.setdefault(b, [None, {}])
            st[1][ev[0]] = ev
        for b in writes:
            self.bufs[b] = [ev, {}]

    def op(self, e, fn, reads=(), writes=()):
        self._wait(e, self._deps(reads, writes))
        ins = fn(self.eng[e])
        self.cnt[e] += 1
        sem = self.psems[self.phase][e]
        ins.then_inc(sem, 1)
        ev = (("p", self.phase, e), sem, self.cnt[e])
        self._update(ev, reads, writes)
        self.nins += 1
        return ev

    def dma(self, q, out, in_, reads=(), writes=()):
        self._wait(q, self._deps(reads, writes))
        i = self.drr[q]
        self.drr[q] = (i + 1) % len(self.dsem[q])
        ent = self.dsem[q][i]
        k = ("d", q, i)
        if ent[1] > 0 and self.seen[q].get(k, 0) < ent[1]:
            self.eng[q].wait_ge(ent[0], ent[1])
            self.seen[q][k] = ent[1]
        self.eng[q].dma_start(out=out, in_=in_).then_inc(ent[0], 16)
        ent[1] += 16
        ev = (k, ent[0], ent[1])
        self._update(ev, reads, writes)
        self.nins += 1
        return ev

    def barrier(self):
        for e in self.ENG:
            for e2 in ("pe", "act", "dve"):
                if self.cnt[e2] > 0:
                    self.eng[e].wait_ge(self.psems[self.phase][e2], self.cnt[e2])
            for q, lst in self.dsem.items():
                for i, ent in enumerate(lst):
                    if ent[1] > 0 and self.seen[e].get(("d", q, i), 0) < ent[1]:
                        self.eng[e].wait_ge(ent[0], ent[1])
                        self.seen[e][("d", q, i)] = ent[1]
        self.phase += 1
        assert self.phase < len(self.psems)
        self.cnt = {e: 0 for e in self.ENG}
        for e in self.ENG:
            self.seen[e] = {k: v for k, v in self.seen[e].items() if k[0] == "d"}
        self.bufs = {}


def build_program():
    nc = bass.Bass("TRN2", target_bir_lowering=False)
    es = ExitStack()

    def din(name, shape, dt=F32):
        return nc.dram_tensor(name, list(shape), dt, kind="ExternalInput").ap()

    def dout(name, shape, dt=F32):
        return nc.dram_tensor(name, list(shape), dt, kind="ExternalOutput").ap()

    def dscr(name, shape, dt=BF16):
        return nc.dram_tensor(name, list(shape), dt).ap()

    xT_own = din("xT_own", [V, D, NT])
    xT_all = din("xT_all", [D, SEQ])
    cT = din("cT", [128, KD * NB])
    w_ada = din("w_ada", [D, 6 * D])
    b_adaT = din("b_adaT", [128, 192])
    g1T = din("g1T", [128, KD])
    g2T = din("g2T", [128, KD])
    gfT = din("gfT", [128, KD])
    w_in = din("w_in", [D, D_IN])
    gq_rep = din("gq_rep", [128, 1024])
    gkv_rep = din("gkv_rep", [128, 512])
    w_uq = din("w_uq", [1024, 6144])
    w_uk = din("w_uk", [512, 4096])
    w_uv = din("w_uv", [512, 4096])
    w_out = din("w_out", [D, D])
    w_up = din("w_up", [D, 4 * D])
    w_down = din("w_down", [4 * D, D])
    cs_mla_own = din("cs_mla_own", [V, NT, 64])
    cs_dsa_own = din("cs_dsa_own", [V, NT, 32])
    cs_mla_all = din("cs_mla_all", [SEQ, 64])
    cs_dsa_all = din("cs_dsa_all", [SEQ, 32])
    c_ckvT = din("c_ckvT", [2 * V, 512, PAST])
    c_krT = din("c_krT", [2 * V, 64, PAST])
    c_kT = din("c_kT", [2 * V, 8, 128, PAST])
    c_v = din("c_v", [2 * V, PAST, 1024])
    c_kiT = din("c_kiT", [2 * V, 128, PAST])
    maskB_in = din("maskB", [128, V * 4])
    admA_in = din("admA", [128, V * 1024])
    ident_in = din("ident", [128, 128], BF16)
    ones_in = din("ones", [128, 128], BF16)
    yT = dout("yT", [V, D, NT])
    o_ckv = dout("o_ckv", [V, NT, 512])
    o_kr = dout("o_kr", [V, NT, 64])
    o_k = dout("o_k", [V, NT, 1024])
    o_v = dout("o_v", [V, NT, 1024])
    o_ki = dout("o_ki", [V, NT, 128])
    if DEBUG:
        dbg_mg = dout("dbg_mg", [V, 32, 128, NT], BF16)
    QBT = dscr("QBT", [V, 32, 128, NT])
    QIT = dscr("QIT", [V, 32, 128, NT])
    QNT = dscr("QNT", [V, 32, 128, NT])
    QRT = dscr("QRT", [V, 16, 128, NT])
    GAT = dscr("GAT", [V, 32, 128, NT])
    GBT = dscr("GBT", [V, 32, 128, NT])
    MAT = dscr("MAT", [V, 32, 128, NT])
    MGT = dscr("MGT", [V, 32, 128, NT])
    WKB = dscr("WKB", [128, KD, 2752])
    CKVT = dscr("CKVT", [4, 128, SEQ])
    KRT = dscr("KRT", [128, SEQ])
    KBT = dscr("KBT", [8, 128, SEQ])
    KIT = dscr("KIT", [128, SEQ])
    VB = dscr("VB", [SEQ, 1024])
    SCKVT = dscr("SCKVT", [2 * V, 4, 128, LS])
    SKRT = dscr("SKRT", [2 * V, 128, LS])
    SKBT = dscr("SKBT", [2 * V, 8, 128, LS])
    SKIT = dscr("SKIT", [2 * V, 128, LS])
    SVB = dscr("SVB", [2 * V, LS + 112, 1024])
    MASKT = dscr("MASKT", [V, 64, 128, NP])
    SMASKT = dscr("SMASKT", [2 * V, 9, 128, 16])
    X1T = dscr("X1T", [V, KD, 128, NT], F32)
    X2T = dscr("X2T", [V, KD, 128, NT], F32)

    S = Sched(nc, es)

    def sb(name, shape, dt=F32):
        return es.enter_context(nc.sbuf_tensor("sb_" + name, list(shape), dt))

    psbig = es.enter_context(nc.psum_tensor("psbig", [128, 4096], F32))
    ps = [psbig[:, i * 512:(i + 1) * 512] for i in range(8)]

    def psk(i):
        return ("ps", i)

    ident = sb("ident", [128, 128], BF16)
    ones = sb("ones", [128, 128], BF16)
    epsT = sb("epsT", [128, 1])
    adaT = sb("adaT", [128, 192, NB])
    A1 = sb("A1", [128, KD, NB])
    A2 = sb("A2", [128, KD, NB])
    gf = sb("gf", [128, KD])
    WI = sb("WI", [128, V, 9, 32])
    maskB = sb("maskB", [128, V, 4])
    admA = sb("admA", [128, V, 1024])
    S.dma("sp", ident[:], ident_in[:, :], writes=["ident"])
    S.dma("sp", ones[:], ones_in[:, :], writes=["ones"])
    S.dma("sp", gf[:], gfT[:, :], writes=["gf"])
    S.dma("sp", maskB[:].rearrange("p v r -> p (v r)"), maskB_in[:, :], writes=["maskB"])
    S.dma("sp", admA[:].rearrange("p v r -> p (v r)"), admA_in[:, :], writes=["admA"])
    S.op("dve", lambda e: e.memset(epsT[:], EPS), writes=["epsT"])
    onesf = sb("onesf", [128, 128])
    S.op("dve", lambda e: e.memset(onesf[:], 1.0), writes=["onesf"])

    def mm(bank, cols, pairs, reads):
        m = pairs[0][0].shape[-1] if len(pairs[0][0].shape) == 2 else None
        n = len(pairs)

        def fn(pe):
            ins = None
            for i, (l, r) in enumerate(pairs):
                mrows = l.shape[1]
                ins = pe.matmul(ps[bank][0:mrows, cols[0]:cols[1]], lhsT=l, rhs=r, start=(i == 0), stop=(i == n - 1))
            return ins
        return S.op("pe", fn, reads=reads, writes=[psk(bank)])

    def transpose_to(bank, col0, src, nrow, reads):
        pv = ps[bank][:].bitcast(BF16)

        def fn(pe):
            return pe.transpose(out=pv[0:src.shape[1], col0:col0 + nrow], in_=src, identity=ident[0:nrow, 0:nrow])
        return S.op("pe", fn, reads=list(reads) + ["ident"], writes=[psk(bank)])

    def psbf(bank):
        return ps[bank][:].bitcast(BF16)

    with ExitStack() as ph:
        def psb(name, shape, dt=F32):
            return ph.enter_context(nc.sbuf_tensor(name, list(shape), dt))
        c_sb = psb("c_sb", [128, KD * NB])
        sT = psb("sT", [128, KD, NB], BF16)
        bada = psb("bada", [128, 192])
        g1 = psb("g1", [128, KD])
        g2 = psb("g2", [128, KD])
        tmpA = psb("tmpA", [128, KD, NB])
        wp = [psb(f"wpa{i}", [128, KD, 512], BF16) for i in range(3)]
        S.dma("sp", c_sb[:], cT[:, :], writes=["c_sb"])
        S.dma("sp", bada[:], b_adaT[:, :], writes=["bada"])
        S.dma("sp", g1[:], g1T[:, :], writes=["g1"])
        S.dma("sp", g2[:], g2T[:, :], writes=["g2"])
        S.op("act", lambda e: e.activation(out=sT[:].rearrange("p k b -> p (k b)"), in_=c_sb[:], func=AF.Silu),
             reads=["c_sb"], writes=["sT"])
        wav = w_ada.rearrange("(k p) c -> p k c", p=128)
        NPA = 48
        for pn in range(min(3, NPA)):
            S.dma("pool", wp[pn % 3][:], wav[:, :, pn * 512:(pn + 1) * 512], writes=[("wpa", pn % 3)])
        for pn in range(NPA):
            buf = wp[pn % 3]
            for cc in range(4):
                j = pn * 4 + cc
                bank = j % 8
                mm(bank, (0, NB), [(buf[:, k, cc * 128:(cc + 1) * 128], sT[:, k, :]) for k in range(KD)],
                   reads=[("wpa", pn % 3), "sT"])
                S.op("dve", lambda e, j=j, bank=bank: e.tensor_scalar(
                    out=adaT[:, j, :], in0=ps[bank][:, 0:NB], scalar1=bada[:, j:j + 1], scalar2=None, op0=ALU.add),
                    reads=[psk(bank), "bada"], writes=[("adaT", j)])
            if pn + 3 < NPA:
                S.dma("pool", wp[pn % 3][:], wav[:, :, (pn + 3) * 512:(pn + 4) * 512], writes=[("wpa", pn % 3)])
        for (A, g, gname, off) in ((A1, g1, "g1", 32), (A2, g2, "g2", 128)):
            S.op("dve", lambda e, off=off: e.tensor_scalar(out=tmpA[:], in0=adaT[:, off:off + 32, :], scalar1=1.0,
                                                          scalar2=None, op0=ALU.add),
                 reads=[("adaT", j) for j in range(off, off + 32)], writes=["tmpA"])
            S.op("dve", lambda e, A=A, g=g: e.tensor_tensor(out=A[:], in0=tmpA[:],
                                                            in1=g[:].unsqueeze(2).to_broadcast([128, KD, NB]), op=ALU.mult),
                 reads=["tmpA", gname], writes=["Amod"])
        S.barrier()
    SH1, GT1, SH2, GT2 = 0, 64, 96, 160

    def bcols(v):
        return [(0, 1024, 0), (1024, 1040, 1 + 2 * v), (1040, 1056, 2 + 2 * v)]

    def norm_phase(src, A, Boff, hT, final=False, tag="n", v=0):
        with ExitStack() as ph:
            def psb(name, shape, dt=F32):
                return ph.enter_context(nc.sbuf_tensor(f"{tag}_{name}", list(shape), dt))
            xb = [psb(f"xb{i}", [128, NT]) for i in range(3)]
            sq = [psb(f"sq{i}", [128, NT], BF16) for i in range(2)]
            rstd = psb("rstd", [128, NT])
            tmp = [psb(f"tmp{i}", [128, NT]) for i in range(2)]
            for k in range(KD):
                S.dma("sp", xb[k % 3][:], src[k], writes=[("xb", k % 3)])
                S.op("act", lambda e, k=k: e.activation(out=sq[k % 2][:], in_=xb[k % 3][:], func=AF.Square),
                     reads=[("xb", k % 3)], writes=[("sq", k % 2)])

                def fn(pe, k=k):
                    ins = None
                    for gi, (c0, n) in enumerate(TGRP):
                        ins = pe.matmul(ps[gi][:, 0:n], lhsT=ones[:, :], rhs=sq[k % 2][:, c0:c0 + n],
                                        start=(k == 0), stop=(k == KD - 1))
                    return ins
                S.op("pe", fn, reads=[("sq", k % 2), "ones"], writes=[psk(0), psk(1), psk(2)])
            for gi, (c0, n) in enumerate(TGRP):
                S.op("act", lambda e, gi=gi, c0=c0, n=n: e.activation(
                    out=rstd[:, c0:c0 + n], in_=ps[gi][:, 0:n], func=AF.Sqrt, bias=epsT[:, 0:1], scale=1.0 / D),
                    reads=[psk(gi), "epsT"], writes=[("rstd", gi)])
                S.op("dve", lambda e, c0=c0, n=n: e.reciprocal(out=rstd[:, c0:c0 + n], in_=rstd[:, c0:c0 + n]),
                     reads=[("rstd", gi)], writes=[("rstd", gi)])
            rk = [("rstd", gi) for gi in range(3)]
            for k in range(KD):
                S.dma("sp", xb[k % 3][:], src[k], writes=[("xb", k % 3)])
                if final:
                    S.op("dve", lambda e, k=k: e.scalar_tensor_tensor(
                        out=tmp[k % 2][:], in0=xb[k % 3][:], scalar=gf[:, k:k + 1], in1=rstd[:],
                        op0=ALU.mult, op1=ALU.mult), reads=[("xb", k % 3), "gf"] + rk, writes=[("tmp", k % 2)])
                    S.dma("sp", yT[v, k * 128:(k + 1) * 128, :], tmp[k % 2][:], reads=[("tmp", k % 2)], writes=[("yT", k)])
                else:
                    S.op("dve", lambda e, k=k: e.tensor_tensor(out=tmp[k % 2][:], in0=xb[k % 3][:], in1=rstd[:], op=ALU.mult),
                         reads=[("xb", k % 3)] + rk, writes=[("tmp", k % 2)])
                    for (c0, c1, b) in bcols(v):
                        S.op("act", lambda e, k=k, c0=c0, c1=c1, b=b: e.activation(
                            out=hT[:, k, c0:c1], in_=tmp[k % 2][:, c0:c1], func=AF.Identity,
                            bias=adaT[:, Boff + k, b:b + 1], scale=A[:, k, b:b + 1]),
                            reads=[("tmp", k % 2)], writes=[("hT", k)])
            S.barrier()

    def rope_inplace(e_tag, z3, n, H, half, cs, tmps, zkey, cskey="cs"):
        cb = cs[0:n, 0:half].unsqueeze(1).to_broadcast([n, H, half])
        sn = cs[0:n, half:2 * half].unsqueeze(1).to_broadcast([n, H, half])
        x1 = z3[:, :, 0:half]
        x2 = z3[:, :, half:2 * half]
        t = [tm[0:n, 0:H * half].rearrange("p (h d) -> p h d", d=half) for tm in tmps]
        tk = [("ropetmp", e_tag, i) for i in range(4)]
        S.op("dve", lambda e: e.tensor_tensor(out=t[0], in0=x1, in1=cb, op=ALU.mult), reads=[zkey, cskey], writes=[tk[0]])
        S.op("dve", lambda e: e.tensor_tensor(out=t[1], in0=x2, in1=sn, op=ALU.mult), reads=[zkey, cskey], writes=[tk[1]])
        S.op("dve", lambda e: e.tensor_tensor(out=t[2], in0=x2, in1=cb, op=ALU.mult), reads=[zkey, cskey], writes=[tk[2]])
        S.op("dve", lambda e: e.tensor_tensor(out=t[3], in0=x1, in1=sn, op=ALU.mult), reads=[zkey, cskey], writes=[tk[3]])
        S.op("dve", lambda e: e.tensor_tensor(out=x1, in0=t[0], in1=t[1], op=ALU.subtract), reads=[tk[0], tk[1]], writes=[zkey])
        S.op("dve", lambda e: e.tensor_tensor(out=x2, in0=t[2], in1=t[3], op=ALU.add), reads=[tk[2], tk[3]], writes=[zkey])

    def rms_rows(z, n, width, grep, out, zkey, gkey, ss, outkey, tag):
        S.op("act", lambda e: e.activation(out=ss[1][0:n, 0:width], in_=z, func=AF.Square, accum_out=ss[0][0:n, 0:1]),
             reads=[zkey], writes=[("ss", tag)])
        S.op("act", lambda e: e.activation(out=ss[0][0:n, 0:1], in_=ss[0][0:n, 0:1], func=AF.Sqrt,
                                           bias=epsT[0:n, 0:1], scale=1.0 / width),
             reads=[("ss", tag), "epsT"], writes=[("ss", tag)])
        S.op("dve", lambda e: e.reciprocal(out=ss[0][0:n, 0:1], in_=ss[0][0:n, 0:1]), reads=[("ss", tag)], writes=[("ss", tag)])
        S.op("dve", lambda e: e.scalar_tensor_tensor(out=out, in0=z, scalar=ss[0][0:n, 0:1], in1=grep[0:n, 0:width],
                                                     op0=ALU.mult, op1=ALU.mult),
             reads=[zkey, ("ss", tag), gkey], writes=[outkey])

    for v in range(V):
        ph12 = ExitStack()
        hT = ph12.enter_context(nc.sbuf_tensor(f"hT{v}", [128, KD, NT], BF16))
        qlnT = ph12.enter_context(nc.sbuf_tensor(f"qlnT{v}", [128, 8, NT], BF16))
        norm_phase(xT_own[v].rearrange("(k p) t -> k p t", p=128), A1, SH1, hT, tag=f"n1{v}", v=v)
        hkeys = [("hT", k) for k in range(KD)]

        with ExitStack() as ph:
            def psb(name, shape, dt=F32):
                return ph.enter_context(nc.sbuf_tensor(f"p2{v}_{name}", list(shape), dt))
            wp = [psb(f"wp{i}", [128, KD, 512], BF16) for i in range(2)]
            zt = [psb(f"z{i}", [128, 512]) for i in range(3)]
            zb = [psb(f"zb{i}", [128, 512], BF16) for i in range(2)]
            rt = [psb(f"rt{i}", [128, 128]) for i in range(4)]
            ssq = [psb("ss0", [128, 1]), psb("ssj", [128, 1024], BF16)]
            ql = psb("ql", [128, 1024])
            stg = [psb(f"stg{i}", [128, 4, NT], BF16) for i in range(2)]
            csm = psb("csm", [128, 9, 64])
            csd = psb("csd", [128, 9, 32])
            gq = psb("gq", [128, 1024])
            gkv = psb("gkv", [128, 512])
            sstg = psb("sstg", [128, 14, 32], BF16)
            S.dma("sp", gq[:], gq_rep[:, :], writes=["gq"])
            S.dma("sp", gkv[:], gkv_rep[:, :], writes=["gkv"])
            for ti, (t0, n) in enumerate(TT):
                S.dma("sp", csm[0:n, ti, :], cs_mla_own[v, t0:t0 + n, :], writes=["cs"])
                S.dma("sp", csd[0:n, ti, :], cs_dsa_own[v, t0:t0 + n, :], writes=["cs"])
            wv = w_in.rearrange("(k p) c -> p k c", p=128)
            cnt = {"pn": 0, "z": 0, "bank": 0, "tb": 0, "stg": 0, "zb": 0}

            def load_panel(c0, w):
                i = cnt["pn"] % 2
                cnt["pn"] += 1
                S.dma("pool", wp[i][:, :, 0:w], wv[:, :, c0:c0 + w], writes=[("wp", i)])
                return i

            def gemm_tok(pi, w, ti, kk=KD, lhs=None, lkeys=None):
                t0, n = TT[ti]
                bank = cnt["bank"] % 4
                cnt["bank"] += 1
                src = hT if lhs is None else lhs
                mm(bank, (0, w), [(src[:, k, t0:t0 + n], wp[pi][:, k, 0:w]) for k in range(kk)],
                   reads=[("wp", pi)] + (hkeys if lkeys is None else lkeys))
                flush()
                return bank

            pending = []

            def flush():
                while pending:
                    pending.pop(0)()

            def defer(fn):
                pending.append(fn)

            def evac_z(bank, n, w):
                zi = cnt["z"] % 3
                cnt["z"] += 1
                S.op("act", lambda e: e.activation(out=zt[zi][0:n, 0:w], in_=ps[bank][0:n, 0:w], func=AF.Copy),
                     reads=[psk(bank)], writes=[("z", zi)])
                return zi

            def to_bf(zi, n, w):
                bi = cnt["zb"] % 2
                cnt["zb"] += 1
                S.op("act", lambda e: e.activation(out=zb[bi][0:n, 0:w], in_=zt[zi][0:n, 0:w], func=AF.Copy),
                     reads=[("z", zi)], writes=[("zb", bi)])
                return bi

            def tr_blocks(*args):
                defer(lambda: tr_blocks_now(*args))

            def tr_blocks_now(src_tile, skey, n, blocks, dst_fn, dkey):
                for bi_, (c0, wdt) in enumerate(blocks):
                    tb = 4 + cnt["tb"] % 4
                    cnt["tb"] += 1
                    transpose_to(tb, 0, src_tile[0:n, c0:c0 + wdt], n, reads=[skey])
                    S.op("dve", lambda e, tb=tb, bi_=bi_, wdt=wdt: e.tensor_copy(out=dst_fn(bi_), in_=psbf(tb)[0:wdt, 0:n]),
                         reads=[psk(tb)], writes=[dkey])

            qlb = psb("qlb", [128, 1024], BF16)
            pis = [load_panel(O_QL + pnl * 512, 512) for pnl in range(2)]
            for ti, (t0, n) in enumerate(TT):
                for pnl in range(2):
                    bank = gemm_tok(pis[pnl], 512, ti)
                    S.op("act", lambda e, bank=bank, n=n, pnl=pnl: e.activation(
                        out=ql[0:n, pnl * 512:(pnl + 1) * 512], in_=ps[bank][0:n, 0:512], func=AF.Copy),
                        reads=[psk(bank)], writes=[("ql", pnl)])
                S.op("act", lambda e, n=n: e.activation(out=ssq[1][0:n, :], in_=ql[0:n, :], func=AF.Square,
                                                        accum_out=ssq[0][0:n, 0:1]),
                     reads=[("ql", 0), ("ql", 1)], writes=["ssq"])
                S.op("act", lambda e, n=n: e.activation(out=ssq[0][0:n, 0:1], in_=ssq[0][0:n, 0:1], func=AF.Sqrt,
                                                        bias=epsT[0:n, 0:1], scale=1.0 / 1024), reads=["ssq", "epsT"], writes=["ssq"])
                S.op("dve", lambda e, n=n: e.reciprocal(out=ssq[0][0:n, 0:1], in_=ssq[0][0:n, 0:1]), reads=["ssq"], writes=["ssq"])
                S.op("dve", lambda e, n=n: e.scalar_tensor_tensor(
                    out=qlb[0:n, :], in0=ql[0:n, :], scalar=ssq[0][0:n, 0:1], in1=gq[0:n, :], op0=ALU.mult, op1=ALU.mult),
                    reads=["ssq", "gq", ("ql", 0), ("ql", 1)], writes=["qlb"])
                tr_blocks(qlb, "qlb", n, [(c * 128, 128) for c in range(8)],
                          lambda bi_, t0=t0, n=n: qlnT[:, bi_, t0:t0 + n], ("qlnT", ti))
            qkeys = [("qlnT", ti) for ti in range(9)]

            pi = load_panel(O_CKV, 512)
            for ti, (t0, n) in enumerate(TT):
                bank = gemm_tok(pi, 512, ti)
                zi = evac_z(bank, n, 512)
                zo = (zi + 1) % 3
                cnt["z"] += 1
                rms_rows(zt[zi][0:n, :], n, 512, gkv, zt[zo][0:n, :], ("z", zi), "gkv", ssq, ("z", zo), "ckv")
                S.dma("sp", o_ckv[v, t0:t0 + n, :], zt[zo][0:n, :], reads=[("z", zo)], writes=[("o_ckv", ti)])
                if ti == 8:
                    bi = to_bf(zo, n, 512)
                    tr_blocks(zb[bi], ("zb", bi), n, [(c * 128, 128) for c in range(4)],
                              lambda bi_: sstg[:, bi_, 0:32], "sstg")
            pi = load_panel(O_KR, 64)
            for ti, (t0, n) in enumerate(TT):
                bank = gemm_tok(pi, 64, ti)
                zi = evac_z(bank, n, 64)
                rope_inplace("a", zt[zi][0:n, 0:64].rearrange("p (h d) -> p h d", h=1), n, 1, 32, csm[:, ti, :], rt, ("z", zi))
                S.dma("sp", o_kr[v, t0:t0 + n, :], zt[zi][0:n, 0:64], reads=[("z", zi)], writes=[("o_kr", ti)])
                if ti == 8:
                    bi = cnt["zb"] % 2
                    cnt["zb"] += 1
                    for hh in range(2):
                        S.op("act", lambda e, hh=hh, bi=bi, zi=zi, n=n: e.activation(
                            out=zb[bi][0:n, hh * 64:(hh + 1) * 64], in_=zt[zi][0:n, 0:64], func=AF.Copy),
                            reads=[("z", zi)], writes=[("zb", bi)])
                    tr_blocks(zb[bi], ("zb", bi), n, [(0, 128)], lambda bi_: sstg[:, 4, 0:32], "sstg")
            for pnl in range(2):
                pi = load_panel(O_KB + pnl * 512, 512)
                for ti, (t0, n) in enumerate(TT):
                    bank = gemm_tok(pi, 512, ti)
                    zi = evac_z(bank, n, 512)
                    rope_inplace("a", zt[zi][0:n, :].rearrange("p (h d) -> p h d", h=4), n, 4, 16, csd[:, ti, :], rt, ("z", zi))
                    S.dma("sp", o_k[v, t0:t0 + n, pnl * 512:(pnl + 1) * 512], zt[zi][0:n, :], reads=[("z", zi)],
                          writes=[("o_k", ti, pnl)])
                    if ti == 8:
                        bi = to_bf(zi, n, 512)
                        tr_blocks(zb[bi], ("zb", bi), n, [(c * 128, 128) for c in range(4)],
                                  lambda bi_, pnl=pnl: sstg[:, 5 + pnl * 4 + bi_, 0:32], "sstg")
            for pnl in range(2):
                pi = load_panel(O_VB + pnl * 512, 512)
                for ti, (t0, n) in enumerate(TT):
                    bank = gemm_tok(pi, 512, ti)
                    zi = evac_z(bank, n, 512)
                    S.dma("sp", o_v[v, t0:t0 + n, pnl * 512:(pnl + 1) * 512], zt[zi][0:n, :], reads=[("z", zi)],
                          writes=[("o_v", ti, pnl)])
                    if ti == 8:
                        bi = to_bf(zi, n, 512)
                        for b in range(2):
                            S.dma("sp", SVB[2 * v + b, PAST:PAST + 16, pnl * 512:(pnl + 1) * 512], zb[bi][b * 16:(b + 1) * 16, :],
                                  reads=[("zb", bi)], writes=[("SVBn", b, pnl)])
            pi = load_panel(O_KI, 128)
            for ti, (t0, n) in enumerate(TT):
                bank = gemm_tok(pi, 128, ti)
                zi = evac_z(bank, n, 128)
                rope_inplace("a", zt[zi][0:n, 0:128].rearrange("p (h d) -> p h d", h=1), n, 1, 16, csd[:, ti, :], rt, ("z", zi))
                S.dma("sp", o_ki[v, t0:t0 + n, :], zt[zi][0:n, 0:128], reads=[("z", zi)], writes=[("o_ki", ti)])
                if ti == 8:
                    bi = to_bf(zi, n, 128)
                    tr_blocks(zb[bi], ("zb", bi), n, [(0, 128)], lambda bi_: sstg[:, 13, 0:32], "sstg")
            flush()
            for b in range(2):
                S.dma("sp", SCKVT[2 * v + b].rearrange("c p l -> p c l")[:, :, PAST:LS], sstg[:, 0:4, b * 16:(b + 1) * 16],
                      reads=["sstg"], writes=[("SCKVTn", b)])
                S.dma("sp", SKRT[2 * v + b][:, PAST:LS], sstg[:, 4, b * 16:(b + 1) * 16], reads=["sstg"], writes=[("SKRTn", b)])
                S.dma("sp", SKBT[2 * v + b].rearrange("c p l -> p c l")[:, :, PAST:LS], sstg[:, 5:13, b * 16:(b + 1) * 16],
                      reads=["sstg"], writes=[("SKBTn", b)])
                S.dma("sp", SKIT[2 * v + b][:, PAST:LS], sstg[:, 13, b * 16:(b + 1) * 16], reads=["sstg"], writes=[("SKITn", b)])
            pi = load_panel(O_WI, 32)
            for ti, (t0, n) in enumerate(TT):
                bank = gemm_tok(pi, 32, ti)
                S.op("act", lambda e, bank=bank, ti=ti, n=n: e.activation(out=WI[0:n, v, ti, :], in_=ps[bank][0:n, 0:32],
                                                                        func=AF.Copy, scale=IDX_W_SCALE),
                     reads=[psk(bank)], writes=[("WI", ti)])
            for (off, dst, dname) in ((O_QB, QBT[v], "QBT"), (O_QI, QIT[v], "QIT")):
                for pnl in range(8):
                    pi = load_panel(off + pnl * 512, 512)
                    si = cnt["stg"] % 2
                    cnt["stg"] += 1
                    for ti, (t0, n) in enumerate(TT):
                        bank = gemm_tok(pi, 512, ti)
                        zi = evac_z(bank, n, 512)
                        rope_inplace("a", zt[zi][0:n, :].rearrange("p (h d) -> p h d", h=4), n, 4, 16, csd[:, ti, :], rt, ("z", zi))
                        bi = to_bf(zi, n, 512)
                        tr_blocks(zb[bi], ("zb", bi), n, [(c * 128, 128) for c in range(4)],
                                  lambda bi_, si=si, t0=t0, n=n: stg[si][:, bi_, t0:t0 + n], ("stg", si))
                    defer(lambda dst=dst, pnl=pnl, si=si, dname=dname: S.dma(
                        "sp", dst[pnl * 4:(pnl + 1) * 4].rearrange("h p t -> p h t"), stg[si][:],
                        reads=[("stg", si)], writes=[(dname, pnl)]))
            flush()
            for (off, dst, dname) in ((O_GA, GAT[v], "GAT"), (O_GB, GBT[v], "GBT")):
                for pnl in range(8):
                    pi = load_panel(off + pnl * 512, 512)
                    si = cnt["stg"] % 2
                    cnt["stg"] += 1
                    for cc in range(4):
                        for gi, (c0, n) in enumerate(TGRP):
                            bank = cnt["bank"] % 4
                            cnt["bank"] += 1
                            mm(bank, (0, n), [(wp[pi][:, k, cc * 128:(cc + 1) * 128], hT[:, k, c0:c0 + n]) for k in range(KD)],
                               reads=[("wp", pi)] + hkeys)
                            S.op("act", lambda e, bank=bank, si=si, cc=cc, c0=c0, n=n: e.activation(
                                out=stg[si][:, cc, c0:c0 + n], in_=ps[bank][:, 0:n], func=AF.Sigmoid),
                                reads=[psk(bank)], writes=[("stg", si)])
                    S.dma("sp", dst[pnl * 4:(pnl + 1) * 4].rearrange("h p t -> p h t"), stg[si][:],
                          reads=[("stg", si)], writes=[(dname, pnl)])
            flush()
            wq = w_uq.rearrange("(k p) c -> p k c", p=128)
            for pr in range(16):
                i = cnt["pn"] % 2
                cnt["pn"] += 1
                S.dma("pool", wp[i][:, 0:8, 0:384], wq[:, :, pr * 384:(pr + 1) * 384], writes=[("wp", i)])
                si = cnt["stg"] % 2
                cnt["stg"] += 1
                for ti, (t0, n) in enumerate(TT):
                    bank = gemm_tok(i, 384, ti, kk=8, lhs=qlnT, lkeys=qkeys)
                    zi = evac_z(bank, n, 384)
                    rope_inplace("a", zt[zi][0:n, 0:384].rearrange("p (h d) -> p h d", h=2)[:, :, 128:192], n, 2, 32,
                                 csm[:, ti, :], rt, ("z", zi))
                    bi = cnt["zb"] % 2
                    cnt["zb"] += 1
                    for (d0, s0, wd) in ((0, 0, 128), (128, 192, 128), (256, 128, 64), (320, 320, 64)):
                        S.op("act", lambda e, d0=d0, s0=s0, wd=wd, bi=bi, zi=zi, n=n: e.activation(
                            out=zb[bi][0:n, d0:d0 + wd], in_=zt[zi][0:n, s0:s0 + wd], func=AF.Copy),
                            reads=[("z", zi)], writes=[("zb", bi)])
                    tr_blocks(zb[bi], ("zb", bi), n, [(0, 128), (128, 128), (256, 128)],
                              lambda bi_, si=si, t0=t0, n=n: stg[si][:, bi_, t0:t0 + n], ("stg", si))
                defer(lambda pr=pr, si=si: S.dma("sp", QNT[v, pr * 2:(pr + 1) * 2].rearrange("h p t -> p h t"), stg[si][:, 0:2, :],
                                                 reads=[("stg", si)], writes=[("QNT", pr)]))
                defer(lambda pr=pr, si=si: S.dma("sp", QRT[v, pr], stg[si][:, 2, :], reads=[("stg", si)], writes=[("QRT", pr)]))
            flush()
            S.barrier()
        ph12.close()

    KOFF = [(O_CKV, 512), (O_KR, 64), (O_KB, 512), (O_KB + 512, 512), (O_VB, 512), (O_VB + 512, 512), (O_KI, 128)]
    KPOS = []
    acc_ = 0
    for (o, w) in KOFF:
        KPOS.append(acc_)
        acc_ += w
    assert acc_ == 2752
    with ExitStack() as ph:
        def psb(name, shape, dt=F32):
            return ph.enter_context(nc.sbuf_tensor(f"p3_{name}", list(shape), dt))
        G = 512
        NTI = G // 128
        hg = psb("hg", [128, KD, G], BF16)
        wp = [psb(f"wp{i}", [128, KD, 512], BF16) for i in range(2)]
        sq = [psb(f"sq{i}", [128, G], BF16) for i in range(2)]
        rstd = psb("rstd", [128, G])
        tmp = [psb(f"tmp{i}", [128, G]) for i in range(2)]
        zt = [psb(f"z{i}", [128, 512]) for i in range(3)]
        zb = [psb(f"zb{i}", [128, 512], BF16) for i in range(2)]
        rt = [psb(f"rt{i}", [128, 128]) for i in range(4)]
        ssq = [psb("ss0", [128, 1]), psb("ssj", [128, 512])]
        gkv = psb("gkv", [128, 512])
        csm = psb("csm", [128, 4, 64])
        csd = psb("csd", [128, 4, 32])
        tstg = psb("tstg", [128, 14, 1024], BF16)
        vstg = psb("vstg", [128, 8, 1024], BF16)
        S.dma("sp", gkv[:], gkv_rep[:, :], writes=["gkv"])
        wv = w_in.rearrange("(k p) c -> p k c", p=128)
        for pi_, (o, w) in enumerate(KOFF):
            S.dma("pool", wp[pi_ % 2][:, :, 0:w], wv[:, :, o:o + w], writes=[("wp", pi_ % 2)])
            S.dma("sp", WKB[:, :, KPOS[pi_]:KPOS[pi_] + w], wp[pi_ % 2][:, :, 0:w], reads=[("wp", pi_ % 2)], writes=[("WKB", pi_)])
        for b in range(2 * V):
            S.dma("pool", tstg[:, 0:4, 0:PAST], c_ckvT[b].rearrange("(c p) l -> p c l", p=128), writes=["tstg"])
            S.dma("sp", SCKVT[b].rearrange("c p l -> p c l")[:, :, 0:PAST], tstg[:, 0:4, 0:PAST], reads=["tstg"], writes=[("SCKVTc", b)])
            for hh in range(2):
                S.dma("pool", tstg[hh * 64:(hh + 1) * 64, 4, 0:PAST], c_krT[b], writes=["tstg"])
            S.dma("sp", SKRT[b][:, 0:PAST], tstg[:, 4, 0:PAST], reads=["tstg"], writes=[("SKRTc", b)])
            S.dma("pool", tstg[:, 5:13, 0:PAST], c_kT[b].rearrange("g p l -> p g l"), writes=["tstg"])
            S.dma("sp", SKBT[b].rearrange("c p l -> p c l")[:, :, 0:PAST], tstg[:, 5:13, 0:PAST], reads=["tstg"], writes=[("SKBTc", b)])
            S.dma("pool", tstg[:, 13, 0:PAST], c_kiT[b], writes=["tstg"])
            S.dma("sp", SKIT[b][:, 0:PAST], tstg[:, 13, 0:PAST], reads=["tstg"], writes=[("SKITc", b)])
            S.dma("pool", vstg[:], c_v[b].rearrange("(t p) d -> p t d", p=128), writes=["vstg"])
            S.dma("sp", SVB[b, 0:PAST, :].rearrange("(t p) d -> p t d", p=128), vstg[:], reads=["vstg"], writes=[("SVBc", b)])
        S.barrier()
        xav = xT_all.rearrange("(k p) t -> p k t", p=128)
        cnt = {"z": 0, "bank": 0, "tb": 0, "zb": 0, "pn": 0}
        pend3 = []
        for g in range(SEQ // G):
            gc = g * G
            for q4 in range(4):
                S.dma("pool", hg[:, q4 * 8:(q4 + 1) * 8, :], xav[:, q4 * 8:(q4 + 1) * 8, gc:gc + G],
                      writes=[("hg", k) for k in range(q4 * 8, q4 * 8 + 8)])
            for ti in range(NTI):
                S.dma("sp", csm[:, ti, :], cs_mla_all[gc + ti * 128:gc + (ti + 1) * 128, :], writes=[("csm", ti)])
                S.dma("sp", csd[:, ti, :], cs_dsa_all[gc + ti * 128:gc + (ti + 1) * 128, :], writes=[("csd", ti)])
            for k in range(KD):
                S.op("act", lambda e, k=k: e.activation(out=sq[k % 2][:], in_=hg[:, k, :], func=AF.Square),
                     reads=[("hg", k)], writes=[("sq", k % 2)])
                S.op("pe", lambda pe, k=k: pe.matmul(ps[0][:, 0:G], lhsT=ones[:, :], rhs=sq[k % 2][:, :],
                                                     start=(k == 0), stop=(k == KD - 1)),
                     reads=[("sq", k % 2), "ones"], writes=[psk(0)])
            S.op("act", lambda e: e.activation(out=rstd[:, :], in_=ps[0][:, 0:G], func=AF.Sqrt, bias=epsT[:, 0:1], scale=1.0 / D),
                 reads=[psk(0), "epsT"], writes=["rstd"])
            S.op("dve", lambda e: e.reciprocal(out=rstd[:, :], in_=rstd[:, :]), reads=["rstd"], writes=["rstd"])
            for k in range(KD):
                S.op("dve", lambda e, k=k: e.tensor_tensor(out=tmp[k % 2][:], in0=hg[:, k, :], in1=rstd[:], op=ALU.mult),
                     reads=[("hg", k), "rstd"], writes=[("tmp", k % 2)])
                S.op("act", lambda e, k=k: e.activation(out=hg[:, k, :], in_=tmp[k % 2][:], func=AF.Identity,
                                                        bias=adaT[:, SH1 + k, 0:1], scale=A1[:, k, 0:1]),
                     reads=[("tmp", k % 2)], writes=[("hg", k)])
            hgk = [("hg", k) for k in range(KD)]
            for pi_, (o, w) in enumerate(KOFF):
                i = cnt["pn"] % 2
                cnt["pn"] += 1
                S.dma("pool", wp[i][:, :, 0:w], WKB[:, :, KPOS[pi_]:KPOS[pi_] + w],
                      reads=[("WKB", p_) for p_ in range(len(KOFF))], writes=[("wp", i)])
                for ti in range(NTI):
                    t0 = ti * 128
                    bank = 1 + cnt["bank"] % 3
                    cnt["bank"] += 1
                    mm(bank, (0, w), [(hg[:, k, t0:t0 + 128], wp[i][:, k, 0:w]) for k in range(KD)], reads=[("wp", i)] + hgk)
                    while pend3:
                        pend3.pop(0)()
                    zi = cnt["z"] % 3
                    cnt["z"] += 1
                    S.op("act", lambda e, zi=zi, bank=bank, w=w: e.activation(out=zt[zi][:, 0:w], in_=ps[bank][:, 0:w], func=AF.Copy),
                         reads=[psk(bank)], writes=[("z", zi)])
                    blocks = None
                    if pi_ == 0:
                        zo = (zi + 1) % 3
                        cnt["z"] += 1
                        rms_rows(zt[zi][:, :], 128, 512, gkv, zt[zo][:, :], ("z", zi), "gkv", ssq, ("z", zo), "ckv")
                        zi = zo
                        blocks, sbase = [(c * 128, 128) for c in range(4)], 0
                    elif pi_ == 1:
                        rope_inplace("a", zt[zi][:, 0:64].rearrange("p (h d) -> p h d", h=1), 128, 1, 32, csm[:, ti, :], rt, ("z", zi), ("csm", ti))
                    elif pi_ in (2, 3):
                        rope_inplace("a", zt[zi][:, :].rearrange("p (h d) -> p h d", h=4), 128, 4, 16, csd[:, ti, :], rt, ("z", zi), ("csd", ti))
                        blocks, sbase = [(c * 128, 128) for c in range(4)], 5 + (pi_ - 2) * 4
                    elif pi_ == 6:
                        rope_inplace("a", zt[zi][:, 0:128].rearrange("p (h d) -> p h d", h=1), 128, 1, 16, csd[:, ti, :], rt, ("z", zi), ("csd", ti))
                        blocks, sbase = [(0, 128)], 13
                    if pi_ in (4, 5):
                        S.op("act", lambda e, zi=zi, ti=ti, pi_=pi_: e.activation(
                            out=vstg[:, ti, (pi_ - 4) * 512:(pi_ - 3) * 512], in_=zt[zi][:, :], func=AF.Copy),
                            reads=[("z", zi)], writes=[("vstg", ti, pi_)])
                        continue
                    bi = cnt["zb"] % 2
                    cnt["zb"] += 1
                    if pi_ == 1:
                        for hh in range(2):
                            S.op("act", lambda e, hh=hh, bi=bi, zi=zi: e.activation(
                                out=zb[bi][:, hh * 64:(hh + 1) * 64], in_=zt[zi][:, 0:64], func=AF.Copy),
                                reads=[("z", zi)], writes=[("zb", bi)])
                        blocks, sbase = [(0, 128)], 4
                    else:
                        S.op("act", lambda e, bi=bi, zi=zi, w=w: e.activation(out=zb[bi][:, 0:w], in_=zt[zi][:, 0:w], func=AF.Copy),
                             reads=[("z", zi)], writes=[("zb", bi)])
                    def trs(blocks=blocks, bi=bi, sbase=sbase, t0=t0, ti=ti):
                        for bi_, (c0, wdt) in enumerate(blocks):
                            tb = 4 + cnt["tb"] % 4
                            cnt["tb"] += 1
                            transpose_to(tb, 0, zb[bi][:, c0:c0 + wdt], 128, reads=[("zb", bi)])
                            S.op("dve", lambda e, tb=tb, bi_=bi_: e.tensor_copy(
                                out=tstg[:, sbase + bi_, t0:t0 + 128], in_=psbf(tb)[:, 0:128]),
                                reads=[psk(tb)], writes=[("tstg", sbase + bi_, ti)])
                    pend3.append(trs)
            while pend3:
                pend3.pop(0)()
            tk = lambda lo, hi: [("tstg", s_, ti_) for s_ in range(lo, hi) for ti_ in range(NTI)]
            S.dma("sp", CKVT.rearrange("c p l -> p c l")[:, :, gc:gc + G], tstg[:, 0:4, 0:G], reads=tk(0, 4), writes=[("CKVT", g)])
            S.dma("sp", KRT[:, gc:gc + G], tstg[:, 4, 0:G], reads=tk(4, 5), writes=[("KRT", g)])
            S.dma("sp", KBT.rearrange("c p l -> p c l")[:, :, gc:gc + G], tstg[:, 5:13, 0:G], reads=tk(5, 13), writes=[("KBT", g)])
            S.dma("sp", KIT[:, gc:gc + G], tstg[:, 13, 0:G], reads=tk(13, 14), writes=[("KIT", g)])
            S.dma("sp", VB[gc:gc + G, :].rearrange("(t p) d -> p t d", p=128), vstg[:, 0:NTI, :],
                  reads=[("vstg", ti_, p_) for ti_ in range(NTI) for p_ in (4, 5)], writes=[("VB", g)])
        S.barrier()

    with ExitStack() as ph:
        def psb(name, shape, dt=F32):
            return ph.enter_context(nc.sbuf_tensor(f"p4_{name}", list(shape), dt))
        kit = psb("kit", [128, SEQ], BF16)
        skit = psb("skit", [128, 2 * V, LS], BF16)
        qi = [psb(f"qi{i}", [128, 32, 128], BF16) for i in range(2)]
        SAs = [psb("SA0", [128, SEQ]), psb("SA1", [128, SEQ])]
        Wk = psb("Wk", [128, SEQ])
        mk = psb("mk", [128, SEQ], BF16)
        rb = [psb(f"rb{i}", [128, 512], BF16) for i in range(4)]
        dg = [psb(f"dg{i}", [128, 32, 128], BF16) for i in range(2)]
        m8 = psb("m8", [128, 8])
        thr = psb("thr", [128, 1])
        mstg = psb("mstg", [128, 64, 128], BF16)
        S.dma("sp", kit[:], KIT[:, :], writes=["kit"])
        for b in range(2 * V):
            S.dma("sp", skit[:, b, :], SKIT[b], writes=["kit"])
        cnt = {"bank": 0, "rr": 0, "tb": 0, "ab": 0, "tile": 0}

        def index_tile(v, j, qcols, nq, keys_ap_fn, L, wi_ap, adm, out_fn, wkey):
            slot = cnt["tile"] % 2
            cnt["tile"] += 1
            SA = SAs[slot]
            qb_ = qi[j % 2]
            S.dma("sp", qb_[:, :, 0:nq], QIT[v].rearrange("h p t -> p h t")[:, :, qcols:qcols + nq], writes=[("qi", j % 2)])
            blocks = [(c0, min(512, L - c0)) for c0 in range(0, L, 512)]
            dgt = dg[j % 2]
            for h in range(32):
                S.op("dve", lambda e, h=h: e.tensor_scalar(out=dgt[0:nq, h, 0:nq], in0=ident[0:nq, 0:nq], scalar1=wi_ap[:, h:h + 1],
                                                          scalar2=None, op0=ALU.mult),
                     reads=["ident", wkey], writes=[("dg", j % 2)])
            for (c0, w) in blocks:
                ab = 4 + cnt["ab"] % 2
                cnt["ab"] += 1

                def s_mm(h, c0=c0, w=w):
                    bank = cnt["bank"] % 4
                    cnt["bank"] += 1
                    mm(bank, (0, w), [(qb_[:, h, 0:nq], keys_ap_fn(c0, w))], reads=[("qi", j % 2), "kit"])
                    return bank
                nxt = s_mm(0)
                for h in range(32):
                    bank = nxt
                    if h + 1 < 32:
                        nxt = s_mm(h + 1)
                    ri = cnt["rr"] % 4
                    cnt["rr"] += 1
                    S.op("act", lambda e, bank=bank, ri=ri, w=w: e.activation(out=rb[ri][0:nq, 0:w], in_=ps[bank][0:nq, 0:w], func=AF.Relu),
                         reads=[psk(bank)], writes=[("rb", ri)])
                    S.op("pe", lambda pe, h=h, ri=ri, w=w, ab=ab: pe.matmul(ps[ab][0:nq, 0:w], lhsT=dgt[0:nq, h, 0:nq], rhs=rb[ri][0:nq, 0:w],
                                                                      start=(h == 0), stop=(h == 31)),
                         reads=[("rb", ri), ("dg", j % 2)], writes=[psk(ab)])
                S.op("act", lambda e, ab=ab, c0=c0, w=w: e.activation(out=SA[0:nq, c0:c0 + w], in_=ps[ab][0:nq, 0:w], func=AF.Copy),
                     reads=[psk(ab)], writes=[("SA", slot, c0)])
            def part_b():
                sak = [("SA", slot, c0) for (c0, w) in blocks]
                if adm:
                    S.op("dve", lambda e: e.tensor_tensor(out=SA[0:nq, L - 1024:L], in0=SA[0:nq, L - 1024:L], in1=admA[0:nq, v, :], op=ALU.add),
                         reads=sak + ["admA"], writes=sak)
                cur = SA
                for r in range(TOPK // 8):
                    S.op("dve", lambda e, cur=cur: e.max(out=m8[0:nq, :], in_=cur[0:nq, 0:L]), reads=sak + ["Wk"], writes=["m8"])
                    if r < TOPK // 8 - 1:
                        S.op("dve", lambda e, cur=cur: e.match_replace(out=Wk[0:nq, 0:L], in_to_replace=m8[0:nq, :],
                                                                       in_values=cur[0:nq, 0:L], imm_value=NEG),
                             reads=sak + ["m8", "Wk"], writes=["Wk"])
                        cur = Wk
                S.op("dve", lambda e: e.tensor_reduce(out=thr[0:nq, :], in_=m8[0:nq, :], axis=AX.X, op=ALU.min), reads=["m8"], writes=["thr"])
                S.op("dve", lambda e: e.tensor_scalar(out=thr[0:nq, :], in0=thr[0:nq, :], scalar1=-1e29, scalar2=None, op0=ALU.max),
                     reads=["thr"], writes=["thr"])
                S.op("dve", lambda e: e.tensor_scalar(out=mk[0:nq, 0:L], in0=SA[0:nq, 0:L], scalar1=thr[0:nq, 0:1], scalar2=None, op0=ALU.is_ge),
                     reads=sak + ["thr"], writes=["mk"])
                nblk = (L + 127) // 128
                for kb in range(nblk):
                    kw = min(128, L - kb * 128)
                    tb = 6 + cnt["tb"] % 2
                    cnt["tb"] += 1
                    transpose_to(tb, 0, mk[0:nq, kb * 128:kb * 128 + kw], nq, reads=["mk"])
                    S.op("act", lambda e, tb=tb, kb=kb, kw=kw: e.activation(out=mstg[0:kw, kb, 0:nq], in_=psbf(tb)[0:kw, 0:nq], func=AF.Copy),
                         reads=[psk(tb)], writes=[("mstg", kb)])
                out_fn(nblk, [("mstg", kb) for kb in range(nblk)])
            return part_b

        wis = [psb(f"wis{b}", [16, 32]) for b in range(2)]
        pend_b = []

        def run_b():
            while pend_b:
                pend_b.pop(0)()
        for v in range(V):
            for j in range(8):
                L = 1024 * (j + 1)
                pb_ = index_tile(v, j, j * 128, 128, lambda c0, w: kit[:, c0:c0 + w], L, WI[:, v, j, :], True,
                           lambda nblk, keys, j=j, v=v: S.dma("sp", MASKT[v, 0:nblk].rearrange("k p q -> p k q")[:, :, j * 128:(j + 1) * 128],
                                                             mstg[:, 0:nblk, :], reads=keys, writes=[("MASKT", j)]), "WIp")
                run_b()
                pend_b.append(pb_)
            for b in range(2):
                def outs(nblk, keys, b=b, v=v):
                    S.dma("sp", SMASKT[2 * v + b, 0:8].rearrange("k p q -> p k q"), mstg[:, 0:8, 0:16], reads=keys, writes=[("SMASKT", b, 0)])
                    S.dma("sp", SMASKT[2 * v + b, 8, 0:16, :], mstg[0:16, 8, 0:16], reads=keys, writes=[("SMASKT", b, 1)])
                S.dma("sp", wis[b][:], WI[b * 16:(b + 1) * 16, v, 8, :], writes=[("wis", b)])
                pb_ = index_tile(v, 8 + b, 1024 + b * 16, 16, lambda c0, w, b=b, v=v: skit[:, 2 * v + b, c0:c0 + w], LS, wis[b], False, outs, ("wis", b))
                run_b()
                pend_b.append(pb_)
        run_b()
        S.barrier()

    def attention(tagp, sbufs, kparts_fn, v_fn, kblocks, q0_fn, QA, QB, scale, mask_fn, rdeps, fin_fn):
        pT = sbufs["pT"]
        nkb = len(kblocks)
        if QB - QA <= 16 and nkb * (QB - QA) <= 512:
            nq = QB - QA
            pt, ptk = pT[0], ("pT", 0)

            def fs(pe):
                ins = None
                for i, (kb, kw) in enumerate(kblocks):
                    parts = kparts_fn(kb, kw)
                    for pi_, (l, rfn) in enumerate(parts):
                        ins = pe.matmul(ps[0][0:kw, i * nq:(i + 1) * nq], lhsT=l, rhs=rfn(QA, QB),
                                        start=(pi_ == 0), stop=(pi_ == len(parts) - 1))
                return ins
            S.op("pe", fs, reads=rdeps, writes=[psk(0)])
            i0 = 0
            while i0 < nkb:
                i1 = i0
                while i1 < nkb and kblocks[i1][1] == kblocks[i0][1]:
                    i1 += 1
                kw = kblocks[i0][1]
                S.op("act", lambda e, i0=i0, i1=i1, kw=kw: e.activation(out=pt[0:kw, i0 * nq:i1 * nq], in_=ps[0][0:kw, i0 * nq:i1 * nq],
                                                                        func=AF.Exp, scale=scale), reads=[psk(0)], writes=[ptk])
                i0 = i1
            for i, (kb, kw) in enumerate(kblocks):
                mask_fn(kb, kw, 0, pt[:, i * nq:(i + 1) * nq], ptk)

            def fpv(pe):
                ins = None
                for i, (kb, kw) in enumerate(kblocks):
                    pe.matmul(ps[4][:, 0:nq], lhsT=v_fn(kb, kw), rhs=pt[0:kw, i * nq:(i + 1) * nq], start=(i == 0), stop=(i == nkb - 1))
                for i, (kb, kw) in enumerate(kblocks):
                    ins = pe.matmul(ps[6][:, 0:nq], lhsT=ones[0:kw, :], rhs=pt[0:kw, i * nq:(i + 1) * nq], start=(i == 0), stop=(i == nkb - 1))
                return ins
            S.op("pe", fpv, reads=[ptk, "ones"] + rdeps, writes=[psk(4), psk(6)])
            rec = sbufs["rec"]
            S.op("dve", lambda e: e.reciprocal(out=rec[:, 0:nq], in_=ps[6][:, 0:nq]), reads=[psk(6)], writes=[("rec", 0)])
            fin_fn(0, QA, nq, rec)
            return

        def s_mm(i):
            kb, kw = kblocks[i]
            q0 = q0_fn(kb)
            par = (i % 2) * 2
            parts = kparts_fn(kb, kw)
            for gi, (c0, c1) in enumerate(((QA, QA + 512), (QA + 512, QB))):
                a, bb = max(c0, q0), min(c1, QB)
                if a >= bb:
                    continue

                def fn(pe, a=a, bb=bb, gi=gi):
                    ins = None
                    for pi_, (l, rfn) in enumerate(parts):
                        ins = pe.matmul(ps[par + gi][0:kw, a - c0:bb - c0], lhsT=l, rhs=rfn(a, bb),
                                        start=(pi_ == 0), stop=(pi_ == len(parts) - 1))
                    return ins
                S.op("pe", fn, reads=rdeps, writes=[psk(par + gi)])

        s_mm(0)
        for i, (kb, kw) in enumerate(kblocks):
            if i + 1 < nkb:
                s_mm(i + 1)
            q0 = q0_fn(kb)
            par = (i % 2) * 2
            pt = pT[i % 2]
            ptk = ("pT", i % 2)
            if q0 < QA + 512 < QB:
                S.op("act", lambda e, par=par, pt=pt, kw=kw, q0=q0: e.activation(
                    out=pt[0:kw, q0 - QA:QB - QA], in_=psbig[0:kw, par * 512 + q0 - QA:par * 512 + QB - QA], func=AF.Exp, scale=scale),
                    reads=[psk(par), psk(par + 1)], writes=[ptk])
            else:
                for gi, (c0, c1) in enumerate(((QA, QA + 512), (QA + 512, QB))):
                    a, bb = max(c0, q0), min(c1, QB)
                    if a >= bb:
                        continue
                    S.op("act", lambda e, a=a, bb=bb, c0=c0, gi=gi, par=par, pt=pt, kw=kw: e.activation(
                        out=pt[0:kw, a - QA:bb - QA], in_=ps[par + gi][0:kw, a - c0:bb - c0], func=AF.Exp, scale=scale),
                        reads=[psk(par + gi)], writes=[ptk])
            mask_fn(kb, kw, q0, pt, ptk)
            assert kw == 128
            sacc = sbufs["sacc"]
            if i == 0:
                assert q0 == QA
                S.op("dve", lambda e, pt=pt: e.tensor_copy(out=sacc[:, 0:QB - QA], in_=pt[:, 0:QB - QA]), reads=[ptk], writes=["sacc"])
            else:
                S.op("dve", lambda e, pt=pt, q0=q0: e.tensor_tensor(out=sacc[:, q0 - QA:QB - QA], in0=sacc[:, q0 - QA:QB - QA],
                                                                    in1=pt[:, q0 - QA:QB - QA], op=ALU.add),
                     reads=[ptk, "sacc"], writes=["sacc"])
            for gi, (c0, c1) in enumerate(((QA, QA + 512), (QA + 512, QB))):
                a, bb = max(c0, q0), min(c1, QB)
                if a >= bb:
                    continue

                def fn(pe, a=a, bb=bb, c0=c0, gi=gi, kb=kb, kw=kw, i=i, pt=pt):
                    return pe.matmul(ps[4 + gi][:, a - c0:bb - c0], lhsT=v_fn(kb, kw), rhs=pt[0:kw, a - QA:bb - QA],
                                     start=(i == 0), stop=(i == nkb - 1))
                S.op("pe", fn, reads=[ptk] + rdeps, writes=[psk(4 + gi)])
        rec = sbufs["rec"]
        for gi, (c0, c1) in enumerate(((QA, QA + 512), (QA + 512, QB))):
            if c0 >= QB:
                continue
            n = min(c1, QB) - c0
            S.op("pe", lambda pe, gi=gi, c0=c0, n=n: pe.matmul(ps[6 + gi][:, 0:n], lhsT=onesf[:, :], rhs=sbufs["sacc"][:, c0 - QA:c0 - QA + n],
                                                               start=True, stop=True),
                 reads=["sacc", "onesf"], writes=[psk(6 + gi)])
            S.op("dve", lambda e, gi=gi, n=n, c0=c0: e.reciprocal(out=rec[:, c0 - QA:c0 - QA + n], in_=ps[6 + gi][:, 0:n]),
                 reads=[psk(6 + gi)], writes=[("rec", gi)])
            fin_fn(gi, c0, n, rec)

    with ExitStack() as ph:
        def psb(name, shape, dt=F32):
            return ph.enter_context(nc.sbuf_tensor(f"p5_{name}", list(shape), dt))
        ckvT = psb("ckvT", [128, 4, SEQ], BF16)
        krT = psb("krT", [128, SEQ], BF16)
        sckvT = psb("sckvT", [128, 2 * V, 4, LS], BF16)
        skrT = psb("skrT", [128, 2 * V, LS], BF16)
        kn = psb("kn", [128, SEQ], BF16)
        vh = psb("vh", [128, 64, 128], BF16)
        wuk = [psb(f"wuk{i}", [128, 4, 128], BF16) for i in range(2)]
        wuv = [psb(f"wuv{i}", [128, 4, 128], BF16) for i in range(2)]
        qn = [psb(f"qn{i}", [128, NT], BF16) for i in range(V)]
        qr = [psb(f"qr{i}", [128, NT], BF16) for i in range(V)]
        ga = [psb(f"ga{i}", [128, NT], BF16) for i in range(V)]
        pT = [psb(f"pT{i}", [128, 1024], BF16) for i in range(2)]
        rec = psb("rec", [128, 1024])
        otmp = psb("otmp", [128, 512])
        mo = [psb(f"mo{i}", [128, NT], BF16) for i in range(V)]
        for c in range(4):
            S.dma("sp", ckvT[:, c, :], CKVT[c], writes=[("ckvT", c)])
        S.dma("sp", krT[:], KRT[:, :], writes=["krT"])
        for b in range(2 * V):
            S.dma("sp", sckvT[:, b, :, :], SCKVT[b].rearrange("c p l -> p c l"), writes=[("sckvT", b)])
            S.dma("sp", skrT[:, b, :], SKRT[b], writes=[("skrT", b)])
        ckeys = [("ckvT", c) for c in range(4)] + [("sckvT", b) for b in range(2 * V)]
        krkeys = ["krT"] + [("skrT", b) for b in range(2 * V)]
        wukv = w_uk.rearrange("(c p) n -> p c n", p=128)
        wuvv = w_uv.rearrange("(c p) n -> p c n", p=128)
        bufs = {"pT": pT, "rec": rec, "sacc": psb("sacc", [128, 1024])}
        PBLK = [(kb, 128) for kb in range(64)]
        SBLK = [(kb, 128) for kb in range(8)] + [(8, 16)]
        for h in range(32):
            hb = h % 2
            rp = (h % 2) * 64
            S.dma("pool", wuk[hb][:], wukv[:, :, h * 128:(h + 1) * 128], writes=[("wuk", hb)])
            S.dma("pool", wuv[hb][:], wuvv[:, :, h * 128:(h + 1) * 128], writes=[("wuv", hb)])
            for v in range(V):
                S.dma("sp", qn[v][:], QNT[v, h], writes=[("qn", v)])
                S.dma("sp", qr[v][:], QRT[v, h // 2], writes=[("qr", v)])
                S.dma("sp", ga[v][:], GAT[v, h], writes=[("ga", v)])

            def materialize(cfn, L, kblocks):
                for c0 in range(0, L, 512):
                    w = min(512, L - c0)
                    bank = (c0 // 512) % 4
                    mm(bank, (0, w), [(wuk[hb][:, c, :], cfn(c, c0, w)) for c in range(4)], reads=[("wuk", hb)] + ckeys)
                    S.op("act", lambda e, bank=bank, c0=c0, w=w: e.activation(out=kn[:, c0:c0 + w], in_=ps[bank][:, 0:w], func=AF.Copy),
                         reads=[psk(bank)], writes=["kn"])
                for k4 in range(0, len(kblocks), 4):
                    bank = 4 + (k4 // 4) % 4
                    blk = kblocks[k4:k4 + 4]

                    def fn(pe, blk=blk, bank=bank):
                        ins = None
                        for bi_, (kb, kw) in enumerate(blk):
                            for c in range(4):
                                ins = pe.matmul(ps[bank][0:kw, bi_ * 128:(bi_ + 1) * 128], lhsT=cfn(c, kb * 128, kw), rhs=wuv[hb][:, c, :],
                                                start=(c == 0), stop=(c == 3))
                        return ins
                    S.op("pe", fn, reads=[("wuv", hb)] + ckeys, writes=[psk(bank)])
                    for bi_, (kb, kw) in enumerate(blk):
                        S.op("dve", lambda e, bank=bank, bi_=bi_, kb=kb, kw=kw: e.tensor_copy(
                            out=vh[0:kw, kb, :], in_=ps[bank][0:kw, bi_ * 128:(bi_ + 1) * 128]), reads=[psk(bank)], writes=["vh"])

            def run_set(v, is_prompt, kr_src, QA, QB, kblocks, q0_fn):
                def kparts(kb, kw):
                    return [(kn[:, kb * 128:kb * 128 + kw], lambda a, bb: qn[v][:, a:bb]),
                            (kr_src(kb * 128, kw), lambda a, bb: qr[v][rp:rp + 64, a:bb])]

                def mask_fn(kb, kw, q0, pt, ptk):
                    if is_prompt:
                        S.op("dve", lambda e: e.tensor_scalar(out=pt[:, q0:q0 + 64], in0=pt[:, q0:q0 + 64],
                                                             scalar1=maskB[:, v, kb % 4:kb % 4 + 1], scalar2=None, op0=ALU.mult),
                             reads=[ptk, "maskB"], writes=[ptk])

                def fin(gi, c0, n, rec_):
                    S.op("dve", lambda e: e.tensor_tensor(out=otmp[:, 0:n], in0=ps[4 + gi][:, 0:n], in1=rec_[:, c0 - QA:c0 - QA + n], op=ALU.mult),
                         reads=[psk(4 + gi), ("rec", gi)], writes=["otmp"])
                    S.op("dve", lambda e: e.tensor_tensor(out=mo[v][:, c0:c0 + n], in0=otmp[:, 0:n], in1=ga[v][:, c0:c0 + n], op=ALU.mult),
                         reads=["otmp", ("ga", v)], writes=[("mo", v, c0)])
                    mokeys[v].append(("mo", v, c0))
                attention("mla", bufs, kparts, lambda kb, kw: vh[0:kw, kb, :], kblocks, q0_fn, QA, QB, MLA_SCALE, mask_fn,
                          ["kn", "vh", ("qn", v), ("qr", v)] + krkeys, fin)

            mokeys = [[] for _ in range(V)]
            materialize(lambda c, a, w: ckvT[:, c, a:a + w], SEQ, PBLK)
            for v in range(V):
                run_set(v, True, lambda a, w: krT[rp:rp + 64, a:a + w], 0, NP, PBLK, lambda kb: 64 * (kb // 4))
            for v in range(V):
                for b in range(2):
                    sb_ = 2 * v + b
                    materialize(lambda c, a, w, sb_=sb_: sckvT[:, sb_, c, a:a + w], LS, SBLK)
                    run_set(v, False, lambda a, w, sb_=sb_: skrT[rp:rp + 64, sb_, a:a + w], NP + 16 * b, NP + 16 * b + 16, SBLK, lambda kb: 0)
            for v in range(V):
                S.dma("sp", MAT[v, h], mo[v][:], reads=mokeys[v], writes=[("MAT", v, h)])
        S.barrier()

    with ExitStack() as ph:
        def psb(name, shape, dt=F32):
            return ph.enter_context(nc.sbuf_tensor(f"p6_{name}", list(shape), dt))
        kg = [psb(f"kg{i}", [128, SEQ], BF16) for i in range(2)]
        vg = [psb(f"vg{i}", [128, 64, 128], BF16) for i in range(2)]
        skg = [psb(f"skg{i}", [128, 2, LS], BF16) for i in range(2)]
        svg = [psb(f"svg{i}", [128, 2, 9, 128], BF16) for i in range(2)]
        MOFF = []
        mo_ = 0
        for kb in range(64):
            MOFF.append(mo_)
            mo_ += NP - 64 * (kb // 4)
        mT = psb("mT", [128, mo_], BF16)
        smT = psb("smT", [128, 2, 9, 16], BF16)
        qb = [psb(f"qb{i}", [128, NT], BF16) for i in range(2)]
        gb = [psb(f"gb{i}", [128, NT], BF16) for i in range(2)]
        ma = [psb(f"ma{i}", [128, NT], BF16) for i in range(2)]
        pT = [psb(f"pT{i}", [128, 1024], BF16) for i in range(2)]
        rec = psb("rec", [128, 1024])
        otmp = psb("otmp", [128, 512])
        mo = [psb(f"mo{i}", [128, NT], BF16) for i in range(2)]
        bufs = {"pT": pT, "rec": rec, "sacc": psb("sacc", [128, 1024])}
        PBLK = [(kb, 128) for kb in range(64)]
        SBLK = [(kb, 128) for kb in range(8)] + [(8, 16)]
        gcount = 0
        for v in range(V):
            for kb in range(64):
                q0 = 64 * (kb // 4)
                S.dma("sp", mT[:, MOFF[kb]:MOFF[kb] + NP - q0], MASKT[v, kb][:, q0:NP], writes=[("mT", kb)])
            for b in range(2):
                S.dma("sp", smT[:, b, :, :], SMASKT[2 * v + b].rearrange("k p q -> p k q"), writes=[("smT", b)])
            for g in range(8):
                gbi = gcount % 2
                gcount += 1
                S.dma("sp", kg[gbi][:], KBT[g], writes=[("kg", gbi)])
                S.dma("sp", vg[gbi][:], VB[:, g * 128:(g + 1) * 128].rearrange("(k p) d -> p k d", p=128), writes=[("vg", gbi)])
                for b in range(2):
                    S.dma("sp", skg[gbi][:, b, :], SKBT[2 * v + b, g], writes=[("skg", gbi, b)])
                    S.dma("sp", svg[gbi][:, b, :, :], SVB[2 * v + b, 0:9 * 128, g * 128:(g + 1) * 128].rearrange("(k p) d -> p k d", p=128),
                          writes=[("svg", gbi, b)])
                kvkeys = [("kg", gbi), ("vg", gbi)] + [("skg", gbi, b) for b in range(2)] + [("svg", gbi, b) for b in range(2)]
                for hh in range(4):
                    h = g * 4 + hh
                    hb = h % 2
                    S.dma("sp", qb[hb][:], QBT[v, h], writes=[("qb", hb)])
                    S.dma("sp", gb[hb][:], GBT[v, h], writes=[("gb", hb)])
                    S.dma("sp", ma[hb][:], MAT[v, h], writes=[("ma", hb)])
                    mokeys = []
                    for (sname, kfn, vfn, QA, QB, kblocks, q0_fn) in (
                        [("p", lambda a, w: kg[gbi][:, a:a + w], lambda kb, kw: vg[gbi][0:kw, kb, :], 0, NP, PBLK, lambda kb: 64 * (kb // 4))] +
                        [(f"s{b}", lambda a, w, b=b: skg[gbi][:, b, a:a + w], lambda kb, kw, b=b: svg[gbi][0:kw, b, kb, :],
                          NP + 16 * b, NP + 16 * b + 16, SBLK, lambda kb: 0) for b in range(2)]):
                        def kparts(kb, kw, kfn=kfn):
                            return [(kfn(kb * 128, kw), lambda a, bb: qb[hb][:, a:bb])]

                        def mask_fn(kb, kw, q0, pt, ptk, sname=sname):
                            if sname == "p":
                                S.op("dve", lambda e: e.tensor_tensor(out=pt[:, q0:NP], in0=pt[:, q0:NP], in1=mT[:, MOFF[kb]:MOFF[kb] + NP - q0], op=ALU.mult),
                                     reads=[ptk, ("mT", kb)], writes=[ptk])
                            else:
                                b = int(sname[1])
                                S.op("dve", lambda e: e.tensor_tensor(out=pt[0:kw, 0:16], in0=pt[0:kw, 0:16], in1=smT[0:kw, b, kb, :], op=ALU.mult),
                                     reads=[ptk, ("smT", b)], writes=[ptk])

                        def fin(gi, c0, n, rec_, QA=QA):
                            S.op("dve", lambda e: e.tensor_tensor(out=otmp[:, 0:n], in0=ps[4 + gi][:, 0:n], in1=rec_[:, c0 - QA:c0 - QA + n], op=ALU.mult),
                                 reads=[psk(4 + gi), ("rec", gi)], writes=["otmp"])
                            S.op("dve", lambda e: e.tensor_tensor(out=otmp[:, 0:n], in0=otmp[:, 0:n], in1=gb[hb][:, c0:c0 + n], op=ALU.mult),
                                 reads=["otmp", ("gb", hb)], writes=["otmp"])
                            S.op("dve", lambda e: e.tensor_tensor(out=mo[hb][:, c0:c0 + n], in0=otmp[:, 0:n], in1=ma[hb][:, c0:c0 + n], op=ALU.add),
                                 reads=["otmp", ("ma", hb)], writes=[("mo", hb, c0)])
                            mokeys.append(("mo", hb, c0))
                        attention("dsa", bufs, kparts, vfn, kblocks, q0_fn, QA, QB, DSA_SCALE, mask_fn, kvkeys + [("qb", hb)], fin)
                    S.dma("sp", MGT[v, h], mo[hb][:], reads=mokeys, writes=[("MGT", v, h)])
                    if DEBUG:
                        S.dma("sp", dbg_mg[v, h], mo[hb][:], reads=mokeys, writes=[("dbg", v, h)])
        S.barrier()

    with ExitStack() as ph:
        def psb(name, shape, dt=F32):
            return ph.enter_context(nc.sbuf_tensor(f"p7_{name}", list(shape), dt))
        mg = psb("mg", [128, KD, NT], BF16)
        wp = [psb(f"wp{i}", [128, KD, 512], BF16) for i in range(2)]
        xb = [psb(f"xb{i}", [128, NT]) for i in range(2)]
        x1 = [psb(f"x1{i}", [128, NT]) for i in range(2)]
        wv = w_out.rearrange("(k p) c -> p k c", p=128)
        pcount = 0
        for v in range(V):
            xov = xT_own[v].rearrange("(k p) t -> k p t", p=128)
            S.dma("sp", mg[:], MGT[v].rearrange("h p t -> p h t"), writes=["mg"])
            for pn in range(8):
                pw = pcount % 2
                pcount += 1
                S.dma("pool", wp[pw][:], wv[:, :, pn * 512:(pn + 1) * 512], writes=[("wp", pw)])
                for cc in range(4):
                    oc = pn * 4 + cc
                    S.dma("sp", xb[oc % 2][:], xov[oc], writes=[("xb", oc % 2)])
                    for gi, (c0, n) in enumerate(TGRP):
                        bank = (oc % 2) * 3 + gi
                        mm(bank, (0, n), [(wp[pw][:, j, cc * 128:(cc + 1) * 128], mg[:, j, c0:c0 + n]) for j in range(KD)],
                           reads=[("wp", pw), "mg"])
                    for bi_, (c0, c1, b) in enumerate(bcols(v)):
                        if c0 == 0:
                            for g2_ in range(2):
                                S.op("dve", lambda e, oc=oc, g2_=g2_: e.scalar_tensor_tensor(
                                    out=x1[oc % 2][:, g2_ * 512:(g2_ + 1) * 512], in0=ps[(oc % 2) * 3 + g2_][:, :], scalar=adaT[:, GT1 + oc, 0:1],
                                    in1=xb[oc % 2][:, g2_ * 512:(g2_ + 1) * 512], op0=ALU.mult, op1=ALU.add),
                                    reads=[psk((oc % 2) * 3 + g2_), ("xb", oc % 2)], writes=[("x1", oc % 2, g2_)])
                        else:
                            S.op("dve", lambda e, oc=oc, c0=c0, c1=c1, b=b: e.scalar_tensor_tensor(
                                out=x1[oc % 2][:, c0:c1], in0=ps[(oc % 2) * 3 + 2][:, c0 - 1024:c1 - 1024], scalar=adaT[:, GT1 + oc, b:b + 1],
                                in1=xb[oc % 2][:, c0:c1], op0=ALU.mult, op1=ALU.add),
                                reads=[psk((oc % 2) * 3 + 2), ("xb", oc % 2)], writes=[("x1", oc % 2, 1 + bi_)])
                    S.dma("sp", X1T[v, oc], x1[oc % 2][:], reads=[("x1", oc % 2, i_) for i_ in range(4)], writes=[("X1T", v, oc)])
        S.barrier()

    H2T = dscr("H2T", [V, KD, 128, NT])
    for v in range(V):
        with ExitStack() as ph7:
            h2T = ph7.enter_context(nc.sbuf_tensor(f"h2T{v}", [128, KD, NT], BF16))
            norm_phase(X1T[v], A2, SH2, h2T, tag=f"n2{v}", v=v)
            S.dma("sp", H2T[v].rearrange("k p t -> p k t"), h2T[:], writes=["H2T"])
            S.barrier()
    with ExitStack() as ph:
        def psb(name, shape, dt=F32):
            return ph.enter_context(nc.sbuf_tensor(f"p8_{name}", list(shape), dt))
        HN = NT // 2
        FB = 8
        h2h = psb("h2h", [128, KD, HN], BF16)
        acc = psb("acc", [128, KD, HN])
        uT = psb("uT", [128, FB, HN], BF16)
        wu = [psb(f"wu{i}", [128, KD, 256], BF16) for i in range(2)]
        wd = [psb(f"wd{i}", [128, FB, 512], BF16) for i in range(2)]
        ur = [psb(f"ur{i}", [128, HN]) for i in range(2)]
        xb = [psb(f"xb{i}", [128, HN]) for i in range(2)]
        xo = [psb(f"xo{i}", [128, HN]) for i in range(2)]
        wuv_ = w_up.rearrange("(k p) c -> p k c", p=128)
        wdv_ = w_down.rearrange("(fb fc p) c -> fb p fc c", fc=FB, p=128)
        HG = [(0, 512), (512, HN - 512)]
        cnt = {"wu": 0, "wd": 0, "bank": 0}
        for vh_ in range(2 * V):
            v, half = vh_ // 2, vh_ % 2
            hc = half * HN
            S.dma("sp", h2h[:], H2T[v].rearrange("k p t -> p k t")[:, :, hc:hc + HN], writes=["h2h"])
            for fb in range(128 // FB):
                for f2 in range(FB // 2):
                    i = cnt["wu"] % 2
                    cnt["wu"] += 1
                    fcol = (fb * FB + f2 * 2) * 128
                    S.dma("pool", wu[i][:], wuv_[:, :, fcol:fcol + 256], writes=[("wu", i)])
                    for fl in range(2):
                        fci = f2 * 2 + fl
                        bp = (cnt["bank"] % 2) * 2
                        cnt["bank"] += 1
                        for gi, (c0, n) in enumerate(HG):
                            mm(bp + gi, (0, n), [(wu[i][:, k, fl * 128:(fl + 1) * 128], h2h[:, k, c0:c0 + n]) for k in range(KD)],
                               reads=[("wu", i), "h2h"])
                        ui = fci % 2
                        for gi, (c0, n) in enumerate(HG):
                            S.op("act", lambda e, bp=bp, gi=gi, c0=c0, n=n, ui=ui: e.activation(
                                out=ur[ui][:, c0:c0 + n], in_=ps[bp + gi][:, 0:n], func=AF.Relu), reads=[psk(bp + gi)], writes=[("ur", ui, gi)])
                        S.op("dve", lambda e, ui=ui, fci=fci: e.tensor_tensor(out=uT[:, fci, :], in0=ur[ui][:], in1=ur[ui][:], op=ALU.mult),
                             reads=[("ur", ui, 0), ("ur", ui, 1)], writes=[("uT", fci)])
                utk = [("uT", f_) for f_ in range(FB)]
                for og in range(8):
                    i = cnt["wd"] % 2
                    cnt["wd"] += 1
                    S.dma("pool", wd[i][:], wdv_[fb][:, :, og * 512:(og + 1) * 512], writes=[("wd", i)])
                    for ocl in range(4):
                        oc = og * 4 + ocl
                        bp = 4 + (oc % 2) * 2
                        for gi, (c0, n) in enumerate(HG):
                            mm(bp + gi, (0, n), [(wd[i][:, fc, ocl * 128:(ocl + 1) * 128], uT[:, fc, c0:c0 + n]) for fc in range(FB)],
                               reads=[("wd", i)] + utk)
                        for gi, (c0, n) in enumerate(HG):
                            if fb == 0:
                                S.op("dve", lambda e, bp=bp, gi=gi, c0=c0, n=n, oc=oc: e.tensor_copy(out=acc[:, oc, c0:c0 + n], in_=ps[bp + gi][:, 0:n]),
                                     reads=[psk(bp + gi)], writes=[("acc", oc, gi)])
                            else:
                                S.op("dve", lambda e, bp=bp, gi=gi, c0=c0, n=n, oc=oc: e.tensor_tensor(
                                    out=acc[:, oc, c0:c0 + n], in0=ps[bp + gi][:, 0:n], in1=acc[:, oc, c0:c0 + n], op=ALU.add),
                                    reads=[psk(bp + gi), ("acc", oc, gi)], writes=[("acc", oc, gi)])
            for oc in range(KD):
                S.dma("sp", xb[oc % 2][:], X1T[v, oc][:, hc:hc + HN], writes=[("xb", oc % 2)])
                wk_ = []
                for bi_, (c0, c1, b) in enumerate(bcols(v)):
                    a, bb = max(c0, hc), min(c1, hc + HN)
                    if a >= bb:
                        continue
                    S.op("dve", lambda e, oc=oc, a=a, bb=bb, b=b: e.scalar_tensor_tensor(
                        out=xo[oc % 2][:, a - hc:bb - hc], in0=acc[:, oc, a - hc:bb - hc], scalar=adaT[:, GT2 + oc, b:b + 1],
                        in1=xb[oc % 2][:, a - hc:bb - hc], op0=ALU.mult, op1=ALU.add),
                        reads=[("acc", oc, 0), ("acc", oc, 1), ("xb", oc % 2)], writes=[("xo", oc % 2, bi_)])
                    wk_.append(("xo", oc % 2, bi_))
                S.dma("sp", X2T[v, oc][:, hc:hc + HN], xo[oc % 2][:], reads=wk_, writes=[("X2T", v, oc, half)])
        S.barrier()

    for v in range(V):
        norm_phase(X2T[v], None, 0, None, final=True, tag=f"n3{v}", v=v)
    es.close()
    return nc


def _rope_table(pos, theta, rot):
    half = rot // 2
    inv = (np.float32(theta) ** (-np.arange(half, dtype=np.float32) / np.float32(half))).astype(np.float32)
    ang = pos.astype(np.float32)[:, None] * inv[None, :]
    return np.concatenate([np.cos(ang).astype(np.float32), np.sin(ang).astype(np.float32)], axis=1)


def _fm(v, nchunk):
    return np.ascontiguousarray(np.asarray(v, np.float32).reshape(nchunk, 128).T)


_NC_CACHE = {}


def kernel(x_prompt, x_sample, c_prompt, c_sample, cache_mla_ckv, cache_mla_krope, cache_dsa_k, cache_dsa_v,
           cache_idx_k, w_ada, b_ada, g_norm1, w_in, g_q_lora, w_uq, g_kv_lora, w_uk, w_uv, w_out, g_norm2,
           w_up, w_down, g_final):
    f32 = np.float32
    x_prompt = np.asarray(x_prompt, f32)
    x_sample = np.asarray(x_sample, f32)
    xT_all = np.ascontiguousarray(x_prompt[0].T)
    shared = {
        "xT_all": xT_all,
        "w_ada": np.asarray(w_ada, f32)[0], "b_adaT": _fm(np.asarray(b_ada)[0], 192),
        "g1T": _fm(np.asarray(g_norm1)[0], KD), "g2T": _fm(np.asarray(g_norm2)[0], KD), "gfT": _fm(g_final, KD),
        "w_in": np.asarray(w_in, f32)[0],
        "gq_rep": np.ascontiguousarray(np.broadcast_to(np.asarray(g_q_lora, f32)[0][None, :], (128, 1024))),
        "gkv_rep": np.ascontiguousarray(np.broadcast_to(np.asarray(g_kv_lora, f32)[0][None, :], (128, 512))),
        "w_uq": np.asarray(w_uq, f32)[0],
        "w_uk": np.asarray(w_uk, f32)[0].reshape(512, 4096), "w_uv": np.asarray(w_uv, f32)[0].reshape(512, 4096),
        "w_out": np.asarray(w_out, f32)[0], "w_up": np.asarray(w_up, f32)[0], "w_down": np.asarray(w_down, f32)[0],
        "cs_mla_all": _rope_table(np.arange(SEQ), 10000.0, 64), "cs_dsa_all": _rope_table(np.arange(SEQ), 500000.0, 32),
        "ident": np.eye(128, dtype=f32).astype(ml_dtypes.bfloat16), "ones": np.ones((128, 128), f32).astype(ml_dtypes.bfloat16),
    }
    in_maps = []
    own_idx = []
    csf = np.asarray(c_sample, f32)
    for pc in range(NPHYS):
        xo_l, csm_l, csd_l, mb_l, adm_l = [], [], [], [], []
        bs = []
        for v in range(V):
            i = pc * V + v
            chunks = [8 * m + i for m in range(16)]
            tok = np.concatenate([np.arange(64 * c, 64 * c + 64) for c in chunks])
            own_idx.append(tok)
            xo = np.concatenate([x_prompt[0][tok], x_sample[2 * i], x_sample[2 * i + 1]], axis=0)
            xo_l.append(np.ascontiguousarray(xo.T))
            pos = np.concatenate([tok, PAST + np.arange(16), PAST + np.arange(16)])
            csm_l.append(_rope_table(pos, 10000.0, 64))
            csd_l.append(_rope_table(pos, 500000.0, 32))
            maskB = np.zeros((128, 4), f32)
            for r in range(4):
                for a_ in range(2):
                    maskB[a_ * 64:(a_ + 1) * 64, r] = 1.0 if i >= 2 * r + a_ else 0.0
            admA = np.full((128, 1024), NEG, f32)
            admA[0:64, 0:64 * (i + 1)] = 0.0
            admA[64:128, 0:64 * (8 + i + 1)] = 0.0
            mb_l.append(maskB)
            adm_l.append(admA)
            bs += [2 * i, 2 * i + 1]
        cc = np.stack([np.asarray(c_prompt, f32)[0]] + [csf[b_] for b_ in bs], axis=1)
        cTm = np.ascontiguousarray(cc.reshape(KD, 128, NB).transpose(1, 0, 2).reshape(128, KD * NB))
        m = dict(shared)
        m.update({
            "xT_own": np.stack(xo_l), "cT": cTm,
            "cs_mla_own": np.stack(csm_l), "cs_dsa_own": np.stack(csd_l),
            "c_ckvT": np.ascontiguousarray(np.asarray(cache_mla_ckv, f32)[0][bs].transpose(0, 2, 1)),
            "c_krT": np.ascontiguousarray(np.asarray(cache_mla_krope, f32)[0][bs].transpose(0, 2, 1)),
            "c_kT": np.ascontiguousarray(np.asarray(cache_dsa_k, f32)[0][bs].transpose(0, 2, 3, 1)),
            "c_v": np.ascontiguousarray(np.asarray(cache_dsa_v, f32)[0][bs].reshape(2 * V, PAST, 1024)),
            "c_kiT": np.ascontiguousarray(np.asarray(cache_idx_k, f32)[0][bs].transpose(0, 2, 1)),
            "maskB": np.ascontiguousarray(np.concatenate(mb_l, axis=1)), "admA": np.ascontiguousarray(np.concatenate(adm_l, axis=1)),
        })
        in_maps.append(m)
    if "nc" not in _NC_CACHE:
        _NC_CACHE["nc"] = build_program()
    nc = _NC_CACHE["nc"]
    res = run_bass_kernel_spmd(nc, in_maps, core_ids=list(range(NPHYS)))
    R = res.results
    y_prompt = np.zeros((1, SEQ, D), f32)
    y_sample = np.zeros((16, 16, D), f32)
    outs_p = {k: np.zeros((1, 1, SEQ, w), f32) for k, w in (("o_ckv", 512), ("o_kr", 64), ("o_k", 1024), ("o_v", 1024), ("o_ki", 128))}
    outs_s = {k: np.zeros((1, 16, 16, w), f32) for k, w in (("o_ckv", 512), ("o_kr", 64), ("o_k", 1024), ("o_v", 1024), ("o_ki", 128))}
    for i in range(NCORES):
        r, v = R[i // V], i % V
        y = np.asarray(r["yT"])[v].T
        y_prompt[0][own_idx[i]] = y[0:NP]
        y_sample[2 * i] = y[NP:NP + 16]
        y_sample[2 * i + 1] = y[NP + 16:NT]
        for k in outs_p:
            a = np.asarray(r[k])[v]
            outs_p[k][0, 0][own_idx[i]] = a[0:NP]
            outs_s[k][0, 2 * i] = a[NP:NP + 16]
            outs_s[k][0, 2 * i + 1] = a[NP + 16:NT]
    if DEBUG:
        kernel.debug = [np.asarray(R[i // V]["dbg_mg"])[i % V] for i in range(NCORES)]
        kernel.own_idx = own_idx
    return (y_prompt, y_sample,
            outs_p["o_ckv"], outs_p["o_kr"], outs_p["o_k"].reshape(1, 1, SEQ, 8, 128), outs_p["o_v"].reshape(1, 1, SEQ, 8, 128), outs_p["o_ki"],
            outs_s["o_ckv"], outs_s["o_kr"], outs_s["o_k"].reshape(1, 16, 16, 8, 128), outs_s["o_v"].reshape(1, 16, 16, 8, 128), outs_s["o_ki"])
```
